# Optimizing a Trainium2 kernel written in Bass

```python
import math
import jax, jax.numpy as jnp
from jax import lax
import numpy as np

D_MODEL = 1024
BATCH = 8
SEQ = 4096
DEPTH = 2
DEC_BATCH = 32
DEC_SEQ = 1
PAST_LEN = 16384
PAGE_SIZE = 128

HEAD_DIM = 64
DIL_GROUPS = ((128, 1), (512, 4), (2048, 16))
HEADS_PER_GROUP = 8
N_DIL_HEADS = HEADS_PER_GROUP * len(DIL_GROUPS)
DIL_QKV = N_DIL_HEADS * HEAD_DIM
DIL_OUT = HEADS_PER_GROUP * HEAD_DIM
DIL_BLOCK = 128
RET_HEADS = 4
RET_DK = 128
RET_DV = 256
RET_QK = RET_HEADS * RET_DK
RET_V = RET_HEADS * RET_DV
RET_CHUNK = 128
D_FF = 2816
N_IN = 3 * DIL_QKV + 2 * RET_QK + 2 * RET_V + 2 * D_MODEL
NORM_EPS = 1e-6
GN_EPS = 1e-6

kernel_name = 'hybrid_dilated_retention_macaron_step'


def rmsnorm(x, g):
    x32 = x.astype(jnp.float32)
    y = x32 * lax.rsqrt(jnp.mean(x32 * x32, axis=-1, keepdims=True) + NORM_EPS)
    return (y * g.astype(jnp.float32)).astype(x.dtype)


def swiglu(x, wg, wu, wd):
    return (jax.nn.silu(x @ wg) * (x @ wu)) @ wd


def alibi_slopes():
    return 2.0 ** (-8.0 * jnp.arange(1, N_DIL_HEADS + 1, dtype=jnp.float32) / N_DIL_HEADS)


def dilated_prompt(q, k, v, dil, n_back, slopes):
    B, S, H, E = q.shape
    unit = dil * DIL_BLOCK
    s_pad = -(-S // unit) * unit
    nb = s_pad // unit

    def blocks(t):
        t = jnp.pad(t, ((0, 0), (0, s_pad - S), (0, 0), (0, 0)))
        return t.reshape(B, nb, DIL_BLOCK, dil, H, E)

    def with_prev(t):
        prev = jnp.pad(t[:, :-1], ((0, 0), (1, 0), (0, 0), (0, 0), (0, 0), (0, 0)))
        return jnp.concatenate([prev, t], axis=2)

    qb = blocks(q)
    kc = with_prev(blocks(k))
    vc = with_prev(blocks(v))
    scores = jnp.einsum('bnqrhe,bnkrhe->bnrhqk', qb, kc) * (E ** -0.5)
    qi = jnp.arange(DIL_BLOCK)[:, None]
    ki = jnp.arange(2 * DIL_BLOCK)[None, :]
    rel = qi + DIL_BLOCK - ki
    band = (rel >= 0) & (rel <= n_back)
    real = (jnp.arange(nb)[:, None, None] > 0) | (ki >= DIL_BLOCK)[None]
    mask = (band[None] & real)[None, :, None, None]
    bias = -slopes[:, None, None] * (dil * rel).astype(jnp.float32)
    s = jnp.where(mask, scores + bias, -jnp.inf)
    m = jnp.max(s, axis=-1, keepdims=True)
    p = jnp.exp(s - m)
    den = jnp.sum(p, axis=-1)
    den_t = jnp.transpose(den, (0, 1, 4, 2, 3))
    o = jnp.einsum('bnrhqk,bnkrhe->bnqrhe', p, vc) / den_t[..., None]
    lse = jnp.transpose(m[..., 0], (0, 1, 4, 2, 3)) + jnp.log(den_t)
    o = o.reshape(B, s_pad, H, E)[:, :S]
    lse = lse.reshape(B, s_pad, H)[:, :S]
    return o, lse


def dilated_sample(q, k_new, v_new, buf, dil, n_back, slopes):
    T, E = q.shape[1], q.shape[-1]
    Wb = buf.shape[1]
    b32 = buf.astype(jnp.float32)
    kc = jnp.concatenate([b32[:, :, 0], k_new], axis=1)
    vc = jnp.concatenate([b32[:, :, 1], v_new], axis=1)
    steps = jnp.arange(n_back + 1)
    idx = Wb + jnp.arange(T)[:, None] - dil * steps[None, :]
    valid = idx >= 0
    idxc = jnp.maximum(idx, 0)
    kg = jnp.take(kc, idxc, axis=1)
    vg = jnp.take(vc, idxc, axis=1)
    scores = jnp.einsum('bthe,btihe->bhti', q, kg) * (E ** -0.5)
    scores = scores - slopes[:, None, None] * (dil * steps).astype(jnp.float32)[None, None, :]
    s = jnp.where(valid[None, None], scores, -jnp.inf)
    m = jnp.max(s, axis=-1, keepdims=True)
    p = jnp.exp(s - m)
    den = jnp.sum(p, axis=-1)
    den_t = jnp.transpose(den, (0, 2, 1))
    o = jnp.einsum('bhti,btihe->bthe', p, vg) / den_t[..., None]
    lse = jnp.transpose(m[..., 0], (0, 2, 1)) + jnp.log(den_t)
    new_buf = jnp.stack([kc[:, T:], vc[:, T:]], axis=2).astype(buf.dtype)
    return o, lse, new_buf


def retention(q, k, v, s0):
    B, L, H, _ = q.shape
    C = L if L <= RET_CHUNK else math.gcd(L, RET_CHUNK)
    nc = L // C
    log_g = jnp.log1p(-(2.0 ** (-5.0 - jnp.arange(H, dtype=jnp.float32))))
    pos = jnp.arange(C, dtype=jnp.float32)
    rel = pos[:, None] - pos[None, :]
    dmask = jnp.where(rel >= 0, jnp.exp(log_g[:, None, None] * jnp.maximum(rel, 0.0)), 0.0)
    q_dec = jnp.exp(log_g[None, :] * (pos[:, None] + 1.0))
    k_dec = jnp.exp(log_g[None, :] * (C - 1.0 - pos[:, None]))
    chunk_dec = jnp.exp(log_g * C)

    def to_chunks(t):
        return jnp.moveaxis(t.reshape(B, nc, C, H, t.shape[-1]), 1, 0)

    def step(state, xs):
        qc, kc, vc = xs
        inner = jnp.einsum('bihd,bjhd->bhij', qc, kc) * dmask
        o = (jnp.einsum('bhij,bjhe->bihe', inner, vc)
             + jnp.einsum('bihd,bhde->bihe', qc, state) * q_dec[None, :, :, None])
        state = (state * chunk_dec[None, :, None, None]
                 + jnp.einsum('bjhd,bjhe->bhde', kc * k_dec[None, :, :, None], vc))
        return state, o

    s_fin, o = lax.scan(step, s0, (to_chunks(q), to_chunks(k), to_chunks(v)))
    o = jnp.moveaxis(o, 0, 1).reshape(B, L, H, v.shape[-1])
    return o, s_fin


def token_mixing(h, w_in, ret_gn, w_br_a, w_br_b, w_out, bufs, s0):
    B, L, _ = h.shape
    f32 = jnp.float32
    points = [int(p) for p in np.cumsum([DIL_QKV] * 3 + [RET_QK] * 2 + [RET_V] * 2 + [D_MODEL])]
    qa, ka, va, qr, kr, vr, gr, ga, gb = jnp.split(h @ w_in, points, axis=-1)

    qa = qa.reshape(B, L, N_DIL_HEADS, HEAD_DIM).astype(f32)
    ka = ka.reshape(B, L, N_DIL_HEADS, HEAD_DIM).astype(f32)
    va = va.reshape(B, L, N_DIL_HEADS, HEAD_DIM).astype(f32)
    slopes = alibi_slopes()
    outs, lses, new_bufs = [], [], []
    for g, (win, dil) in enumerate(DIL_GROUPS):
        sl = slice(g * HEADS_PER_GROUP, (g + 1) * HEADS_PER_GROUP)
        q_g, k_g, v_g = qa[:, :, sl], ka[:, :, sl], va[:, :, sl]
        if bufs is None:
            o, lse = dilated_prompt(q_g, k_g, v_g, dil, win // dil, slopes[sl])
            keep = min(win, L)
            nbuf = jnp.stack([k_g[:, L - keep:], v_g[:, L - keep:]], axis=2).astype(h.dtype)
        else:
            o, lse, nbuf = dilated_sample(q_g, k_g, v_g, bufs[g], dil, win // dil, slopes[sl])
        outs.append(o)
        lses.append(lse)
        new_bufs.append(nbuf)
    wgt = jax.nn.softmax(jnp.stack(lses), axis=0)
    o_a = jnp.sum(wgt[..., None] * jnp.stack(outs), axis=0).reshape(B, L, DIL_OUT)

    qr = qr.reshape(B, L, RET_HEADS, RET_DK).astype(f32)
    kr = kr.reshape(B, L, RET_HEADS, RET_DK).astype(f32) * (RET_DK ** -0.5)
    vr = vr.reshape(B, L, RET_HEADS, RET_DV).astype(f32)
    if s0 is None:
        st0 = jnp.zeros((B, RET_HEADS, RET_DK, RET_DV), f32)
        out_dtype = h.dtype
    else:
        st0 = s0.astype(f32)
        out_dtype = s0.dtype
    o_r, s_fin = retention(qr, kr, vr, st0)
    mu = jnp.mean(o_r, axis=-1, keepdims=True)
    var = jnp.mean(jnp.square(o_r - mu), axis=-1, keepdims=True)
    o_n = ((o_r - mu) * lax.rsqrt(var + GN_EPS)).reshape(B, L, RET_V) * ret_gn.astype(f32)
    y_r = jax.nn.silu(gr.astype(f32)) * o_n

    ya = o_a.astype(h.dtype) @ w_br_a
    yb = y_r.astype(h.dtype) @ w_br_b
    merged = jax.nn.sigmoid(ga) * ya + jax.nn.sigmoid(gb) * yb
    return merged @ w_out, new_bufs, s_fin.astype(out_dtype)


def trunk(x, bufs, s0, P):
    new_bufs = [[] for _ in DIL_GROUPS]
    new_states = []
    for l in range(DEPTH):
        x = x + 0.5 * swiglu(rmsnorm(x, P['norm_ffn1'][l]), P['ffn1_wg'][l], P['ffn1_wu'][l], P['ffn1_wd'][l])
        lb = None if bufs is None else tuple(b[l] for b in bufs)
        ls = None if s0 is None else s0[l]
        mix, nb, ns = token_mixing(rmsnorm(x, P['norm_mix'][l]), P['w_in'][l], P['ret_gn'][l],
                                   P['w_br_a'][l], P['w_br_b'][l], P['w_out'][l], lb, ls)
        x = x + mix
        x = x + 0.5 * swiglu(rmsnorm(x, P['norm_ffn2'][l]), P['ffn2_wg'][l], P['ffn2_wu'][l], P['ffn2_wd'][l])
        for g in range(len(DIL_GROUPS)):
            new_bufs[g].append(nb[g])
        new_states.append(ns)
    y = rmsnorm(x, P['norm_final'])
    return y, [jnp.stack(b) for b in new_bufs], jnp.stack(new_states)


def setup_inputs(seed: int = 0) -> dict:
    key = jax.random.key(seed)
    ks = jax.random.split(key, 24)
    f32 = jnp.float32
    nrm = lambda k, shape, s: jax.random.normal(k, shape, f32) * s
    gain = lambda k, shape: 1.0 + 0.02 * jax.random.normal(k, shape, f32)
    def buf(k, win):
        return nrm(k, (DEPTH, DEC_BATCH, min(win, PAST_LEN), 2, HEADS_PER_GROUP, HEAD_DIM), 1.0)
    return {
        'x_prompt': nrm(ks[0], (BATCH, SEQ, D_MODEL), 1.0),
        'x_sample': nrm(ks[1], (DEC_BATCH, DEC_SEQ, D_MODEL), 1.0),
        'cache_kv_w128': buf(ks[2], 128),
        'cache_kv_w512': buf(ks[3], 512),
        'cache_kv_w2048': buf(ks[4], 2048),
        'state_ret': nrm(ks[5], (DEPTH, DEC_BATCH, RET_HEADS, RET_DK, RET_DV), 0.5),
        'norm_ffn1': gain(ks[6], (DEPTH, D_MODEL)),
        'ffn1_wg': nrm(ks[7], (DEPTH, D_MODEL, D_FF), D_MODEL ** -0.5),
        'ffn1_wu': nrm(ks[8], (DEPTH, D_MODEL, D_FF), D_MODEL ** -0.5),
        'ffn1_wd': nrm(ks[9], (DEPTH, D_FF, D_MODEL), D_FF ** -0.5),
        'norm_mix': gain(ks[10], (DEPTH, D_MODEL)),
        'w_in': nrm(ks[11], (DEPTH, D_MODEL, N_IN), D_MODEL ** -0.5),
        'ret_gn': gain(ks[12], (DEPTH, RET_V)),
        'w_br_a': nrm(ks[13], (DEPTH, DIL_OUT, D_MODEL), DIL_OUT ** -0.5),
        'w_br_b': nrm(ks[14], (DEPTH, RET_V, D_MODEL), RET_V ** -0.5),
        'w_out': nrm(ks[15], (DEPTH, D_MODEL, D_MODEL), D_MODEL ** -0.5),
        'norm_ffn2': gain(ks[16], (DEPTH, D_MODEL)),
        'ffn2_wg': nrm(ks[17], (DEPTH, D_MODEL, D_FF), D_MODEL ** -0.5),
        'ffn2_wu': nrm(ks[18], (DEPTH, D_MODEL, D_FF), D_MODEL ** -0.5),
        'ffn2_wd': nrm(ks[19], (DEPTH, D_FF, D_MODEL), D_FF ** -0.5),
        'norm_final': gain(ks[20], (D_MODEL,)),
    }


def reference(x_prompt, x_sample, cache_kv_w128, cache_kv_w512, cache_kv_w2048, state_ret,
              norm_ffn1, ffn1_wg, ffn1_wu, ffn1_wd, norm_mix, w_in, ret_gn, w_br_a, w_br_b, w_out,
              norm_ffn2, ffn2_wg, ffn2_wu, ffn2_wd, norm_final):
    P = dict(norm_ffn1=norm_ffn1, ffn1_wg=ffn1_wg, ffn1_wu=ffn1_wu, ffn1_wd=ffn1_wd,
             norm_mix=norm_mix, w_in=w_in, ret_gn=ret_gn, w_br_a=w_br_a, w_br_b=w_br_b, w_out=w_out,
             norm_ffn2=norm_ffn2, ffn2_wg=ffn2_wg, ffn2_wu=ffn2_wu, ffn2_wd=ffn2_wd, norm_final=norm_final)
    y_prompt, bufs_p, ret_p = trunk(x_prompt, None, None, P)
    y_sample, bufs_s, ret_s = trunk(x_sample, (cache_kv_w128, cache_kv_w512, cache_kv_w2048), state_ret, P)
    return (y_prompt, y_sample, bufs_p[0], bufs_s[0], bufs_p[1], bufs_s[1], bufs_p[2], bufs_s[2], ret_p, ret_s)
```

```python
import math
from contextlib import ExitStack
from functools import partial

import numpy as np
import ml_dtypes

import concourse.bass as bass
import concourse.mybir as mybir
from concourse.bass_utils import run_bass_kernel_spmd

F32 = mybir.dt.float32
BF16 = mybir.dt.bfloat16
AF = mybir.ActivationFunctionType
ALU = mybir.AluOpType

S = 4096
D = 1024
DFF = 2816
NJ = DFF // 128
NIN = 9728
DEPTH = 2
NS = 4
TT = 1024
NTILE = S // TT
NCORES = 8
NORM_EPS = 1e-6
GN_EPS = 1e-6
DILS = (1, 4, 16)
WINS = (128, 512, 2048)
NHEAD_RET = 4

ENG_NAMES = ("pe", "act", "dve", "pool", "sp")
SAME_ENGINE_SYNC = True
N_DMA_SEMS = 24


class Buf:
    __slots__ = ("name", "w", "r")

    def __init__(self, name=""):
        self.name = name
        self.w = None
        self.r = {}


class _Op:
    __slots__ = ("eng", "fn", "deps", "dma", "signal", "snap")

    def __init__(self, eng, fn):
        self.eng = eng
        self.fn = fn
        self.deps = []
        self.dma = None
        self.signal = False
        self.snap = None


class Tracer:
    def __init__(self, nc):
        self.nc = nc
        self.h = {"pe": nc.tensor, "act": nc.scalar, "dve": nc.vector, "pool": nc.gpsimd, "sp": nc.sync}
        self.ops = []
        self.eng_ops = {e: [] for e in ENG_NAMES}
        self.known = {e: {e2: -1 for e2 in ENG_NAMES} for e in ENG_NAMES}
        self.known_dma = {e: set() for e in ENG_NAMES}
        self.dmas = []
        self.dma_q_count = {e: 0 for e in ENG_NAMES}
        self.bar_dma_start = 0

    def _add_dep(self, op, tok):
        if tok is None:
            return
        eng = op.eng
        if tok[0] == "e":
            _, e2, n = tok
            if e2 == eng and (eng == "pe" or eng == "sp" or not SAME_ENGINE_SYNC):
                return
            if n <= self.known[eng][e2]:
                return
            op.deps.append(tok)
            src = self.eng_ops[e2][n]
            src.signal = True
            kn = self.known[eng]
            kn[e2] = n
            for e3, v in src.snap.items():
                if v > kn[e3]:
                    kn[e3] = v
        else:
            did = tok[1]
            if did in self.known_dma[eng]:
                return
            op.deps.append(tok)
            self.known_dma[eng].add(did)
            snap = self.dmas[did][2]
            kn = self.known[eng]
            for e3, v in snap.items():
                if v > kn[e3]:
                    kn[e3] = v

    def op(self, eng, fn, reads=(), writes=(), dma=False, extra=()):
        op = _Op(eng, fn)
        idx = len(self.eng_ops[eng])
        if dma:
            did = len(self.dmas)
            op.dma = did
            tok = ("d", did)
        else:
            tok = ("e", eng, idx)
        for t in extra:
            self._add_dep(op, t)
        for b in reads:
            self._add_dep(op, b.w)
        for b in writes:
            self._add_dep(op, b.w)
            for rt in b.r.values():
                self._add_dep(op, rt)
        for b in reads:
            if dma:
                b.r[tok] = tok
            else:
                b.r[eng] = tok
        for b in writes:
            b.w = tok
            b.r = {}
        op.snap = dict(self.known[eng])
        if dma:
            self.dmas.append((eng, self.dma_q_count[eng], dict(self.known[eng])))
            self.dma_q_count[eng] += 1
        self.eng_ops[eng].append(op)
        self.ops.append(op)
        return tok

    def barrier(self):
        extra = [("d", i) for i in range(self.bar_dma_start, len(self.dmas)) if self.dmas[i][0] != "pool"]
        self.bar_dma_start = len(self.dmas)
        for e in ENG_NAMES:
            if e != "sp" and self.eng_ops[e]:
                extra.append(("e", e, len(self.eng_ops[e]) - 1))
        b = Buf("barrier")
        sp = self.h["sp"]
        self.op("sp", lambda: sp.nop(), writes=[b], extra=extra)
        for e in ENG_NAMES:
            if e != "sp":
                h = self.h[e]
                self.op(e, (lambda h=h: h.nop()), reads=[b])

    def emit(self):
        nc = self.nc
        with ExitStack() as es:
            esem = {e: es.enter_context(nc.semaphore("s_" + e)) for e in ENG_NAMES}
            dsem = {}
            for q in ENG_NAMES:
                if self.dma_q_count[q]:
                    dsem[q] = [es.enter_context(nc.semaphore(f"d_{q}{i}")) for i in range(N_DMA_SEMS)]
            cum = {}
            for e in ENG_NAMES:
                c = 0
                arr = []
                for o in self.eng_ops[e]:
                    if o.signal and o.dma is None:
                        c += 1
                    arr.append(c)
                cum[e] = arr
            nwait = 0
            for o in self.ops:
                h = self.h[o.eng]
                for tok in o.deps:
                    if tok[0] == "e":
                        h.wait_ge(esem[tok[1]], cum[tok[1]][tok[2]])
                    else:
                        q, k, _ = self.dmas[tok[1]]
                        h.wait_ge(dsem[q][k % N_DMA_SEMS], 16 * (k // N_DMA_SEMS + 1))
                    nwait += 1
                if o.dma is not None:
                    q, k, _ = self.dmas[o.dma]
                    sem = dsem[q][k % N_DMA_SEMS]
                    if k >= N_DMA_SEMS:
                        h.wait_ge(sem, 16 * (k // N_DMA_SEMS))
                    o.fn().then_inc(sem, 16)
                else:
                    ins = o.fn()
                    if o.signal:
                        ins.then_inc(esem[o.eng], 1)
            h = self.h["sp"]
            for q, sems in dsem.items():
                n = self.dma_q_count[q]
                for i, sem in enumerate(sems):
                    cnt = (n - i + N_DMA_SEMS - 1) // N_DMA_SEMS if n > i else 0
                    if cnt > 0:
                        h.wait_ge(sem, 16 * cnt)
            self.stats = dict(n_ops=len(self.ops), n_wait=nwait,
                              per_eng={e: len(v) for e, v in self.eng_ops.items()}, n_dma=len(self.dmas))


def _gammas():
    return [1.0 - 2.0 ** (-5.0 - h) for h in range(NHEAD_RET)]


def _const_tables():
    bf = ml_dtypes.bfloat16
    eb = np.zeros((3, 4, 128, 512), np.float64)
    kk = np.arange(128)[:, None].astype(np.float64)
    qi = np.arange(128)[None, :].astype(np.float64)
    for g in range(3):
        for c in range(4):
            for hp in range(2):
                hglob = 8 * g + 2 * c + hp
                slope = 2.0 ** (-8.0 * (hglob + 1) / 24.0)
                cc = slope * DILS[g]
                rel_prev = qi + 128.0 - kk
                prev = np.where(rel_prev <= 128.0, np.exp(-cc * rel_prev), 0.0)
                rel_cur = qi - kk
                cur = np.where(rel_cur >= 0.0, np.exp(-cc * np.maximum(rel_cur, 0.0)), 0.0)
                eb[g, c, :, hp * 128:hp * 128 + 128] = prev
                eb[g, c, :, 256 + hp * 128:256 + hp * 128 + 128] = cur
    gam = _gammas()
    sc = 128.0 ** -0.5
    dmaskT = np.zeros((128, 4 * 128), np.float64)
    kdec = np.zeros((128, 4 * 128), np.float64)
    qdec = np.zeros((128, 4 * 128), np.float64)
    jj = np.arange(128)[:, None].astype(np.float64)
    ii = np.arange(128)[None, :].astype(np.float64)
    for h in range(4):
        dm = np.where(ii >= jj, gam[h] ** np.maximum(ii - jj, 0.0), 0.0) * sc
        dmaskT[:, h * 128:(h + 1) * 128] = dm
        kdec[:, h * 128:(h + 1) * 128] = (gam[h] ** (127.0 - jj)) * sc
        qdec[:, h * 128:(h + 1) * 128] = gam[h] ** (ii + 1.0)
    ident = np.eye(128)
    sbias = np.zeros((128, 24), np.float64)
    kidx = np.arange(128).astype(np.float64)
    for g in range(3):
        for h in range(8):
            slope = 2.0 ** (-8.0 * (8 * g + h + 1) / 24.0)
            sbias[:, g * 8 + h] = -slope * DILS[g] * (128.0 - kidx)
    sel = np.zeros((128, 4, 4), np.float64)
    for s_ in range(4):
        sel[:, s_, s_] = 1.0
    return dict(
        c_sbias=sbias.astype(np.float32),
        c_sel=sel.reshape(128, 16).astype(np.float32),
        c_eb=eb.reshape(12, 128, 512).astype(bf),
        c_dmask=dmaskT.astype(np.float32),
        c_kdec=kdec.astype(np.float32),
        c_qdec=qdec.astype(np.float32),
        c_ident=ident.astype(np.float32),
        c_identb=ident.astype(bf),
    )


class Arena:
    def __init__(self, t, nbytes):
        self.t = t
        self.nbytes = nbytes
        self.off = 0

    def alloc(self, free_shape, dtype):
        esz = 4 if dtype == F32 else 2
        nel = int(np.prod(free_shape))
        nb = nel * esz
        start = self.off
        self.off = start + ((nb + 63) // 64) * 64
        assert self.off <= self.nbytes, f"SBUF arena overflow {self.off} > {self.nbytes}"
        self.last = start
        return self.view(start, free_shape, dtype)

    def view(self, start, free_shape, dtype):
        esz = 4 if dtype == F32 else 2
        nb = int(np.prod(free_shape)) * esz
        v = self.t[:, start // 2:(start + nb) // 2]
        if dtype == F32:
            v = v.bitcast(F32)
        if len(free_shape) == 2:
            v = v.rearrange("p (a b) -> p a b", b=free_shape[1])
        elif len(free_shape) == 3:
            v = v.rearrange("p (a b c) -> p a b c", b=free_shape[1], c=free_shape[2])
        return v


class Rot:
    def __init__(self, items):
        self.items = list(items)
        self.i = 0

    def next(self):
        it = self.items[self.i % len(self.items)]
        self.i += 1
        return it


def build_program(debug=False, stop_after=None, only=None, cut=99):
    nc = bass.Bass("TRN2", target_bir_lowering=False)
    T = Tracer(nc)
    es = ExitStack()

    def din(name, shape, dt=F32):
        return nc.dram_tensor(name, list(shape), dt, kind="ExternalInput").ap()

    def dout(name, shape, dt=F32):
        return nc.dram_tensor(name, list(shape), dt, kind="ExternalOutput").ap()

    def dscr(name, shape, dt):
        kind = "ExternalOutput" if debug else "Internal"
        if only is not None and name in ("QK", "VX", "QRT", "KRT", "KRK", "VR", "GT"):
            kind = "ExternalInput"
        return nc.dram_tensor(name, list(shape), dt, kind=kind).ap()

    xp = din("xp", [S, D])
    xs = din("xs", [NS, D])
    cache_in = [din(f"c{w}", [DEPTH, NS, w, 1024]) for w in WINS]
    sret = din("sret", [DEPTH, NS, 4, 128, 256])
    gains_d = din("gains", [128, 72])
    w_g = [din("ffn1_wg", [DEPTH, D, DFF]), din("ffn2_wg", [DEPTH, D, DFF])]
    w_u = [din("ffn1_wu", [DEPTH, D, DFF]), din("ffn2_wu", [DEPTH, D, DFF])]
    w_d = [din("ffn1_wd", [DEPTH, DFF, D]), din("ffn2_wd", [DEPTH, DFF, D])]
    w_in = din("w_in", [DEPTH, D, NIN])
    w_a = din("w_br_a", [DEPTH, 512, D])
    w_b = din("w_br_b", [DEPTH, D, D])
    w_o = din("w_out", [DEPTH, D, D])
    c_eb = din("c_eb", [12, 128, 512], BF16)
    c_dmask = din("c_dmask", [128, 512])
    c_kdec = din("c_kdec", [128, 512])
    c_qdec = din("c_qdec", [128, 512])
    c_ident = din("c_ident", [128, 128])
    c_identb = din("c_identb", [128, 128], BF16)
    c_sbias = din("c_sbias", [128, 24])
    c_sel = din("c_sel", [128, 16])

    yp = dout("yp", [S, D])
    ys = dout("ys", [NS, D])
    kvp = [dout(f"kv{w}p", [DEPTH, w, 1024]) for w in WINS]
    kvs = [dout(f"kv{w}s", [DEPTH, NS, w, 1024]) for w in WINS]
    retp = dout("retp", [DEPTH, 4, 128, 256])
    rets = dout("rets", [DEPTH, NS, 4, 128, 256])

    XRES = dscr("XRES", [8, 128, S], F32)
    QK = dscr("QK", [2, 3, 4, 128, S], BF16)
    VX = dscr("VX", [3, 4, S, 256], BF16)
    QRT = dscr("QRT", [4, 128, S], BF16)
    KRT = dscr("KRT", [4, 128, S], BF16)
    KRK = dscr("KRK", [S, 512], BF16)
    VR = dscr("VR", [S, 1024], BF16)
    GT = dscr("GT", [24, 128, S], BF16)
    OAT = dscr("OAT", [4, 128, S], BF16)
    YRT = dscr("YRT", [8, 128, S], BF16)
    STOK = dscr("STOK", [NS, 6656], F32)

    ARENA_BYTES = 206 * 1024
    arena_t = es.enter_context(nc.sbuf_tensor("arena", [128, ARENA_BYTES // 2], BF16))
    AR = Arena(arena_t, ARENA_BYTES)
    psw = [es.enter_context(nc.psum_tensor(f"psw{i}", [128, 1024], F32)) for i in range(4)]
    psb = []
    for i in range(4):
        psb.append(psw[i][:, 0:512])
        psb.append(psw[i][:, 512:1024])
    bps = [Buf(f"ps{i}") for i in range(8)]

    ws = [AR.alloc([8, 512], BF16) for _ in range(4)]
    bws = [Buf(f"ws{i}") for i in range(4)]
    wds = [AR.alloc([NJ, 256], BF16) for _ in range(2)]
    bwds = [Buf(f"wd{i}") for i in range(2)]
    ident = AR.alloc([128], F32)
    identb = AR.alloc([128], BF16)
    ones_b = AR.alloc([128], BF16)
    gains = AR.alloc([72], F32)
    gains32 = AR.alloc([72], F32)
    kdec = AR.alloc([512], F32)
    bconst = Buf("const")
    xsT = AR.alloc([8, NS], F32)
    hsT = AR.alloc([8, NS], BF16)
    actS = AR.alloc([NJ, NS], BF16)
    rstdS = AR.alloc([NS], F32)
    qrS = AR.alloc([4, NS], F32)
    gS = AR.alloc([24, NS], BF16)
    bxs = [Buf() for _ in range(8)]
    bhs = [Buf() for _ in range(8)]
    bas = [Buf() for _ in range(NJ)]
    brss = Buf()
    bqrS = Buf()
    bgS = Buf()
    PERSIST_END = AR.off

    sp, act, dve, pe, pool = nc.sync, nc.scalar, nc.vector, nc.tensor, nc.gpsimd

    def dma(q, out, in_, reads=(), writes=()):
        h = T.h[q]
        return T.op(q, (lambda: h.dma_start(out=out, in_=in_)), reads=reads, writes=writes, dma=True)

    def mm(out, lhsT, rhs, start, stop, reads, writes):
        return T.op("pe", (lambda: pe.matmul(out, lhsT, rhs, start=start, stop=stop)), reads=reads, writes=writes)

    def tr(out, in_, idn, reads, writes):
        return T.op("pe", (lambda: pe.transpose(out, in_, idn)), reads=reads, writes=writes)

    def actf(out, in_, func, reads, writes, scale=1.0, bias=None):
        if bias is None:
            return T.op("act", (lambda: act.activation(out=out, in_=in_, func=func, scale=scale)),
                        reads=reads, writes=writes)
        return T.op("act", (lambda: act.activation(out=out, in_=in_, func=func, scale=scale, bias=bias)),
                    reads=reads, writes=writes)

    def v_recip(out, in_, reads, writes):
        return T.op("dve", (lambda: dve.reciprocal(out=out, in_=in_)), reads=reads, writes=writes)

    def v_copy(eng, out, in_, reads, writes):
        if eng == "act":
            return T.op("act", (lambda: act.copy(out=out, in_=in_)), reads=reads, writes=writes)
        h = T.h[eng]
        return T.op(eng, (lambda: h.tensor_copy(out=out, in_=in_)), reads=reads, writes=writes)

    def v_tt(out, in0, in1, op, reads, writes, eng="dve"):
        h = T.h[eng]
        return T.op(eng, (lambda: h.tensor_tensor(out=out, in0=in0, in1=in1, op=op)), reads=reads, writes=writes)

    def v_ts(out, in0, s1, s2, op0, op1, reads, writes, eng="dve"):
        h = T.h[eng]
        if s2 is None:
            return T.op(eng, (lambda: h.tensor_scalar(out=out, in0=in0, scalar1=s1, scalar2=None, op0=op0)),
                        reads=reads, writes=writes)
        return T.op(eng, (lambda: h.tensor_scalar(out=out, in0=in0, scalar1=s1, scalar2=s2, op0=op0, op1=op1)),
                    reads=reads, writes=writes)

    def v_stt(out, in0, scalar, in1, op0, op1, reads, writes, eng="dve"):
        h = T.h[eng]
        return T.op(eng, (lambda: h.scalar_tensor_tensor(out=out, in0=in0, scalar=scalar, in1=in1, op0=op0, op1=op1)),
                    reads=reads, writes=writes)

    dma("sp", ident, c_ident, writes=[bconst])
    dma("sp", identb, c_identb, writes=[bconst])
    dma("sp", gains, gains_d, writes=[bconst])
    dma("sp", kdec, c_kdec, writes=[bconst])
    T.op("dve", lambda: dve.memset(ones_b, 1.0), writes=[bconst])
    v_ts(gains32, gains, 1.0, None, ALU.mult, None, reads=[bconst], writes=[bconst])

    def load_sample_x():
        AR.off = PERSIST_END
        xs_sb = AR.alloc([D], F32)
        bxs_sb = Buf()
        dma("sp", xs_sb[0:NS, :], xs, writes=[bxs_sb])
        for c in range(8):
            tr(psb[0][:, c * NS:(c + 1) * NS], xs_sb[0:NS, c * 128:(c + 1) * 128], ident[0:NS, 0:NS],
               reads=[bxs_sb, bconst], writes=[bps[0]])
        v_copy("dve", xsT[:, :, :], psb[0][:, 0:8 * NS].rearrange("p (c s) -> p c s", s=NS), reads=[],
               writes=[bps[0]] + bxs)

    cache_pieces = []
    for l_ in range(DEPTH):
        for s_ in range(NS):
            for g_, wb_ in enumerate(WINS):
                npiece = 4 if wb_ == 2048 else 1
                rows = wb_ - 1
                step = (rows + npiece - 1) // npiece
                for r0_ in range(0, rows, step):
                    r1_ = min(rows, r0_ + step)
                    cache_pieces.append((kvs[g_][l_, s_, r0_:r1_, :].rearrange("r f -> (r f)"),
                                         cache_in[g_][l_, s_, r0_ + 1:r1_ + 1, :].rearrange("r f -> (r f)")))

    def copy_cache_piece():
        if cache_pieces and only is None:
            dst, src = cache_pieces.pop(0)
            dma("act", dst, src)

    CG = [(i * 512, 512) for i in range(5)] + [(2560, 256)]

    class WStream:
        def __init__(self, slots, bufs, prefetch):
            self.slots, self.bufs, self.prefetch = slots, bufs, prefetch
            self.items = []
            self.issued = 0
            self.used = 0

        def add(self, src_ap, kc, ncols):
            self.items.append((src_ap, kc, ncols))

        def _issue(self, i):
            src, kc, ncols = self.items[i]
            s = i % len(self.slots)
            dma("pool", self.slots[s][:, 0:kc, 0:ncols], src, writes=[self.bufs[s]])

        def get(self):
            i = self.used
            self.used += 1
            lim = min(len(self.items), i + self.prefetch + 1)
            while self.issued < lim:
                self._issue(self.issued)
                self.issued += 1
            s = i % len(self.slots)
            return self.slots[s], self.bufs[s]

    WS1 = WStream(ws, bws, 2)
    WS2 = WStream(wds, bwds, 1)

    def plan_ffn(which, l, ntiles):
        for _ in range(ntiles):
            wgv = w_g[which][l].rearrange("(k p) n -> p k n", p=128)
            wuv = w_u[which][l].rearrange("(k p) n -> p k n", p=128)
            for (c0, n) in CG:
                WS1.add(wgv[:, :, c0:c0 + n], 8, n)
                WS1.add(wuv[:, :, c0:c0 + n], 8, n)
            wdv = w_d[which][l].rearrange("(j p) n -> p j n", p=128)
            for m2 in range(4):
                WS2.add(wdv[:, :, m2 * 256:(m2 + 1) * 256], NJ, 256)

    def plan_phaseA_tile(l):
        plan_ffn(0, l, 1)
        wv = w_in[l].rearrange("(k p) n -> p k n", p=128)
        for G in range(19):
            WS1.add(wv[:, :, G * 512:(G + 1) * 512], 8, 512)

    def plan_phaseC_tile(l):
        wav = w_a[l].rearrange("(k p) n -> p k n", p=128)
        wbv = w_b[l].rearrange("(k p) n -> p k n", p=128)
        wov = w_o[l].rearrange("(k p) n -> p k n", p=128)
        for hf in range(2):
            WS1.add(wav[:, :, hf * 512:(hf + 1) * 512], 4, 512)
            WS1.add(wbv[:, :, hf * 512:(hf + 1) * 512], 8, 512)
        for hf in range(2):
            WS1.add(wov[:, :, hf * 512:(hf + 1) * 512], 8, 512)
        plan_ffn(1, l, 1)

    for l in range(DEPTH):
        for t in range(NTILE):
            plan_phaseA_tile(l)
        for t in range(NTILE):
            plan_phaseC_tile(l)

    class PsRot:
        def __init__(self, idxs):
            self.idxs = list(idxs)
            self.i = 0

        def next(self):
            k = self.idxs[self.i % len(self.idxs)]
            self.i += 1
            return psb[k], bps[k]

    class Seg:
        pass

    def make_segs(xT, hT, actT, rstd, bx, bh, bact, brs):
        segs = []
        for s in range(2):
            sg = Seg()
            sg.n = 512
            sg.s = s
            sl = slice(s * 512, (s + 1) * 512)
            sg.x = (lambda c, sl=sl: xT[:, c, sl])
            sg.h = (lambda c, sl=sl: hT[:, c, sl])
            sg.a = (lambda j, sl=sl: actT[:, j, sl])
            sg.rstd = rstd[:, sl]
            sg.bx = [bx[c][s] for c in range(8)]
            sg.bh = [bh[c][s] for c in range(8)]
            sg.ba = [bact[j][s] for j in range(NJ)]
            sg.brs = brs[s]
            segs.append(sg)
        return segs

    sseg = Seg()
    sseg.n = NS
    sseg.s = None
    sseg.x = (lambda c: xsT[:, c, :])
    sseg.h = (lambda c: hsT[:, c, :])
    sseg.a = (lambda j: actS[:, j, :])
    sseg.rstd = rstdS
    sseg.bx, sseg.bh, sseg.ba, sseg.brs = bxs, bhs, bas, brss

    def rmsnorm(segs, gcol, psr):
        for sg in segs:
            n = sg.n
            for c in range(8):
                actf(sg.a(c), sg.x(c), AF.Square, reads=[sg.bx[c]], writes=[sg.ba[c]])
            pst, pbf = psr.next()
            for c in range(8):
                mm(pst[:, :n], ones_b, sg.a(c), c == 0, c == 7, reads=[sg.ba[c], bconst], writes=[pbf])
            actf(sg.rstd, pst[:, :n], AF.Sqrt, reads=[], writes=[pbf, sg.brs], scale=1.0 / 1024.0, bias=NORM_EPS)
            v_recip(sg.rstd, sg.rstd, reads=[], writes=[sg.brs])
            for c in range(8):
                v_stt(sg.h(c), sg.x(c), gains32[:, gcol + c:gcol + c + 1], sg.rstd, ALU.mult, ALU.mult,
                      reads=[sg.bx[c], sg.brs, bconst], writes=[sg.bh[c]])

    def ffn(segs, tmp_of, btmp_of):
        psr = PsRot(range(6))
        ti = 0
        for (c0, ncol) in CG:
            copy_cache_piece()
            wg_s, bwg = WS1.get()
            wu_s, bwu = WS1.get()
            for jj in range(ncol // 128):
                j = c0 // 128 + jj
                for sg in segs:
                    n = sg.n
                    pg, bpg = psr.next()
                    pu, bpu = psr.next()
                    for k in range(8):
                        mm(pg[:, :n], wg_s[:, k, jj * 128:(jj + 1) * 128], sg.h(k), k == 0, k == 7,
                           reads=[bwg, sg.bh[k]], writes=[bpg])
                        mm(pu[:, :n], wu_s[:, k, jj * 128:(jj + 1) * 128], sg.h(k), k == 0, k == 7,
                           reads=[bwu, sg.bh[k]], writes=[bpu])
                    tmp, btmp = tmp_of(ti), btmp_of(ti)
                    ti += 1
                    actf(tmp[:, :n], pg[:, :n], AF.Silu, reads=[], writes=[bpg, btmp])
                    v_tt(sg.a(j), tmp[:, :n], pu[:, :n], ALU.mult, reads=[btmp], writes=[bpu, sg.ba[j]])
        for m2 in range(4):
            wd_s, bwd = WS2.get()
            for mm_ in range(2):
                m = m2 * 2 + mm_
                for sg in segs:
                    n = sg.n
                    py, bpy = psr.next()
                    for j in range(NJ):
                        mm(py[:, :n], wd_s[:, j, mm_ * 128:(mm_ + 1) * 128], sg.a(j), j == 0, j == NJ - 1,
                           reads=[bwd, sg.ba[j]], writes=[bpy])
                    v_stt(sg.x(m), py[:, :n], 0.5, sg.x(m), ALU.mult, ALU.add, reads=[], writes=[bpy, sg.bx[m]])

    def phase_A(l):
        AR.off = PERSIST_END
        xT = AR.alloc([8, TT], F32)
        hT = AR.alloc([8, TT], BF16)
        actT = AR.alloc([NJ, TT], BF16)
        act_off = AR.last

        def act_f32(j):
            return AR.view(act_off + j * 2048, [512], F32)
        rstd = AR.alloc([TT], F32)
        tmps = [AR.alloc([512], BF16) for _ in range(3)]
        ust = AR.alloc([8, 2048], BF16)
        vxst = [AR.alloc([8, 128], BF16) for _ in range(4)]
        stokst = [AR.alloc([512], F32) for _ in range(2)]
        bstok = [Buf(), Buf()]
        bx = [[Buf() for _ in range(2)] for _ in range(8)]
        bh = [[Buf() for _ in range(2)] for _ in range(8)]
        bact = [[Buf() for _ in range(2)] for _ in range(NJ)]
        brs = [Buf(), Buf()]
        btmps = [Buf() for _ in range(3)]
        bust = [Buf() for _ in range(8)]
        bvxst = [Buf() for _ in range(4)]
        segs = make_segs(xT, hT, actT, rstd, bx, bh, bact, brs)
        for i in range(4):
            T.op("dve", (lambda i=i: dve.memset(vxst[i], 1.0)), writes=[bvxst[i]])

        def chunk_bufs(j):
            return [bact[j][0], bact[j][1]]

        fm_rot = Rot([8, 9, 10, 11, 12, 13])
        vr_rot = Rot([14, 15, 16])
        krk_rot = Rot([(17, 0), (17, 1)])
        f32_rot = Rot([18, 19, 20, 21])
        ev_rot = Rot(["act", "dve"])

        for t in range(NTILE):
            t0 = t * TT
            if l == 0:
                xin = AR.view(act_off, [4, D], F32)
                psr = PsRot(range(8))
                for half in range(2):
                    for tb in range(4):
                        r0 = t0 + half * 512 + tb * 128
                        dma("sp", xin[:, tb, :], xp[r0:r0 + 128, :], writes=chunk_bufs(2 * tb) + chunk_bufs(2 * tb + 1))
                    for c in range(8):
                        pst, pbf = psr.next()
                        for tb in range(4):
                            tr(pst[:, tb * 128:(tb + 1) * 128], xin[:, tb, c * 128:(c + 1) * 128], ident,
                               reads=chunk_bufs(2 * tb) + chunk_bufs(2 * tb + 1) + [bconst], writes=[pbf])
                        v_copy(ev_rot.next(), xT[:, c, half * 512:(half + 1) * 512], pst[:, :], reads=[],
                               writes=[pbf, bx[c][half]])
            elif t == 0:
                dma("sp", xT[:, :, :], XRES.rearrange("c p t -> p c t")[:, :, t0:t0 + TT],
                    writes=[bx[c][s] for c in range(8) for s in range(2)])

            last = (t == NTILE - 1)
            segs_t = segs + ([sseg] if last else [])
            rmsnorm(segs_t, l * 32 + 0, PsRot([6, 7]))
            ffn(segs_t, lambda i: tmps[i % 3], lambda i: btmps[i % 3])
            rmsnorm(segs_t, l * 32 + 8, PsRot([6, 7]))
            dma("sp", XRES.rearrange("c p t -> p c t")[:, :, t0:t0 + TT], xT[:, :, :],
                reads=[bx[c][s] for c in range(8) for s in range(2)])
            if l > 0 and t + 1 < NTILE:
                dma("sp", xT[:, :, :], XRES.rearrange("c p t -> p c t")[:, :, t0 + TT:t0 + 2 * TT],
                    writes=[bx[c][s] for c in range(8) for s in range(2)])

            psr = PsRot(range(8))

            def proj_fm(slot, bslot, cc, sg):
                pst, pbf = psr.next()
                for k in range(8):
                    mm(pst[:, :sg.n], slot[:, k, cc * 128:(cc + 1) * 128], sg.h(k), k == 0, k == 7,
                       reads=[bslot, sg.bh[k]], writes=[pbf])
                return pst, pbf

            def proj_tm(slot, bslot, tb):
                pst, pbf = psr.next()
                s = tb // 4
                for k in range(8):
                    mm(pst[:, :], hT[:, k, tb * 128:(tb + 1) * 128], slot[:, k, :], k == 0, k == 7,
                       reads=[bslot, bh[k][s]], writes=[pbf])
                return pst, pbf

            def stage_fm(j):
                return actT[:, j, :], chunk_bufs(j)

            def sample_proj(G, slot, bslot):
                if G <= 12:
                    pst, pbf = psr.next()
                    for k in range(8):
                        mm(pst[0:NS, :], hsT[:, k, :], slot[:, k, :], k == 0, k == 7, reads=[bslot, bhs[k]], writes=[pbf])
                    si = G % 2
                    v_copy(ev_rot.next(), stokst[si][0:NS, :], pst[0:NS, :], reads=[], writes=[pbf, bstok[si]])
                    dma("sp", STOK[:, G * 512:(G + 1) * 512], stokst[si][0:NS, :], reads=[bstok[si]])
                    if 3 <= G < 9:
                        g_ = (G - 3) % 3
                        half = 0 if G < 6 else 1
                        wb_ = WINS[g_]
                        dma("sp", kvs[g_][l, :, wb_ - 1, half * 512:(half + 1) * 512], stokst[si][0:NS, :],
                            reads=[bstok[si]])
                if G == 9:
                    for cc in range(4):
                        pst, pbf = proj_fm(slot, bslot, cc, sseg)
                        v_copy(ev_rot.next(), qrS[:, cc, :], pst[:, 0:NS], reads=[], writes=[pbf, bqrS])
                if G >= 13:
                    func = AF.Silu if G < 15 else AF.Sigmoid
                    for cc in range(4):
                        pst, pbf = proj_fm(slot, bslot, cc, sseg)
                        actf(gS[:, (G - 13) * 4 + cc, :], pst[:, 0:NS], func, reads=[], writes=[pbf, bgS])

            for G in range(19):
                slot, bslot = WS1.get()
                if last:
                    sample_proj(G, slot, bslot)
                if G < 6:
                    which, g = (0, G) if G < 3 else (1, G - 3)
                    for cc in range(4):
                        if g < 2:
                            j = fm_rot.next()
                            st, bst = stage_fm(j)
                        for sg in segs:
                            pst, pbf = proj_fm(slot, bslot, cc, sg)
                            eng = ev_rot.next()
                            if g == 0:
                                v_copy(eng, st[:, sg.s * 512:(sg.s + 1) * 512], pst[:, :], reads=[], writes=[pbf, bst[sg.s]])
                            elif g == 1:
                                v_copy(eng, st[:, sg.s * 512:(sg.s + 1) * 512].rearrange("p (r i) -> p r i", r=4),
                                       pst[:, :].rearrange("p (i r) -> p r i", r=4), reads=[], writes=[pbf, bst[sg.s]])
                            else:
                                i0 = ((t0 + sg.s * 512) % 2048) // 16
                                v_copy(eng, ust[:, which * 4 + cc, :].rearrange("p (r i) -> p r i", r=16)[:, :, i0:i0 + 32],
                                       pst[:, :].rearrange("p (i r) -> p r i", r=16), reads=[],
                                       writes=[pbf, bust[which * 4 + cc]])
                        if g < 2:
                            dma("sp", QK[which, g, cc, :, t0:t0 + TT], st, reads=bst)
                        elif t % 2 == 1:
                            u0 = (t // 2) * 2048
                            dma("sp", QK[which, 2, cc, :, u0:u0 + 2048], ust[:, which * 4 + cc, :],
                                reads=[bust[which * 4 + cc]])
                    if which == 1:
                        win = WINS[g]
                        for tb in range(8):
                            r0 = t0 + tb * 128
                            if r0 >= S - win:
                                pst, pbf = proj_tm(slot, bslot, tb)
                                j = f32_rot.next()
                                st = act_f32(j)
                                v_copy(ev_rot.next(), st, pst[:, :], reads=[], writes=[pbf] + chunk_bufs(j))
                                o0 = r0 - (S - win)
                                dma("sp", kvp[g][l, o0:o0 + 128, 0:512], st, reads=chunk_bufs(j))
                elif G < 9:
                    g = G - 6
                    win = WINS[g]
                    for tb in range(8):
                        r0 = t0 + tb * 128
                        pst, pbf = proj_tm(slot, bslot, tb)
                        vi = (tb + 8 * g) % 4
                        dst = vxst[vi].rearrange("p (m hp) e -> p m hp e", hp=2)
                        src = pst[:, :].rearrange("p (m hp e) -> p m hp e", hp=2, e=64)
                        v_copy(ev_rot.next(), dst[:, :, 0, 0:64], src[:, :, 0, :], reads=[], writes=[pbf, bvxst[vi]])
                        v_copy(ev_rot.next(), dst[:, :, 1, 64:128], src[:, :, 1, :], reads=[], writes=[pbf, bvxst[vi]])
                        dma("sp", VX[g, :, r0:r0 + 128, :].rearrange("c t f -> t c f"),
                            vxst[vi].rearrange("p (c h) e -> p c (h e)", h=2), reads=[bvxst[vi]])
                        if r0 >= S - win:
                            j = f32_rot.next()
                            st = act_f32(j)
                            v_copy(ev_rot.next(), st, pst[:, :], reads=[], writes=[pbf] + chunk_bufs(j))
                            o0 = r0 - (S - win)
                            dma("sp", kvp[g][l, o0:o0 + 128, 512:1024], st, reads=chunk_bufs(j))
                elif G in (9, 10):
                    dstT = QRT if G == 9 else KRT
                    for cc in range(4):
                        j = fm_rot.next()
                        st, bst = stage_fm(j)
                        for sg in segs:
                            pst, pbf = proj_fm(slot, bslot, cc, sg)
                            v_copy(ev_rot.next(), st[:, sg.s * 512:(sg.s + 1) * 512], pst[:, :], reads=[],
                                   writes=[pbf, bst[sg.s]])
                        dma("sp", dstT[cc, :, t0:t0 + TT], st, reads=bst)
                    if G == 10:
                        for tb in range(8):
                            r0 = t0 + tb * 128
                            pst, pbf = proj_tm(slot, bslot, tb)
                            j, hf = krk_rot.next()
                            st = actT[:, j, hf * 512:(hf + 1) * 512]
                            v_tt(st, pst[:, :], kdec, ALU.mult, reads=[bconst], writes=[pbf, bact[j][hf]])
                            dma("sp", KRK[r0:r0 + 128, :], st, reads=[bact[j][hf]])
                elif G < 13:
                    hf = G - 11
                    for tb in range(8):
                        r0 = t0 + tb * 128
                        pst, pbf = proj_tm(slot, bslot, tb)
                        j, h2 = krk_rot.next()
                        st = actT[:, j, h2 * 512:(h2 + 1) * 512]
                        v_copy(ev_rot.next(), st, pst[:, :], reads=[], writes=[pbf, bact[j][h2]])
                        dma("sp", VR[r0:r0 + 128, hf * 512:(hf + 1) * 512], st, reads=[bact[j][h2]])
                else:
                    gi = (G - 13) * 4
                    func = AF.Silu if G < 15 else AF.Sigmoid
                    for cc in range(4):
                        j = fm_rot.next()
                        st, bst = stage_fm(j)
                        for sg in segs:
                            pst, pbf = proj_fm(slot, bslot, cc, sg)
                            actf(st[:, sg.s * 512:(sg.s + 1) * 512], pst[:, :], func, reads=[], writes=[pbf, bst[sg.s]])
                        dma("sp", GT[gi + cc, :, t0:t0 + TT], st, reads=bst)
        T.barrier()

    def phase_B_attn(l):
        AR.off = PERSIST_END
        NE = 4
        PIPE = 2
        eb = AR.alloc([12, 512], BF16)
        SAB2 = [AR.alloc([2, 2048], F32) for _ in range(2)]
        qbd = [AR.alloc([16, 256], BF16) for _ in range(2)]
        kbuf = [AR.alloc([4096], BF16) for _ in range(2)]
        vbuf = [AR.alloc([32, 256], BF16) for _ in range(2)]
        ebuf = [AR.alloc([512], BF16) for _ in range(NE)]
        pbuf = [AR.alloc([512], BF16) for _ in range(NE)]
        oast = [AR.alloc([2048], BF16) for _ in range(2)]
        rd = AR.alloc([2048], F32)
        beb = Buf()
        bS2 = [Buf(), Buf()]
        bq = [Buf(), Buf()]
        bk = [Buf(), Buf()]
        bv = [[Buf() for _ in range(32)] for _ in range(2)]
        be = [Buf() for _ in range(NE)]
        bp = [Buf() for _ in range(NE)]
        bpp = [Buf() for _ in range(NE)]
        boa = [Buf(), Buf()]
        brd = Buf()
        dma("sp", eb, c_eb.rearrange("i p n -> p i n"), writes=[beb])
        ps_s = PsRot([0, 1, 2, 3])
        ps_o = PsRot([4, 5, 6, 7])
        for i_ in range(2):
            T.op("dve", (lambda i_=i_: dve.memset(qbd[i_], 0.0)), writes=[bq[i_]])
        groups = [(u, c, g) for u in range(2) for c in range(4) for g in range(3)]
        NG = len(groups)
        iters = [(gi, bl) for gi in range(NG) for bl in range(16)]
        NI = len(iters)

        def ginfo(gi):
            u, c, g = groups[gi]
            dil = DILS[g]
            nprev = dil if u == 1 else 0
            return u, c, g, dil, 128 * dil, 16 * u - nprev, 16 + nprev, gi % 2

        def load_group(gi):
            u, c, g, dil, unit_g, wstart, nblk, bi = ginfo(gi)
            qsrc = QK[0, g, c, :, u * 2048:(u + 1) * 2048].rearrange("p (b q) -> p b q", q=128)
            dma("sp", qbd[bi][0:64, :, 0:128], qsrc[0:64], writes=[bq[bi]])
            dma("sp", qbd[bi][64:128, :, 128:256], qsrc[64:128], writes=[bq[bi]])
            dma("sp", kbuf[bi][:, 0:nblk * 128], QK[1, g, c, :, wstart * 128:(wstart + nblk) * 128], writes=[bk[bi]])
            for n_ in range(wstart // dil, (wstart + nblk) // dil):
                sl0 = n_ * dil - wstart
                src = VX[g, c, n_ * unit_g:(n_ + 1) * unit_g, :].rearrange("(p r) f -> p r f", r=dil)
                dma("sp", vbuf[bi][:, sl0:sl0 + dil, :], src, writes=[bv[bi][sl] for sl in range(sl0, sl0 + dil)])

        st1 = {}

        def stage1(k):
            gi, bl = iters[k]
            u, c, g, dil, unit_g, wstart, nblk, bi = ginfo(gi)
            b = 16 * u + bl
            has_prev = (b - dil) >= 0
            sc = b - wstart
            spv = b - dil - wstart
            pss, bpss = ps_s.next()
            if has_prev:
                mm(pss[:, 0:256], kbuf[bi][:, spv * 128:(spv + 1) * 128], qbd[bi][:, bl, :], True, True,
                   reads=[bk[bi], bq[bi]], writes=[bpss])
            mm(pss[:, 256:512], kbuf[bi][:, sc * 128:(sc + 1) * 128], qbd[bi][:, bl, :], True, True,
               reads=[bk[bi], bq[bi]], writes=[bpss])
            e_i = k % NE
            ebt, pbt = ebuf[e_i], pbuf[e_i]
            ebg = eb[:, g * 4 + c, :]
            c0_ = 0 if has_prev else 256
            actf(ebt[:, c0_:512], pss[:, c0_:512], AF.Exp, reads=[], writes=[bpss, be[e_i]], scale=0.125)
            if has_prev:
                v_tt(pbt[:, 0:256], ebt[:, 0:256], ebg[:, 0:256], ALU.mult, reads=[be[e_i], beb], writes=[bpp[e_i]],
                     eng="pool")
            v_tt(pbt[:, 256:512], ebt[:, 256:512], ebg[:, 256:512], ALU.mult, reads=[be[e_i], beb], writes=[bp[e_i]])

        def stage2(k):
            gi, bl = iters[k]
            u, c, g, dil, unit_g, wstart, nblk, bi = ginfo(gi)
            b = 16 * u + bl
            has_prev = (b - dil) >= 0
            sc = b - wstart
            spv = b - dil - wstart
            e_i = k % NE
            pbt = pbuf[e_i]
            si = (u * 4 + c) % 2
            SAB, bS = SAB2[si], bS2[si]
            pso, bpso = ps_o.next()
            for hp in range(2):
                oc = slice(hp * 128, (hp + 1) * 128)
                if has_prev:
                    mm(pso[:, oc], vbuf[bi][:, spv, oc], pbt[:, hp * 128:hp * 128 + 128], True, False,
                       reads=[bv[bi][spv], bpp[e_i]], writes=[bpso])
                mm(pso[:, oc], vbuf[bi][:, sc, oc], pbt[:, 256 + hp * 128:256 + hp * 128 + 128], not has_prev, True,
                   reads=[bv[bi][sc], bp[e_i]], writes=[bpso])
            nl_, r_ = bl // dil, bl % dil
            off = nl_ * unit_g + r_
            dst = SAB[:, :, off:off + 127 * dil + 1:dil]
            src = pso[:, 0:256].rearrange("p (a q) -> p a q", a=2)
            if g == 0:
                v_copy("act", dst, src, reads=[], writes=[bpso, bS])
            else:
                v_tt(dst, dst, src, ALU.add, reads=[], writes=[bpso, bS])
            if g == 2 and bl == 15:
                oi = si

                def mk(kind, cs_):
                    def f():
                        if kind == 0:
                            actf(rd[0:64, cs_], SAB[64:128, 0, cs_], AF.Ln, reads=[bS], writes=[brd])
                        elif kind == 1:
                            actf(rd[64:128, cs_], SAB[0:64, 1, cs_], AF.Ln, reads=[bS], writes=[brd])
                        elif kind == 2:
                            actf(rd[:, cs_], rd[:, cs_], AF.Exp, reads=[], writes=[brd], scale=-1.0)
                        elif kind == 3:
                            v_tt(oast[oi][0:64, cs_], SAB[0:64, 0, cs_], rd[0:64, cs_], ALU.mult, reads=[bS, brd],
                                 writes=[boa[oi]])
                        elif kind == 4:
                            v_tt(oast[oi][64:128, cs_], SAB[64:128, 1, cs_], rd[64:128, cs_], ALU.mult, reads=[bS, brd],
                                 writes=[boa[oi]])
                    return f
                for q4 in range(4):
                    cs_ = slice(q4 * 512, (q4 + 1) * 512)
                    for kind in range(5):
                        pending.append(mk(kind, cs_))
                pending.append(lambda: dma("sp", OAT[c, :, u * 2048:(u + 1) * 2048], oast[oi], reads=[boa[oi]]))

        pending = []
        load_group(0)
        load_group(1)
        for k in range(NI + PIPE):
            if k < NI:
                gi, bl = iters[k]
                if bl == PIPE and gi >= 1 and gi + 1 < NG:
                    load_group(gi + 1)
                stage1(k)
            if k - PIPE >= 0:
                stage2(k - PIPE)
            if pending:
                pending.pop(0)()
        while pending:
            pending.pop(0)()
        T.barrier()

    def phase_B_ret(l):
        AR.off = PERSIST_END
        dmask = AR.alloc([512], F32)
        qdec = AR.alloc([512], F32)
        state_f = AR.alloc([4, 256], F32)
        state_b = AR.alloc([4, 256], BF16)
        TG = 512
        NCH = TG // 128
        qrg = [AR.alloc([4, TG], BF16) for _ in range(2)]
        krg = [AR.alloc([4, TG], BF16) for _ in range(2)]
        krk = [AR.alloc([NCH, 512], BF16) for _ in range(2)]
        vrg = [AR.alloc([NCH, 1024], BF16) for _ in range(2)]
        gtg = [AR.alloc([8, TG], BF16) for _ in range(2)]
        yst = [AR.alloc([8, TG], BF16) for _ in range(2)]
        NR = 10
        innT = [AR.alloc([128], BF16) for _ in range(NR)]
        qd = [AR.alloc([128], BF16) for _ in range(NR)]
        on = [AR.alloc([256], BF16) for _ in range(NR)]
        stats = [AR.alloc([6], F32) for _ in range(NR)]
        mv = [AR.alloc([2], F32) for _ in range(NR)]
        rs = [AR.alloc([1], F32) for _ in range(NR)]
        nmr = [AR.alloc([1], F32) for _ in range(NR)]
        bc2 = Buf()
        bsf = [Buf() for _ in range(4)]
        bsb = [Buf() for _ in range(4)]
        bin_ = [Buf(), Buf()]
        byst = [Buf(), Buf()]
        binn = [Buf() for _ in range(NR)]
        bqd = [Buf() for _ in range(NR)]
        bon = [Buf() for _ in range(NR)]
        bst = [Buf() for _ in range(NR)]
        gam = _gammas()
        dma("sp", dmask, c_dmask, writes=[bc2])
        dma("sp", qdec, c_qdec, writes=[bc2])
        T.op("dve", lambda: dve.memset(state_f, 0.0), writes=bsf)
        ps_i = PsRot([0, 1])
        ps_oo = PsRot([2, 3])
        ps_st = PsRot([4, 5])
        ps_t = PsRot([6, 7])
        NGRP = S // TG
        NIT = NGRP * NCH * 4
        ctx = {}

        def load_grp(tg):
            t0 = tg * TG
            gi = tg % 2
            dma("sp", qrg[gi], QRT.rearrange("h p t -> p h t")[:, :, t0:t0 + TG], writes=[bin_[gi]])
            dma("sp", krg[gi], KRT.rearrange("h p t -> p h t")[:, :, t0:t0 + TG], writes=[bin_[gi]])
            dma("sp", krk[gi], KRK[t0:t0 + TG, :].rearrange("(n p) f -> p n f", p=128), writes=[bin_[gi]])
            dma("sp", vrg[gi], VR[t0:t0 + TG, :].rearrange("(n p) f -> p n f", p=128), writes=[bin_[gi]])
            dma("sp", gtg[gi], GT.rearrange("c p t -> p c t")[:, 0:8, t0:t0 + TG], writes=[bin_[gi]])

        def info(k):
            tg = k // (NCH * 4)
            nl = (k // 4) % NCH
            h = k % 4
            return tg, tg % 2, nl, tg * NCH + nl, h, slice(nl * 128, (nl + 1) * 128), slice(h * 128, (h + 1) * 128), k % NR

        ps_po = PsRot([2, 3, 4])
        ps_st2 = PsRot([5, 6])
        ps_it = Rot([0, 1, 7])

        def t0_(k):
            tg, gi, nl, n, h, cs, hs, r3 = info(k)
            bk_ = ps_it.next()
            ctx[("i", k)] = bk_
            mm(psb[bk_][:, 0:128], krg[gi][:, h, cs], qrg[gi][:, h, cs], True, True, reads=[bin_[gi]], writes=[bps[bk_]])

        def t1_(k):
            tg, gi, nl, n, h, cs, hs, r3 = info(k)
            bk_ = ctx[("i", k)]
            v_tt(innT[r3], psb[bk_][:, 0:128], dmask[:, hs], ALU.mult, reads=[bc2], writes=[bps[bk_], binn[r3]])
            if n > 0:
                v_tt(qd[r3], qrg[gi][:, h, cs], qdec[:, hs], ALU.mult, reads=[bin_[gi], bc2], writes=[bqd[r3]], eng="pool")

        def t2_(k):
            tg, gi, nl, n, h, cs, hs, r3 = info(k)
            po, bpo = ps_po.next()
            pst_, bpst = ps_st2.next()
            ctx[k] = (po, bpo, pst_, bpst)
            mm(po[:, 0:256], innT[r3], vrg[gi][:, nl, h * 256:(h + 1) * 256], True, n == 0,
               reads=[binn[r3], bin_[gi]], writes=[bpo])
            if n > 0:
                mm(po[:, 0:256], qd[r3], state_b[:, h, :], False, True, reads=[bqd[r3], bsb[h]], writes=[bpo])
            mm(pst_[:, 0:256], krk[gi][:, nl, hs], vrg[gi][:, nl, h * 256:(h + 1) * 256], True, True,
               reads=[bin_[gi]], writes=[bpst])

        def t3_(k):
            tg, gi, nl, n, h, cs, hs, r3 = info(k)
            po, bpo, pst_, bpst = ctx[k]
            v_stt(state_f[:, h, :], state_f[:, h, :], float(gam[h] ** 128), pst_[:, 0:256], ALU.mult, ALU.add,
                  reads=[], writes=[bpst, bsf[h]])
            T.op("dve", (lambda a=stats[r3], b_=po[:, 0:256]: dve.bn_stats(out=a, in_=b_)), reads=[], writes=[bpo, bst[r3]])
            T.op("dve", (lambda a=mv[r3], b_=stats[r3]: dve.bn_aggr(out=a, in_=b_)), reads=[], writes=[bst[r3]])
            if n < 31:
                v_copy("act", state_b[:, h, :], state_f[:, h, :], reads=[bsf[h]], writes=[bsb[h]])
            actf(rs[r3], mv[r3][:, 1:2], AF.Sqrt, reads=[], writes=[bst[r3]], bias=GN_EPS)

        def t4_(k):
            tg, gi, nl, n, h, cs, hs, r3 = info(k)
            po, bpo, pst_, bpst = ctx.pop(k)
            ctx[("po", k)] = (po, bpo)
            v_recip(rs[r3], rs[r3], reads=[], writes=[bst[r3]])
            v_stt(nmr[r3], mv[r3][:, 0:1], -1.0, rs[r3], ALU.mult, ALU.mult, reads=[], writes=[bst[r3]])
            T.op("act", (lambda o_=on[r3], i_=po[:, 0:256], sc_=rs[r3][:, 0:1], b_=nmr[r3][:, 0:1]:
                         act.activation(out=o_, in_=i_, func=AF.Identity, scale=sc_, bias=b_)),
                 reads=[bst[r3]], writes=[bpo, bon[r3]])

        def t5_(k):
            tg, gi, nl, n, h, cs, hs, r3 = info(k)
            ctx.pop(("i", k))
            po, bpo = ctx.pop(("po", k))
            ptb = po[:, 256:512].bitcast(BF16)
            ctx[("t", k)] = (ptb, bpo)
            for ec in range(2):
                tr(ptb[:, ec * 128:(ec + 1) * 128], on[r3][:, ec * 128:(ec + 1) * 128], identb,
                   reads=[bon[r3], bconst], writes=[bpo])

        def t6_(k):
            tg, gi, nl, n, h, cs, hs, r3 = info(k)
            ptb, bpt = ctx.pop(("t", k))
            for ec in range(2):
                ch = 2 * h + ec
                gcol = l * 32 + 24 + ch
                v_stt(yst[gi][:, ch, cs], ptb[:, ec * 128:(ec + 1) * 128], gains[:, gcol:gcol + 1],
                      gtg[gi][:, ch, cs], ALU.mult, ALU.mult, reads=[bin_[gi], bconst], writes=[bpt, byst[gi]])
            if nl == NCH - 1 and h == 3:
                t0g = tg * TG
                dma("sp", YRT.rearrange("c p t -> p c t")[:, :, t0g:t0g + TG], yst[gi], reads=[byst[gi]])

        stages_ = [t0_, t1_, t2_, t3_, t4_, t5_, t6_]
        NST = len(stages_)
        load_grp(0)
        load_grp(1)
        per = NCH * 4
        for k in range(NIT + NST - 1):
            if k < NIT:
                tg = k // per
                if k % per == NST - 1 and tg >= 1 and tg + 1 < NGRP:
                    load_grp(tg + 1)
            for si_, fn_ in enumerate(stages_):
                kk = k - si_
                if 0 <= kk < NIT:
                    fn_(kk)
        dma("sp", retp[l].rearrange("h d e -> d h e"), state_f, reads=bsf)
        T.barrier()

    def phase_B_sample(l):
        AR.off = PERSIST_END
        gam = _gammas()
        stk = AR.alloc([6656], F32)
        sbias = AR.alloc([3, 8], F32)
        sel = AR.alloc([4, 4], F32)
        kvt = [AR.alloc([1024], F32) for _ in range(2)]
        qbc = [AR.alloc([512], F32) for _ in range(2)]
        prod = [AR.alloc([512], F32) for _ in range(2)]
        scb = [AR.alloc([8], F32) for _ in range(2)]
        pb2 = [AR.alloc([8], F32) for _ in range(2)]
        wt = [AR.alloc([512], F32) for _ in range(2)]
        prn = AR.alloc([512], F32)
        sn = AR.alloc([8], F32)
        pn = AR.alloc([3, 8], F32)
        wn = AR.alloc([512], F32)
        numn = AR.alloc([512], F32)
        denn = AR.alloc([8], F32)
        oS = AR.alloc([512], F32)
        S0 = AR.alloc([4, 4, 256], F32)
        Snew = AR.alloc([4, 4, 256], F32)
        qsel = AR.alloc([4, 16], F32)
        qk = AR.alloc([4], F32)
        o1 = AR.alloc([1024], F32)
        oR = AR.alloc([1024], F32)
        onS = AR.alloc([1024], F32)
        statS = AR.alloc([4, 6], F32)
        mvS = AR.alloc([4, 2], F32)
        rsS = AR.alloc([4], F32)
        ksel = [AR.alloc([512], F32) for _ in range(2)]
        bstk, bcs = Buf(), Buf()
        bkvt = [Buf(), Buf()]
        bqbc = [Buf(), Buf()]
        bprod = [Buf(), Buf()]
        bsc = [Buf(), Buf()]
        bpb = [Buf(), Buf()]
        bwt = [Buf(), Buf()]
        bmisc = Buf()
        bS0, bSn, bqsel = Buf(), Buf(), Buf()
        bksel = [Buf(), Buf()]
        AX = mybir.AxisListType.X

        def red(out, in_, reads, writes):
            return T.op("dve", (lambda: dve.tensor_reduce(out=out, in_=in_, axis=AX, op=ALU.add)), reads=reads,
                        writes=writes)

        dma("sp", stk[0:NS, :], STOK, writes=[bstk])
        dma("sp", sbias, c_sbias, writes=[bcs])
        dma("sp", sel, c_sel, writes=[bcs])
        dma("sp", S0, sret[l].rearrange("s h d e -> d s h e"), writes=[bS0])
        NUMps, bNUM = psb[0], bps[0]
        DENps, bDEN = psb[1], bps[1]
        idx = 0
        for s_ in range(NS):
            for g in range(3):
                dil, wb_ = DILS[g], WINS[g]
                i2 = idx % 2
                dma("sp", kvt[i2], cache_in[g][l, s_, 0:wb_ - dil + 1:dil, :], writes=[bkvt[i2]])
                dma("sp", qbc[i2], STOK[s_:s_ + 1, g * 512:(g + 1) * 512].broadcast_to([128, 512]), reads=[],
                    writes=[bqbc[i2]], )
                v_tt(prod[i2], kvt[i2][:, 0:512], qbc[i2], ALU.mult, reads=[bkvt[i2], bqbc[i2]], writes=[bprod[i2]])
                red(scb[i2], prod[i2].rearrange("p (h e) -> p h e", e=64), reads=[bprod[i2]], writes=[bsc[i2]])
                v_stt(scb[i2], scb[i2], 0.125, sbias[:, g, :], ALU.mult, ALU.add, reads=[bcs], writes=[bsc[i2]])
                actf(pb2[i2], scb[i2], AF.Exp, reads=[bsc[i2]], writes=[bpb[i2]])
                v_tt(wt[i2].rearrange("p (h e) -> p h e", e=64), kvt[i2][:, 512:1024].rearrange("p (h e) -> p h e", e=64),
                     pb2[i2].unsqueeze(2).broadcast_to([128, 8, 64]), ALU.mult, reads=[bkvt[i2], bpb[i2]],
                     writes=[bwt[i2]])
                mm(NUMps[0:NS, :], sel[:, s_, :], wt[i2], idx == 0, idx == 11, reads=[bcs, bwt[i2]], writes=[bNUM])
                mm(DENps[0:NS, 0:8], sel[:, s_, :], pb2[i2], idx == 0, idx == 11, reads=[bcs, bpb[i2]], writes=[bDEN])
                idx += 1
        for g in range(3):
            qn = stk[0:NS, g * 512:(g + 1) * 512]
            kn = stk[0:NS, 1536 + g * 512:1536 + (g + 1) * 512]
            vn = stk[0:NS, 3072 + g * 512:3072 + (g + 1) * 512]
            v_tt(prn[0:NS, :], qn, kn, ALU.mult, reads=[bstk], writes=[bmisc])
            red(sn[0:NS, :], prn[0:NS, :].rearrange("p (h e) -> p h e", e=64), reads=[], writes=[bmisc])
            actf(pn[0:NS, g, :], sn[0:NS, :], AF.Exp, reads=[], writes=[bmisc], scale=0.125)
            dstn = numn if g == 0 else wn
            v_tt(dstn[0:NS, :].rearrange("p (h e) -> p h e", e=64), vn.rearrange("p (h e) -> p h e", e=64),
                 pn[0:NS, g, :].unsqueeze(2).broadcast_to([NS, 8, 64]), ALU.mult, reads=[bstk], writes=[bmisc])
            if g > 0:
                v_tt(numn[0:NS, :], numn[0:NS, :], wn[0:NS, :], ALU.add, reads=[], writes=[bmisc])
        v_tt(denn[0:NS, :], pn[0:NS, 0, :], pn[0:NS, 1, :], ALU.add, reads=[], writes=[bmisc])
        v_tt(denn[0:NS, :], denn[0:NS, :], pn[0:NS, 2, :], ALU.add, reads=[], writes=[bmisc])
        v_tt(numn[0:NS, :], numn[0:NS, :], NUMps[0:NS, :], ALU.add, reads=[], writes=[bmisc, bNUM])
        v_tt(denn[0:NS, :], denn[0:NS, :], DENps[0:NS, 0:8], ALU.add, reads=[], writes=[bmisc, bDEN])
        v_recip(denn[0:NS, :], denn[0:NS, :], reads=[], writes=[bmisc])
        v_tt(oS[0:NS, :].rearrange("p (h e) -> p h e", e=64), numn[0:NS, :].rearrange("p (h e) -> p h e", e=64),
             denn[0:NS, :].unsqueeze(2).broadcast_to([NS, 8, 64]), ALU.mult, reads=[], writes=[bmisc])
        pt, bpt = psb[2], bps[2]
        for cc in range(4):
            tr(pt[:, cc * NS:(cc + 1) * NS], oS[0:NS, cc * 128:(cc + 1) * 128], ident[0:NS, 0:NS], reads=[bmisc, bconst],
               writes=[bpt])
        v_copy("dve", actS[:, 0:4, :], pt[:, 0:4 * NS].rearrange("p (c s) -> p c s", s=NS), reads=[],
               writes=[bpt] + bas[0:4])

        QR0, KR0, VR0 = 4608, 5120, 5632
        T.op("dve", (lambda: dve.memset(qsel, 0.0)), writes=[bqsel])
        v_copy("dve", qsel[:, :, 0:16:5], qrS[:, :, :], reads=[bqrS], writes=[bqsel])
        pq = [psb[3], psb[4]]
        bpq = [bps[3], bps[4]]
        for h in range(4):
            for s_ in range(NS):
                mm(pq[h // 2][0:NS, (h % 2) * 256:(h % 2) * 256 + 256], qsel[:, h, s_ * 4:(s_ + 1) * 4], S0[:, s_, h, :],
                   s_ == 0, s_ == NS - 1, reads=[bqsel, bS0], writes=[bpq[h // 2]])
        v_tt(prn[0:NS, :], stk[0:NS, QR0:QR0 + 512], stk[0:NS, KR0:KR0 + 512], ALU.mult, reads=[bstk], writes=[bmisc])
        red(qk[0:NS, :], prn[0:NS, :].rearrange("p (h d) -> p h d", d=128), reads=[], writes=[bmisc])
        v_ts(qk[0:NS, :], qk[0:NS, :], float(128.0 ** -0.5), None, ALU.mult, None, reads=[], writes=[bmisc])
        v_tt(o1[0:NS, :].rearrange("p (h e) -> p h e", e=256), stk[0:NS, VR0:VR0 + 1024].rearrange("p (h e) -> p h e", e=256),
             qk[0:NS, :].unsqueeze(2).broadcast_to([NS, 4, 256]), ALU.mult, reads=[bstk], writes=[bmisc])
        for h in range(4):
            hs = slice(h * 256, (h + 1) * 256)
            v_stt(oR[0:NS, hs], pq[h // 2][0:NS, (h % 2) * 256:(h % 2) * 256 + 256], float(gam[h]), o1[0:NS, hs],
                  ALU.mult, ALU.add, reads=[], writes=[bmisc, bpq[h // 2]])
            T.op("dve", (lambda h=h, hs=hs: dve.bn_stats(out=statS[0:NS, h, :], in_=oR[0:NS, hs])), reads=[],
                 writes=[bmisc])
            T.op("dve", (lambda h=h: dve.bn_aggr(out=mvS[0:NS, h, :], in_=statS[0:NS, h, :])), reads=[], writes=[bmisc])
        actf(rsS[0:NS, :], mvS[0:NS, :, 1], AF.Sqrt, reads=[], writes=[bmisc], bias=GN_EPS)
        v_recip(rsS[0:NS, :], rsS[0:NS, :], reads=[], writes=[bmisc])
        for h in range(4):
            hs = slice(h * 256, (h + 1) * 256)
            v_ts(onS[0:NS, hs], oR[0:NS, hs], mvS[0:NS, h, 0:1], rsS[0:NS, h:h + 1], ALU.subtract, ALU.mult, reads=[],
                 writes=[bmisc])
        pt2, bpt2 = psb[5], bps[5]
        for ch in range(8):
            tr(pt2[:, ch * NS:(ch + 1) * NS], onS[0:NS, ch * 128:(ch + 1) * 128], ident[0:NS, 0:NS], reads=[bmisc, bconst],
               writes=[bpt2])
        for ch in range(8):
            gcol = l * 32 + 24 + ch
            v_stt(actS[:, 4 + ch, :], pt2[:, ch * NS:(ch + 1) * NS], gains[:, gcol:gcol + 1], gS[:, ch, :], ALU.mult,
                  ALU.mult, reads=[bgS, bconst], writes=[bpt2, bas[4 + ch]])
        ps_r = PsRot([6, 7])
        for s_ in range(NS):
            k2 = s_ % 2
            v_ts(ksel[k2][0:NS, :], stk[0:NS, KR0:KR0 + 512], ident[0:NS, s_:s_ + 1], float(128.0 ** -0.5), ALU.mult,
                 ALU.mult, reads=[bstk, bconst], writes=[bksel[k2]])
            for h in range(4):
                pr, bpr = ps_r.next()
                mm(pr[:, 0:256], ksel[k2][0:NS, h * 128:(h + 1) * 128], stk[0:NS, VR0 + h * 256:VR0 + (h + 1) * 256], True,
                   True, reads=[bksel[k2], bstk], writes=[bpr])
                v_stt(Snew[:, s_, h, :], S0[:, s_, h, :], float(gam[h]), pr[:, 0:256], ALU.mult, ALU.add, reads=[bS0],
                      writes=[bpr, bSn])
        dma("sp", rets[l].rearrange("s h d e -> d s h e"), Snew, reads=[bSn])
        T.barrier()

    def phase_C(l):
        AR.off = PERSIST_END
        xT = AR.alloc([8, TT], F32)
        hT = AR.alloc([8, TT], BF16)
        actT = AR.alloc([NJ, TT], BF16)
        act_off = AR.last
        rstd = AR.alloc([TT], F32)
        tmps = [AR.alloc([512], BF16) for _ in range(3)]
        t12 = [AR.alloc([512], F32) for _ in range(4)]
        ystg = [AR.alloc([D], F32) for _ in range(2)]
        cin = AR.alloc([12, TT], BF16)
        gbuf = [[AR.alloc([TT], BF16) for _ in range(2)] for _ in range(2)]
        bx = [[Buf() for _ in range(2)] for _ in range(8)]
        bh = [[Buf() for _ in range(2)] for _ in range(8)]
        bact = [[Buf() for _ in range(2)] for _ in range(NJ)]
        bcin = [[Buf() for _ in range(2)] for _ in range(12)]
        bgb2 = [[[Buf(), Buf()] for _ in range(2)] for _ in range(2)]
        brs = [Buf(), Buf()]
        btmps = [Buf() for _ in range(3)]
        bt12 = [Buf() for _ in range(4)]
        bystg = [Buf(), Buf()]
        segs = make_segs(xT, hT, actT, rstd, bx, bh, bact, brs)
        ev_rot = Rot(["act", "dve"])
        XR = XRES.rearrange("c p t -> p c t")

        def load_cin(t):
            t0_ = t * TT
            dma("sp", cin[:, 0:4, :], OAT.rearrange("c p t -> p c t")[:, :, t0_:t0_ + TT],
                writes=[bcin[j][s] for j in range(0, 4) for s in range(2)])
            dma("sp", cin[:, 4:12, :], YRT.rearrange("c p t -> p c t")[:, :, t0_:t0_ + TT],
                writes=[bcin[j][s] for j in range(4, 12) for s in range(2)])

        load_cin(0)
        for t in range(NTILE):
            t0 = t * TT
            psr = PsRot(range(8))
            last = (t == NTILE - 1)
            segs_t = segs + ([sseg] if last else [])
            for hf in range(2):
                wa_s, bwa = WS1.get()
                wb_s, bwb = WS1.get()
                for mi in range(4):
                    m = hf * 4 + mi
                    gi_ = m % 2
                    dma("sp", gbuf[gi_][0], GT[8 + m, :, t0:t0 + TT], writes=bgb2[gi_][0])
                    dma("sp", gbuf[gi_][1], GT[16 + m, :, t0:t0 + TT], writes=bgb2[gi_][1])
                    if m == 1:
                        dma("sp", xT[:, :, :], XR[:, :, t0:t0 + TT], writes=[bx[c][s] for c in range(8) for s in range(2)])
                    for sg in segs_t:
                        n = sg.n
                        if sg is sseg:
                            ga_ap, gb_ap = gS[:, 8 + m, :], gS[:, 16 + m, :]
                            bga, bgb = bgS, bgS
                            i1 = 0
                            oa_ = sg.a
                            boa_ = sg.ba
                        else:
                            ssl = slice(sg.s * 512, (sg.s + 1) * 512)
                            ga_ap, gb_ap = gbuf[gi_][0][:, ssl], gbuf[gi_][1][:, ssl]
                            bga, bgb = bgb2[gi_][0][sg.s], bgb2[gi_][1][sg.s]
                            i1 = (m * 2 + sg.s) % 2
                            oa_ = (lambda k, ssl=ssl: cin[:, k, ssl])
                            boa_ = [bcin[k][sg.s] for k in range(12)]
                        pa, bpa = psr.next()
                        pb_, bpb = psr.next()
                        for k in range(4):
                            mm(pa[:, :n], wa_s[:, k, mi * 128:(mi + 1) * 128], oa_(k), k == 0, k == 3,
                               reads=[bwa, boa_[k]], writes=[bpa])
                        for k in range(8):
                            mm(pb_[:, :n], wb_s[:, k, mi * 128:(mi + 1) * 128], oa_(4 + k), k == 0, k == 7,
                               reads=[bwb, boa_[4 + k]], writes=[bpb])
                        ta, tb_ = t12[2 * i1], t12[2 * i1 + 1]
                        v_tt(ta[:, :n], pa[:, :n], ga_ap, ALU.mult, reads=[bga], writes=[bpa, bt12[2 * i1]])
                        v_tt(tb_[:, :n], pb_[:, :n], gb_ap, ALU.mult, reads=[bgb], writes=[bpb, bt12[2 * i1 + 1]])
                        v_tt(sg.h(m), ta[:, :n], tb_[:, :n], ALU.add, reads=[bt12[2 * i1], bt12[2 * i1 + 1]],
                             writes=[sg.bh[m]], eng="pool")
            if t + 1 < NTILE:
                load_cin(t + 1)
            for hf in range(2):
                wo_s, bwo = WS1.get()
                for mi in range(4):
                    m = hf * 4 + mi
                    for sg in segs_t:
                        n = sg.n
                        pm, bpm = psr.next()
                        for k in range(8):
                            mm(pm[:, :n], wo_s[:, k, mi * 128:(mi + 1) * 128], sg.h(k), k == 0, k == 7,
                               reads=[bwo, sg.bh[k]], writes=[bpm])
                        v_tt(sg.x(m), pm[:, :n], sg.x(m), ALU.add, reads=[], writes=[bpm, sg.bx[m]])
            rmsnorm(segs_t, l * 32 + 16, PsRot([6, 7]))
            ffn(segs_t, lambda i: tmps[i % 3], lambda i: btmps[i % 3])
            if l < DEPTH - 1:
                dma("sp", XR[:, :, t0:t0 + TT], xT[:, :, :], reads=[bx[c][s] for c in range(8) for s in range(2)])
            else:
                yfin = AR.view(act_off, [8, TT], F32)
                psn = PsRot([6, 7])
                for sg in segs:
                    ssl = slice(sg.s * 512, (sg.s + 1) * 512)
                    for c in range(8):
                        actf(sg.h(c), sg.x(c), AF.Square, reads=[sg.bx[c]], writes=[sg.bh[c]])
                    pst, pbf = psn.next()
                    for c in range(8):
                        mm(pst[:, :], ones_b, sg.h(c), c == 0, c == 7, reads=[sg.bh[c], bconst], writes=[pbf])
                    actf(sg.rstd, pst[:, :], AF.Sqrt, reads=[], writes=[pbf, sg.brs], scale=1.0 / 1024.0, bias=NORM_EPS)
                    v_recip(sg.rstd, sg.rstd, reads=[], writes=[sg.brs])
                    for c in range(8):
                        v_stt(yfin[:, c, ssl], sg.x(c), gains[:, 64 + c:65 + c], sg.rstd, ALU.mult, ALU.mult,
                              reads=[sg.bx[c], sg.brs, bconst],
                              writes=[bact[2 * c][sg.s], bact[2 * c + 1][sg.s], bact[2 * c][1 - sg.s], bact[2 * c + 1][1 - sg.s]])
                pst_r = PsRot(range(6))
                for tb in range(8):
                    yi = tb % 2
                    s_ = tb // 4
                    for cg in range(2):
                        pst, pbf = pst_r.next()
                        for ci in range(4):
                            c = cg * 4 + ci
                            tr(pst[:, ci * 128:(ci + 1) * 128], yfin[:, c, tb * 128:(tb + 1) * 128], ident,
                               reads=[bact[2 * c][s_], bact[2 * c + 1][s_], bconst], writes=[pbf])
                        v_copy(ev_rot.next(), ystg[yi][:, cg * 512:(cg + 1) * 512], pst[:, :], reads=[],
                               writes=[pbf, bystg[yi]])
                    dma("sp", yp[t0 + tb * 128:t0 + (tb + 1) * 128, :], ystg[yi], reads=[bystg[yi]])
                if last:
                    yfS = t12[0].rearrange("p (c s) -> p c s", s=64)
                    for c in range(8):
                        actf(hsT[:, c, :], xsT[:, c, :], AF.Square, reads=[bxs[c]], writes=[bhs[c]])
                    pst, pbf = psn.next()
                    for c in range(8):
                        mm(pst[:, 0:NS], ones_b, hsT[:, c, :], c == 0, c == 7, reads=[bhs[c], bconst], writes=[pbf])
                    actf(rstdS, pst[:, 0:NS], AF.Sqrt, reads=[], writes=[pbf, brss], scale=1.0 / 1024.0, bias=NORM_EPS)
                    v_recip(rstdS, rstdS, reads=[], writes=[brss])
                    for c in range(8):
                        v_stt(yfS[:, c, 0:NS], xsT[:, c, :], gains[:, 64 + c:65 + c], rstdS, ALU.mult, ALU.mult,
                              reads=[bxs[c], brss, bconst], writes=[bt12[0]])
                    for cg in range(2):
                        pst, pbf = pst_r.next()
                        for ci in range(4):
                            c = cg * 4 + ci
                            tr(pst[0:NS, ci * 128:(ci + 1) * 128], yfS[:, c, 0:NS], ident, reads=[bt12[0], bconst],
                               writes=[pbf])
                        v_copy(ev_rot.next(), ystg[0][0:NS, cg * 512:(cg + 1) * 512], pst[0:NS, :], reads=[],
                               writes=[pbf, bystg[0]])
                    dma("sp", ys, ystg[0][0:NS, :], reads=[bystg[0]])
        T.barrier()

    stages = []
    for l in range(DEPTH):
        stages += [("A", l), ("BS", l), ("B1", l), ("B2", l), ("C", l)]
    if only is None:
        load_sample_x()
        T.barrier()
    for (ph, l) in stages:
        if only is not None and ph != only:
            continue
        if ph == "A":
            phase_A(l)
        elif ph == "BS":
            phase_B_sample(l)
        elif ph == "B1":
            phase_B_attn(l)
        elif ph == "B2":
            phase_B_ret(l)
        else:
            phase_C(l)
        if stop_after == f"{ph}{l}":
            break
    T.emit()
    es.close()
    return nc, T


def _gains_table(inp):
    g = np.zeros((128, 72), np.float32)
    for l in range(DEPTH):
        for i, nm in enumerate(("norm_ffn1", "norm_mix", "norm_ffn2", "ret_gn")):
            g[:, l * 32 + i * 8:l * 32 + i * 8 + 8] = np.asarray(inp[nm][l], np.float32).reshape(8, 128).T
    g[:, 64:72] = np.asarray(inp["norm_final"], np.float32).reshape(8, 128).T
    return g


def make_in_maps(inp):
    consts = _const_tables()
    gains = _gains_table(inp)
    shared = dict(consts)
    shared["gains"] = gains
    for nm in ("ffn1_wg", "ffn1_wu", "ffn1_wd", "ffn2_wg", "ffn2_wu", "ffn2_wd", "w_in", "w_br_a", "w_br_b", "w_out"):
        shared[nm] = np.ascontiguousarray(inp[nm], dtype=np.float32)
    maps = []
    for c in range(NCORES):
        m = dict(shared)
        m["xp"] = np.ascontiguousarray(inp["x_prompt"][c])
        sl = slice(c * NS, (c + 1) * NS)
        m["xs"] = np.ascontiguousarray(inp["x_sample"][sl, 0, :])
        for w, nm in zip(WINS, ("cache_kv_w128", "cache_kv_w512", "cache_kv_w2048")):
            m[f"c{w}"] = np.ascontiguousarray(inp[nm][:, sl]).reshape(DEPTH, NS, w, 1024)
        m["sret"] = np.ascontiguousarray(inp["state_ret"][:, sl])
        maps.append(m)
    return maps


_PROGRAM_CACHE = {}


def kernel(**inputs):
    inp = {k: np.asarray(v) for k, v in inputs.items()}
    if "prog" not in _PROGRAM_CACHE:
        _PROGRAM_CACHE["prog"] = build_program(debug=False)
    nc, _ = _PROGRAM_CACHE["prog"]
    maps = make_in_maps(inp)
    res = run_bass_kernel_spmd(nc, maps, core_ids=list(range(NCORES)))
    R = res.results
    f32 = np.float32
    y_prompt = np.stack([np.asarray(R[c]["yp"], f32) for c in range(NCORES)], 0)
    y_sample = np.concatenate([np.asarray(R[c]["ys"], f32) for c in range(NCORES)], 0).reshape(NCORES * NS, 1, D)
    outs = [y_prompt, y_sample]
    for w in WINS:
        kp = np.stack([np.asarray(R[c][f"kv{w}p"], f32) for c in range(NCORES)], 1)
        ks = np.concatenate([np.asarray(R[c][f"kv{w}s"], f32) for c in range(NCORES)], 1)
        outs.append(kp.reshape(DEPTH, NCORES, w, 2, 8, 64))
        outs.append(ks.reshape(DEPTH, NCORES * NS, w, 2, 8, 64))
    rp = np.stack([np.asarray(R[c]["retp"], f32) for c in range(NCORES)], 1)
    rs_ = np.concatenate([np.asarray(R[c]["rets"], f32) for c in range(NCORES)], 1)
    outs.append(rp)
    outs.append(rs_)
    return tuple(outs)
```

```python
import math
from contextlib import ExitStack
from functools import partial

import numpy as np
import ml_dtypes

import concourse.bass as bass
import concourse.mybir as mybir
from concourse.bass_utils import run_bass_kernel_spmd

F32 = mybir.dt.float32
BF16 = mybir.dt.bfloat16
AF = mybir.ActivationFunctionType
ALU = mybir.AluOpType

S = 4096
D = 1024
DFF = 2816
NJ = DFF // 128
NIN = 9728
DEPTH = 2
NS = 4
TT = 1024
NTILE = S // TT
NCORES = 8
NORM_EPS = 1e-6
GN_EPS = 1e-6
DILS = (1, 4, 16)
WINS = (128, 512, 2048)
NHEAD_RET = 4

ENG_NAMES = ("pe", "act", "dve", "pool", "sp")
SAME_ENGINE_SYNC = True
N_DMA_SEMS = 24


class Buf:
    __slots__ = ("name", "w", "r")

    def __init__(self, name=""):
        self.name = name
        self.w = None
        self.r = {}


class _Op:
    __slots__ = ("eng", "fn", "deps", "dma", "signal", "snap")

    def __init__(self, eng, fn):
        self.eng = eng
        self.fn = fn
        self.deps = []
        self.dma = None
        self.signal = False
        self.snap = None


class Tracer:
    def __init__(self, nc):
        self.nc = nc
        self.h = {"pe": nc.tensor, "act": nc.scalar, "dve": nc.vector, "pool": nc.gpsimd, "sp": nc.sync}
        self.ops = []
        self.eng_ops = {e: [] for e in ENG_NAMES}
        self.known = {e: {e2: -1 for e2 in ENG_NAMES} for e in ENG_NAMES}
        self.known_dma = {e: set() for e in ENG_NAMES}
        self.dmas = []
        self.dma_q_count = {e: 0 for e in ENG_NAMES}
        self.bar_dma_start = 0

    def _add_dep(self, op, tok):
        if tok is None:
            return
        eng = op.eng
        if tok[0] == "e":
            _, e2, n = tok
            if e2 == eng and (eng == "pe" or eng == "sp" or not SAME_ENGINE_SYNC):
                return
            if n <= self.known[eng][e2]:
                return
            op.deps.append(tok)
            src = self.eng_ops[e2][n]
            src.signal = True
            kn = self.known[eng]
            kn[e2] = n
            for e3, v in src.snap.items():
                if v > kn[e3]:
                    kn[e3] = v
        else:
            did = tok[1]
            if did in self.known_dma[eng]:
                return
            op.deps.append(tok)
            self.known_dma[eng].add(did)
            snap = self.dmas[did][2]
            kn = self.known[eng]
            for e3, v in snap.items():
                if v > kn[e3]:
                    kn[e3] = v

    def op(self, eng, fn, reads=(), writes=(), dma=False, extra=()):
        op = _Op(eng, fn)
        idx = len(self.eng_ops[eng])
        if dma:
            did = len(self.dmas)
            op.dma = did
            tok = ("d", did)
        else:
            tok = ("e", eng, idx)
        for t in extra:
            self._add_dep(op, t)
        for b in reads:
            self._add_dep(op, b.w)
        for b in writes:
            self._add_dep(op, b.w)
            for rt in b.r.values():
                self._add_dep(op, rt)
        for b in reads:
            if dma:
                b.r[tok] = tok
            else:
                b.r[eng] = tok
        for b in writes:
            b.w = tok
            b.r = {}
        op.snap = dict(self.known[eng])
        if dma:
            self.dmas.append((eng, self.dma_q_count[eng], dict(self.known[eng])))
            self.dma_q_count[eng] += 1
        self.eng_ops[eng].append(op)
        self.ops.append(op)
        return tok

    def barrier(self):
        latest = {}
        covered = []
        for i in range(self.bar_dma_start, len(self.dmas)):
            q, k, _ = self.dmas[i]
            if q == "pool":
                continue
            covered.append(i)
            key = (q, k % N_DMA_SEMS)
            if key not in latest or k > self.dmas[latest[key]][1]:
                latest[key] = i
        extra = [("d", i) for i in sorted(latest.values())]
        self.bar_dma_start = len(self.dmas)
        for e in ENG_NAMES:
            if e != "sp" and self.eng_ops[e]:
                extra.append(("e", e, len(self.eng_ops[e]) - 1))
        b = Buf("barrier")
        sp = self.h["sp"]
        self.op("sp", lambda: sp.nop(), writes=[b], extra=extra)
        self.known_dma["sp"].update(covered)
        for e in ENG_NAMES:
            if e != "sp":
                h = self.h[e]
                self.op(e, (lambda h=h: h.nop()), reads=[b])

    def emit(self):
        nc = self.nc
        with ExitStack() as es:
            esem = {e: es.enter_context(nc.semaphore("s_" + e)) for e in ENG_NAMES}
            dsem = {}
            for q in ENG_NAMES:
                if self.dma_q_count[q]:
                    dsem[q] = [es.enter_context(nc.semaphore(f"d_{q}{i}")) for i in range(N_DMA_SEMS)]
            cum = {}
            for e in ENG_NAMES:
                c = 0
                arr = []
                for o in self.eng_ops[e]:
                    if o.signal and o.dma is None:
                        c += 1
                    arr.append(c)
                cum[e] = arr
            nwait = 0
            for o in self.ops:
                h = self.h[o.eng]
                for tok in o.deps:
                    if tok[0] == "e":
                        h.wait_ge(esem[tok[1]], cum[tok[1]][tok[2]])
                    else:
                        q, k, _ = self.dmas[tok[1]]
                        h.wait_ge(dsem[q][k % N_DMA_SEMS], 16 * (k // N_DMA_SEMS + 1))
                    nwait += 1
                if o.dma is not None:
                    q, k, _ = self.dmas[o.dma]
                    sem = dsem[q][k % N_DMA_SEMS]
                    if k >= N_DMA_SEMS:
                        h.wait_ge(sem, 16 * (k // N_DMA_SEMS))
                    o.fn().then_inc(sem, 16)
                else:
                    ins = o.fn()
                    if o.signal:
                        ins.then_inc(esem[o.eng], 1)
            h = self.h["sp"]
            for q, sems in dsem.items():
                n = self.dma_q_count[q]
                for i, sem in enumerate(sems):
                    cnt = (n - i + N_DMA_SEMS - 1) // N_DMA_SEMS if n > i else 0
                    if cnt > 0:
                        h.wait_ge(sem, 16 * cnt)
            self.stats = dict(n_ops=len(self.ops), n_wait=nwait,
                              per_eng={e: len(v) for e, v in self.eng_ops.items()}, n_dma=len(self.dmas))


def _gammas():
    return [1.0 - 2.0 ** (-5.0 - h) for h in range(NHEAD_RET)]


def _const_tables():
    bf = ml_dtypes.bfloat16
    eb = np.zeros((3, 4, 128, 512), np.float64)
    kk = np.arange(128)[:, None].astype(np.float64)
    qi = np.arange(128)[None, :].astype(np.float64)
    for g in range(3):
        for c in range(4):
            for hp in range(2):
                hglob = 8 * g + 2 * c + hp
                slope = 2.0 ** (-8.0 * (hglob + 1) / 24.0)
                cc = slope * DILS[g]
                rel_prev = qi + 128.0 - kk
                prev = np.where(rel_prev <= 128.0, np.exp(-cc * rel_prev), 0.0)
                rel_cur = qi - kk
                cur = np.where(rel_cur >= 0.0, np.exp(-cc * np.maximum(rel_cur, 0.0)), 0.0)
                eb[g, c, :, hp * 128:hp * 128 + 128] = prev
                eb[g, c, :, 256 + hp * 128:256 + hp * 128 + 128] = cur
    gam = _gammas()
    sc = 128.0 ** -0.5
    dmaskT = np.zeros((128, 4 * 128), np.float64)
    kdec = np.zeros((128, 4 * 128), np.float64)
    qdec = np.zeros((128, 4 * 128), np.float64)
    jj = np.arange(128)[:, None].astype(np.float64)
    ii = np.arange(128)[None, :].astype(np.float64)
    for h in range(4):
        dm = np.where(ii >= jj, gam[h] ** np.maximum(ii - jj, 0.0), 0.0) * sc
        dmaskT[:, h * 128:(h + 1) * 128] = dm
        kdec[:, h * 128:(h + 1) * 128] = (gam[h] ** (127.0 - jj)) * sc
        qdec[:, h * 128:(h + 1) * 128] = gam[h] ** (ii + 1.0)
    ident = np.eye(128)
    sbias = np.zeros((128, 24), np.float64)
    kidx = np.arange(128).astype(np.float64)
    for g in range(3):
        for h in range(8):
            slope = 2.0 ** (-8.0 * (8 * g + h + 1) / 24.0)
            sbias[:, g * 8 + h] = -slope * DILS[g] * (128.0 - kidx)
    sel = np.zeros((128, 4, 4), np.float64)
    for s_ in range(4):
        sel[:, s_, s_] = 1.0
    return dict(
        c_sbias=sbias.astype(np.float32),
        c_sel=sel.reshape(128, 16).astype(np.float32),
        c_eb=eb.reshape(12, 128, 512).astype(bf),
        c_dmask=dmaskT.astype(np.float32),
        c_kdec=kdec.astype(np.float32),
        c_qdec=qdec.astype(np.float32),
        c_ident=ident.astype(np.float32),
        c_identb=ident.astype(bf),
    )


class Arena:
    def __init__(self, t, nbytes):
        self.t = t
        self.nbytes = nbytes
        self.off = 0

    def alloc(self, free_shape, dtype):
        esz = 4 if dtype == F32 else 2
        nel = int(np.prod(free_shape))
        nb = nel * esz
        start = self.off
        self.off = start + ((nb + 63) // 64) * 64
        assert self.off <= self.nbytes, f"SBUF arena overflow {self.off} > {self.nbytes}"
        self.last = start
        return self.view(start, free_shape, dtype)

    def view(self, start, free_shape, dtype):
        esz = 4 if dtype == F32 else 2
        nb = int(np.prod(free_shape)) * esz
        v = self.t[:, start // 2:(start + nb) // 2]
        if dtype == F32:
            v = v.bitcast(F32)
        if len(free_shape) == 2:
            v = v.rearrange("p (a b) -> p a b", b=free_shape[1])
        elif len(free_shape) == 3:
            v = v.rearrange("p (a b c) -> p a b c", b=free_shape[1], c=free_shape[2])
        return v


class Rot:
    def __init__(self, items):
        self.items = list(items)
        self.i = 0

    def next(self):
        it = self.items[self.i % len(self.items)]
        self.i += 1
        return it


def build_program(debug=False, stop_after=None, only=None, cut=99):
    nc = bass.Bass("TRN2", target_bir_lowering=False)
    T = Tracer(nc)
    es = ExitStack()

    def din(name, shape, dt=F32):
        return nc.dram_tensor(name, list(shape), dt, kind="ExternalInput").ap()

    def dout(name, shape, dt=F32):
        return nc.dram_tensor(name, list(shape), dt, kind="ExternalOutput").ap()

    def dscr(name, shape, dt):
        kind = "ExternalOutput" if debug else "Internal"
        if only is not None and name in ("QK", "VX", "QRT", "KRT", "KRK", "VR", "GT"):
            kind = "ExternalInput"
        return nc.dram_tensor(name, list(shape), dt, kind=kind).ap()

    xp = din("xp", [S, D])
    xs = din("xs", [NS, D])
    cache_in = [din(f"c{w}", [DEPTH, NS, w, 1024]) for w in WINS]
    sret = din("sret", [DEPTH, NS, 4, 128, 256])
    gains_d = din("gains", [128, 72])
    w_g = [din("ffn1_wg", [DEPTH, D, DFF]), din("ffn2_wg", [DEPTH, D, DFF])]
    w_u = [din("ffn1_wu", [DEPTH, D, DFF]), din("ffn2_wu", [DEPTH, D, DFF])]
    w_d = [din("ffn1_wd", [DEPTH, DFF, D]), din("ffn2_wd", [DEPTH, DFF, D])]
    w_in = din("w_in", [DEPTH, D, NIN])
    w_a = din("w_br_a", [DEPTH, 512, D])
    w_b = din("w_br_b", [DEPTH, D, D])
    w_o = din("w_out", [DEPTH, D, D])
    c_eb = din("c_eb", [12, 128, 512], BF16)
    c_dmask = din("c_dmask", [128, 512])
    c_kdec = din("c_kdec", [128, 512])
    c_qdec = din("c_qdec", [128, 512])
    c_ident = din("c_ident", [128, 128])
    c_identb = din("c_identb", [128, 128], BF16)
    c_sbias = din("c_sbias", [128, 24])
    c_sel = din("c_sel", [128, 16])

    yp = dout("yp", [S, D])
    ys = dout("ys", [NS, D])
    kvp = [dout(f"kv{w}p", [DEPTH, w, 1024]) for w in WINS]
    kvs = [dout(f"kv{w}s", [DEPTH, NS, w, 1024]) for w in WINS]
    retp = dout("retp", [DEPTH, 4, 128, 256])
    rets = dout("rets", [DEPTH, NS, 4, 128, 256])

    XRES = dscr("XRES", [8, 128, S], F32)
    QK = dscr("QK", [2, 3, 4, 128, S], BF16)
    VX = dscr("VX", [3, 4, S, 256], BF16)
    QRT = dscr("QRT", [4, 128, S], BF16)
    KRT = dscr("KRT", [4, 128, S], BF16)
    KRK = dscr("KRK", [S, 512], BF16)
    VR = dscr("VR", [S, 1024], BF16)
    GT = dscr("GT", [24, 128, S], BF16)
    OAT = dscr("OAT", [4, 128, S], BF16)
    YRT = dscr("YRT", [8, 128, S], BF16)
    STOK = dscr("STOK", [NS, 6656], F32)

    ARENA_BYTES = 206 * 1024
    arena_t = es.enter_context(nc.sbuf_tensor("arena", [128, ARENA_BYTES // 2], BF16))
    AR = Arena(arena_t, ARENA_BYTES)
    psw = [es.enter_context(nc.psum_tensor(f"psw{i}", [128, 1024], F32)) for i in range(4)]
    psb = []
    for i in range(4):
        psb.append(psw[i][:, 0:512])
        psb.append(psw[i][:, 512:1024])
    bps = [Buf(f"ps{i}") for i in range(8)]

    ws = [AR.alloc([8, 512], BF16) for _ in range(4)]
    bws = [Buf(f"ws{i}") for i in range(4)]
    wds = [AR.alloc([NJ, 256], BF16) for _ in range(2)]
    bwds = [Buf(f"wd{i}") for i in range(2)]
    ident = AR.alloc([128], F32)
    identb = AR.alloc([128], BF16)
    ones_b = AR.alloc([128], BF16)
    gains = AR.alloc([72], F32)
    gains32 = AR.alloc([72], F32)
    kdec = AR.alloc([512], F32)
    bconst = Buf("const")
    xsT = AR.alloc([8, NS], F32)
    hsT = AR.alloc([8, NS], BF16)
    actS = AR.alloc([NJ, NS], BF16)
    rstdS = AR.alloc([NS], F32)
    qrS = AR.alloc([4, NS], F32)
    gS = AR.alloc([24, NS], BF16)
    bxs = [Buf() for _ in range(8)]
    bhs = [Buf() for _ in range(8)]
    bas = [Buf() for _ in range(NJ)]
    brss = Buf()
    bqrS = Buf()
    bgS = Buf()
    PERSIST_END = AR.off

    sp, act, dve, pe, pool = nc.sync, nc.scalar, nc.vector, nc.tensor, nc.gpsimd

    def dma(q, out, in_, reads=(), writes=()):
        h = T.h[q]
        return T.op(q, (lambda: h.dma_start(out=out, in_=in_)), reads=reads, writes=writes, dma=True)

    def mm(out, lhsT, rhs, start, stop, reads, writes):
        return T.op("pe", (lambda: pe.matmul(out, lhsT, rhs, start=start, stop=stop)), reads=reads, writes=writes)

    def tr(out, in_, idn, reads, writes):
        return T.op("pe", (lambda: pe.transpose(out, in_, idn)), reads=reads, writes=writes)

    def actf(out, in_, func, reads, writes, scale=1.0, bias=None):
        if bias is None:
            return T.op("act", (lambda: act.activation(out=out, in_=in_, func=func, scale=scale)),
                        reads=reads, writes=writes)
        return T.op("act", (lambda: act.activation(out=out, in_=in_, func=func, scale=scale, bias=bias)),
                    reads=reads, writes=writes)

    def v_recip(out, in_, reads, writes):
        return T.op("dve", (lambda: dve.reciprocal(out=out, in_=in_)), reads=reads, writes=writes)

    def v_copy(eng, out, in_, reads, writes):
        if eng == "act":
            return T.op("act", (lambda: act.copy(out=out, in_=in_)), reads=reads, writes=writes)
        h = T.h[eng]
        return T.op(eng, (lambda: h.tensor_copy(out=out, in_=in_)), reads=reads, writes=writes)

    def v_tt(out, in0, in1, op, reads, writes, eng="dve"):
        h = T.h[eng]
        return T.op(eng, (lambda: h.tensor_tensor(out=out, in0=in0, in1=in1, op=op)), reads=reads, writes=writes)

    def v_ts(out, in0, s1, s2, op0, op1, reads, writes, eng="dve"):
        h = T.h[eng]
        if s2 is None:
            return T.op(eng, (lambda: h.tensor_scalar(out=out, in0=in0, scalar1=s1, scalar2=None, op0=op0)),
                        reads=reads, writes=writes)
        return T.op(eng, (lambda: h.tensor_scalar(out=out, in0=in0, scalar1=s1, scalar2=s2, op0=op0, op1=op1)),
                    reads=reads, writes=writes)

    def v_stt(out, in0, scalar, in1, op0, op1, reads, writes, eng="dve"):
        h = T.h[eng]
        return T.op(eng, (lambda: h.scalar_tensor_tensor(out=out, in0=in0, scalar=scalar, in1=in1, op0=op0, op1=op1)),
                    reads=reads, writes=writes)

    dma("sp", ident, c_ident, writes=[bconst])
    dma("sp", identb, c_identb, writes=[bconst])
    dma("sp", gains, gains_d, writes=[bconst])
    dma("sp", kdec, c_kdec, writes=[bconst])
    T.op("dve", lambda: dve.memset(ones_b, 1.0), writes=[bconst])
    v_ts(gains32, gains, 1.0, None, ALU.mult, None, reads=[bconst], writes=[bconst])

    def load_sample_x():
        AR.off = PERSIST_END
        xs_sb = AR.alloc([D], F32)
        bxs_sb = Buf()
        dma("sp", xs_sb[0:NS, :], xs, writes=[bxs_sb])
        for c in range(8):
            tr(psb[0][:, c * NS:(c + 1) * NS], xs_sb[0:NS, c * 128:(c + 1) * 128], ident[0:NS, 0:NS],
               reads=[bxs_sb, bconst], writes=[bps[0]])
        v_copy("dve", xsT[:, :, :], psb[0][:, 0:8 * NS].rearrange("p (c s) -> p c s", s=NS), reads=[],
               writes=[bps[0]] + bxs)

    cache_pieces = []
    for l_ in range(DEPTH):
        for s_ in range(NS):
            for g_, wb_ in enumerate(WINS):
                npiece = 4 if wb_ == 2048 else 1
                rows = wb_ - 1
                step = (rows + npiece - 1) // npiece
                for r0_ in range(0, rows, step):
                    r1_ = min(rows, r0_ + step)
                    cache_pieces.append((kvs[g_][l_, s_, r0_:r1_, :].rearrange("r f -> (r f)"),
                                         cache_in[g_][l_, s_, r0_ + 1:r1_ + 1, :].rearrange("r f -> (r f)")))

    def copy_cache_piece():
        if cache_pieces and only is None:
            dst, src = cache_pieces.pop(0)
            dma("act", dst, src)

    CG = [(i * 512, 512) for i in range(5)] + [(2560, 256)]

    class WStream:
        def __init__(self, slots, bufs, prefetch):
            self.slots, self.bufs, self.prefetch = slots, bufs, prefetch
            self.items = []
            self.issued = 0
            self.used = 0

        def add(self, src_ap, kc, ncols):
            self.items.append((src_ap, kc, ncols))

        def _issue(self, i):
            src, kc, ncols = self.items[i]
            s = i % len(self.slots)
            dma("pool", self.slots[s][:, 0:kc, 0:ncols], src, writes=[self.bufs[s]])

        def get(self):
            i = self.used
            self.used += 1
            lim = min(len(self.items), i + self.prefetch + 1)
            while self.issued < lim:
                self._issue(self.issued)
                self.issued += 1
            s = i % len(self.slots)
            return self.slots[s], self.bufs[s]

    WS1 = WStream(ws, bws, 2)
    WS2 = WStream(wds, bwds, 1)

    def plan_ffn(which, l, ntiles):
        for _ in range(ntiles):
            wgv = w_g[which][l].rearrange("(k p) n -> p k n", p=128)
            wuv = w_u[which][l].rearrange("(k p) n -> p k n", p=128)
            for (c0, n) in CG:
                WS1.add(wgv[:, :, c0:c0 + n], 8, n)
                WS1.add(wuv[:, :, c0:c0 + n], 8, n)
            wdv = w_d[which][l].rearrange("(j p) n -> p j n", p=128)
            for m2 in range(4):
                WS2.add(wdv[:, :, m2 * 256:(m2 + 1) * 256], NJ, 256)

    def plan_phaseA_tile(l):
        plan_ffn(0, l, 1)
        wv = w_in[l].rearrange("(k p) n -> p k n", p=128)
        for G in range(19):
            WS1.add(wv[:, :, G * 512:(G + 1) * 512], 8, 512)

    def plan_phaseC_tile(l):
        wav = w_a[l].rearrange("(k p) n -> p k n", p=128)
        wbv = w_b[l].rearrange("(k p) n -> p k n", p=128)
        wov = w_o[l].rearrange("(k p) n -> p k n", p=128)
        for hf in range(2):
            WS1.add(wav[:, :, hf * 512:(hf + 1) * 512], 4, 512)
            WS1.add(wbv[:, :, hf * 512:(hf + 1) * 512], 8, 512)
        for hf in range(2):
            WS1.add(wov[:, :, hf * 512:(hf + 1) * 512], 8, 512)
        plan_ffn(1, l, 1)

    for l in range(DEPTH):
        for t in range(NTILE):
            plan_phaseA_tile(l)
        for t in range(NTILE):
            plan_phaseC_tile(l)

    class PsRot:
        def __init__(self, idxs):
            self.idxs = list(idxs)
            self.i = 0

        def next(self):
            k = self.idxs[self.i % len(self.idxs)]
            self.i += 1
            return psb[k], bps[k]

    class Seg:
        pass

    def make_segs(xT, hT, actT, rstd, bx, bh, bact, brs):
        segs = []
        for s in range(2):
            sg = Seg()
            sg.n = 512
            sg.s = s
            sl = slice(s * 512, (s + 1) * 512)
            sg.x = (lambda c, sl=sl: xT[:, c, sl])
            sg.h = (lambda c, sl=sl: hT[:, c, sl])
            sg.a = (lambda j, sl=sl: actT[:, j, sl])
            sg.rstd = rstd[:, sl]
            sg.bx = [bx[c][s] for c in range(8)]
            sg.bh = [bh[c][s] for c in range(8)]
            sg.ba = [bact[j][s] for j in range(NJ)]
            sg.brs = brs[s]
            segs.append(sg)
        return segs

    sseg = Seg()
    sseg.n = NS
    sseg.s = None
    sseg.x = (lambda c: xsT[:, c, :])
    sseg.h = (lambda c: hsT[:, c, :])
    sseg.a = (lambda j: actS[:, j, :])
    sseg.rstd = rstdS
    sseg.bx, sseg.bh, sseg.ba, sseg.brs = bxs, bhs, bas, brss

    def rmsnorm(segs, gcol, psr):
        for sg in segs:
            n = sg.n
            for c in range(8):
                actf(sg.a(c), sg.x(c), AF.Square, reads=[sg.bx[c]], writes=[sg.ba[c]])
            pst, pbf = psr.next()
            for c in range(8):
                mm(pst[:, :n], ones_b, sg.a(c), c == 0, c == 7, reads=[sg.ba[c], bconst], writes=[pbf])
            actf(sg.rstd, pst[:, :n], AF.Sqrt, reads=[], writes=[pbf, sg.brs], scale=1.0 / 1024.0, bias=NORM_EPS)
            v_recip(sg.rstd, sg.rstd, reads=[], writes=[sg.brs])
            for c in range(8):
                v_stt(sg.h(c), sg.x(c), gains32[:, gcol + c:gcol + c + 1], sg.rstd, ALU.mult, ALU.mult,
                      reads=[sg.bx[c], sg.brs, bconst], writes=[sg.bh[c]])

    def ffn(segs, tmp_of, btmp_of):
        psr = PsRot(range(6))
        ti = 0
        for (c0, ncol) in CG:
            copy_cache_piece()
            wg_s, bwg = WS1.get()
            wu_s, bwu = WS1.get()
            for jj in range(ncol // 128):
                j = c0 // 128 + jj
                for sg in segs:
                    n = sg.n
                    pg, bpg = psr.next()
                    pu, bpu = psr.next()
                    for k in range(8):
                        mm(pg[:, :n], wg_s[:, k, jj * 128:(jj + 1) * 128], sg.h(k), k == 0, k == 7,
                           reads=[bwg, sg.bh[k]], writes=[bpg])
                        mm(pu[:, :n], wu_s[:, k, jj * 128:(jj + 1) * 128], sg.h(k), k == 0, k == 7,
                           reads=[bwu, sg.bh[k]], writes=[bpu])
                    tmp, btmp = tmp_of(ti), btmp_of(ti)
                    ti += 1
                    actf(tmp[:, :n], pg[:, :n], AF.Silu, reads=[], writes=[bpg, btmp])
                    v_tt(sg.a(j), tmp[:, :n], pu[:, :n], ALU.mult, reads=[btmp], writes=[bpu, sg.ba[j]])
        for m2 in range(4):
            wd_s, bwd = WS2.get()
            for mm_ in range(2):
                m = m2 * 2 + mm_
                for sg in segs:
                    n = sg.n
                    py, bpy = psr.next()
                    for j in range(NJ):
                        mm(py[:, :n], wd_s[:, j, mm_ * 128:(mm_ + 1) * 128], sg.a(j), j == 0, j == NJ - 1,
                           reads=[bwd, sg.ba[j]], writes=[bpy])
                    v_stt(sg.x(m), py[:, :n], 0.5, sg.x(m), ALU.mult, ALU.add, reads=[], writes=[bpy, sg.bx[m]])

    def phase_A(l):
        AR.off = PERSIST_END
        xT = AR.alloc([8, TT], F32)
        hT = AR.alloc([8, TT], BF16)
        actT = AR.alloc([NJ, TT], BF16)
        act_off = AR.last

        def act_f32(j):
            return AR.view(act_off + j * 2048, [512], F32)
        rstd = AR.alloc([TT], F32)
        tmps = [AR.alloc([512], BF16) for _ in range(3)]
        ust = AR.alloc([8, 2048], BF16)
        vxst = [AR.alloc([8, 128], BF16) for _ in range(4)]
        stokst = [AR.alloc([512], F32) for _ in range(2)]
        bstok = [Buf(), Buf()]
        bx = [[Buf() for _ in range(2)] for _ in range(8)]
        bh = [[Buf() for _ in range(2)] for _ in range(8)]
        bact = [[Buf() for _ in range(2)] for _ in range(NJ)]
        brs = [Buf(), Buf()]
        btmps = [Buf() for _ in range(3)]
        bust = [Buf() for _ in range(8)]
        bvxst = [Buf() for _ in range(4)]
        segs = make_segs(xT, hT, actT, rstd, bx, bh, bact, brs)
        for i in range(4):
            T.op("dve", (lambda i=i: dve.memset(vxst[i], 1.0)), writes=[bvxst[i]])

        def chunk_bufs(j):
            return [bact[j][0], bact[j][1]]

        fm_rot = Rot([8, 9, 10, 11, 12, 13])
        vr_rot = Rot([14, 15, 16])
        krk_rot = Rot([(17, 0), (17, 1)])
        f32_rot = Rot([18, 19, 20, 21])
        ev_rot = Rot(["act", "dve"])

        for t in range(NTILE):
            t0 = t * TT
            if l == 0:
                xin = AR.view(act_off, [4, D], F32)
                psr = PsRot(range(8))
                for half in range(2):
                    for tb in range(4):
                        r0 = t0 + half * 512 + tb * 128
                        dma("sp", xin[:, tb, :], xp[r0:r0 + 128, :], writes=chunk_bufs(2 * tb) + chunk_bufs(2 * tb + 1))
                    for c in range(8):
                        pst, pbf = psr.next()
                        for tb in range(4):
                            tr(pst[:, tb * 128:(tb + 1) * 128], xin[:, tb, c * 128:(c + 1) * 128], ident,
                               reads=chunk_bufs(2 * tb) + chunk_bufs(2 * tb + 1) + [bconst], writes=[pbf])
                        v_copy(ev_rot.next(), xT[:, c, half * 512:(half + 1) * 512], pst[:, :], reads=[],
                               writes=[pbf, bx[c][half]])
            elif t == 0:
                dma("sp", xT[:, :, :], XRES.rearrange("c p t -> p c t")[:, :, t0:t0 + TT],
                    writes=[bx[c][s] for c in range(8) for s in range(2)])

            last = (t == NTILE - 1)
            segs_t = segs + ([sseg] if last else [])
            rmsnorm(segs_t, l * 32 + 0, PsRot([6, 7]))
            ffn(segs_t, lambda i: tmps[i % 3], lambda i: btmps[i % 3])
            rmsnorm(segs_t, l * 32 + 8, PsRot([6, 7]))
            dma("sp", XRES.rearrange("c p t -> p c t")[:, :, t0:t0 + TT], xT[:, :, :],
                reads=[bx[c][s] for c in range(8) for s in range(2)])
            if l > 0 and t + 1 < NTILE:
                dma("sp", xT[:, :, :], XRES.rearrange("c p t -> p c t")[:, :, t0 + TT:t0 + 2 * TT],
                    writes=[bx[c][s] for c in range(8) for s in range(2)])

            psr = PsRot(range(8))

            def proj_fm(slot, bslot, cc, sg):
                pst, pbf = psr.next()
                for k in range(8):
                    mm(pst[:, :sg.n], slot[:, k, cc * 128:(cc + 1) * 128], sg.h(k), k == 0, k == 7,
                       reads=[bslot, sg.bh[k]], writes=[pbf])
                return pst, pbf

            def proj_tm(slot, bslot, tb):
                pst, pbf = psr.next()
                s = tb // 4
                for k in range(8):
                    mm(pst[:, :], hT[:, k, tb * 128:(tb + 1) * 128], slot[:, k, :], k == 0, k == 7,
                       reads=[bslot, bh[k][s]], writes=[pbf])
                return pst, pbf

            def stage_fm(j):
                return actT[:, j, :], chunk_bufs(j)

            def sample_proj(G, slot, bslot):
                if G <= 12:
                    pst, pbf = psr.next()
                    for k in range(8):
                        mm(pst[0:NS, :], hsT[:, k, :], slot[:, k, :], k == 0, k == 7, reads=[bslot, bhs[k]], writes=[pbf])
                    si = G % 2
                    v_copy(ev_rot.next(), stokst[si][0:NS, :], pst[0:NS, :], reads=[], writes=[pbf, bstok[si]])
                    dma("sp", STOK[:, G * 512:(G + 1) * 512], stokst[si][0:NS, :], reads=[bstok[si]])
                    if 3 <= G < 9:
                        g_ = (G - 3) % 3
                        half = 0 if G < 6 else 1
                        wb_ = WINS[g_]
                        dma("sp", kvs[g_][l, :, wb_ - 1, half * 512:(half + 1) * 512], stokst[si][0:NS, :],
                            reads=[bstok[si]])
                if G == 9:
                    for cc in range(4):
                        pst, pbf = proj_fm(slot, bslot, cc, sseg)
                        v_copy(ev_rot.next(), qrS[:, cc, :], pst[:, 0:NS], reads=[], writes=[pbf, bqrS])
                if G >= 13:
                    func = AF.Silu if G < 15 else AF.Sigmoid
                    for cc in range(4):
                        pst, pbf = proj_fm(slot, bslot, cc, sseg)
                        actf(gS[:, (G - 13) * 4 + cc, :], pst[:, 0:NS], func, reads=[], writes=[pbf, bgS])

            for G in range(19):
                slot, bslot = WS1.get()
                if last:
                    sample_proj(G, slot, bslot)
                if G < 6:
                    which, g = (0, G) if G < 3 else (1, G - 3)
                    for cc in range(4):
                        if g < 2:
                            j = fm_rot.next()
                            st, bst = stage_fm(j)
                        for sg in segs:
                            pst, pbf = proj_fm(slot, bslot, cc, sg)
                            eng = ev_rot.next()
                            if g == 0:
                                v_copy(eng, st[:, sg.s * 512:(sg.s + 1) * 512], pst[:, :], reads=[], writes=[pbf, bst[sg.s]])
                            elif g == 1:
                                v_copy(eng, st[:, sg.s * 512:(sg.s + 1) * 512].rearrange("p (r i) -> p r i", r=4),
                                       pst[:, :].rearrange("p (i r) -> p r i", r=4), reads=[], writes=[pbf, bst[sg.s]])
                            else:
                                i0 = ((t0 + sg.s * 512) % 2048) // 16
                                v_copy(eng, ust[:, which * 4 + cc, :].rearrange("p (r i) -> p r i", r=16)[:, :, i0:i0 + 32],
                                       pst[:, :].rearrange("p (i r) -> p r i", r=16), reads=[],
                                       writes=[pbf, bust[which * 4 + cc]])
                        if g < 2:
                            dma("sp", QK[which, g, cc, :, t0:t0 + TT], st, reads=bst)
                        elif t % 2 == 1:
                            u0 = (t // 2) * 2048
                            dma("sp", QK[which, 2, cc, :, u0:u0 + 2048], ust[:, which * 4 + cc, :],
                                reads=[bust[which * 4 + cc]])
                    if which == 1:
                        win = WINS[g]
                        for tb in range(8):
                            r0 = t0 + tb * 128
                            if r0 >= S - win:
                                pst, pbf = proj_tm(slot, bslot, tb)
                                j = f32_rot.next()
                                st = act_f32(j)
                                v_copy(ev_rot.next(), st, pst[:, :], reads=[], writes=[pbf] + chunk_bufs(j))
                                o0 = r0 - (S - win)
                                dma("sp", kvp[g][l, o0:o0 + 128, 0:512], st, reads=chunk_bufs(j))
                elif G < 9:
                    g = G - 6
                    win = WINS[g]
                    for tb in range(8):
                        r0 = t0 + tb * 128
                        pst, pbf = proj_tm(slot, bslot, tb)
                        vi = (tb + 8 * g) % 4
                        dst = vxst[vi].rearrange("p (m hp) e -> p m hp e", hp=2)
                        src = pst[:, :].rearrange("p (m hp e) -> p m hp e", hp=2, e=64)
                        v_copy(ev_rot.next(), dst[:, :, 0, 0:64], src[:, :, 0, :], reads=[], writes=[pbf, bvxst[vi]])
                        v_copy(ev_rot.next(), dst[:, :, 1, 64:128], src[:, :, 1, :], reads=[], writes=[pbf, bvxst[vi]])
                        dma("sp", VX[g, :, r0:r0 + 128, :].rearrange("c t f -> t c f"),
                            vxst[vi].rearrange("p (c h) e -> p c (h e)", h=2), reads=[bvxst[vi]])
                        if r0 >= S - win:
                            j = f32_rot.next()
                            st = act_f32(j)
                            v_copy(ev_rot.next(), st, pst[:, :], reads=[], writes=[pbf] + chunk_bufs(j))
                            o0 = r0 - (S - win)
                            dma("sp", kvp[g][l, o0:o0 + 128, 512:1024], st, reads=chunk_bufs(j))
                elif G in (9, 10):
                    dstT = QRT if G == 9 else KRT
                    for cc in range(4):
                        j = fm_rot.next()
                        st, bst = stage_fm(j)
                        for sg in segs:
                            pst, pbf = proj_fm(slot, bslot, cc, sg)
                            v_copy(ev_rot.next(), st[:, sg.s * 512:(sg.s + 1) * 512], pst[:, :], reads=[],
                                   writes=[pbf, bst[sg.s]])
                        dma("sp", dstT[cc, :, t0:t0 + TT], st, reads=bst)
                    if G == 10:
                        for tb in range(8):
                            r0 = t0 + tb * 128
                            pst, pbf = proj_tm(slot, bslot, tb)
                            j, hf = krk_rot.next()
                            st = actT[:, j, hf * 512:(hf + 1) * 512]
                            v_tt(st, pst[:, :], kdec, ALU.mult, reads=[bconst], writes=[pbf, bact[j][hf]])
                            dma("sp", KRK[r0:r0 + 128, :], st, reads=[bact[j][hf]])
                elif G < 13:
                    hf = G - 11
                    for tb in range(8):
                        r0 = t0 + tb * 128
                        pst, pbf = proj_tm(slot, bslot, tb)
                        j, h2 = krk_rot.next()
                        st = actT[:, j, h2 * 512:(h2 + 1) * 512]
                        v_copy(ev_rot.next(), st, pst[:, :], reads=[], writes=[pbf, bact[j][h2]])
                        dma("sp", VR[r0:r0 + 128, hf * 512:(hf + 1) * 512], st, reads=[bact[j][h2]])
                else:
                    gi = (G - 13) * 4
                    func = AF.Silu if G < 15 else AF.Sigmoid
                    for cc in range(4):
                        j = fm_rot.next()
                        st, bst = stage_fm(j)
                        for sg in segs:
                            pst, pbf = proj_fm(slot, bslot, cc, sg)
                            actf(st[:, sg.s * 512:(sg.s + 1) * 512], pst[:, :], func, reads=[], writes=[pbf, bst[sg.s]])
                        dma("sp", GT[gi + cc, :, t0:t0 + TT], st, reads=bst)
        T.barrier()

    def phase_B_attn(l):
        AR.off = PERSIST_END
        NE = 4
        PIPE = 2
        eb = AR.alloc([12, 512], BF16)
        SAB2 = [AR.alloc([2, 2048], F32) for _ in range(2)]
        qbd = [AR.alloc([16, 256], BF16) for _ in range(2)]
        kbuf = [AR.alloc([4096], BF16) for _ in range(2)]
        vbuf = [AR.alloc([32, 256], BF16) for _ in range(2)]
        ebuf = [AR.alloc([512], BF16) for _ in range(NE)]
        pbuf = [AR.alloc([512], BF16) for _ in range(NE)]
        oast = [AR.alloc([2048], BF16) for _ in range(2)]
        rd = AR.alloc([2048], F32)
        beb = Buf()
        bS2 = [Buf(), Buf()]
        bq = [Buf(), Buf()]
        bk = [Buf(), Buf()]
        bv = [[Buf() for _ in range(32)] for _ in range(2)]
        be = [Buf() for _ in range(NE)]
        bp = [Buf() for _ in range(NE)]
        bpp = [Buf() for _ in range(NE)]
        boa = [Buf(), Buf()]
        brd = Buf()
        dma("sp", eb, c_eb.rearrange("i p n -> p i n"), writes=[beb])
        ps_s = PsRot([0, 1, 2, 3])
        ps_o = PsRot([4, 5, 6, 7])
        for i_ in range(2):
            T.op("dve", (lambda i_=i_: dve.memset(qbd[i_], 0.0)), writes=[bq[i_]])
        groups = [(u, c, g) for u in range(2) for c in range(4) for g in range(3)]
        NG = len(groups)
        iters = [(gi, bl) for gi in range(NG) for bl in range(16)]
        NI = len(iters)

        def ginfo(gi):
            u, c, g = groups[gi]
            dil = DILS[g]
            nprev = dil if u == 1 else 0
            return u, c, g, dil, 128 * dil, 16 * u - nprev, 16 + nprev, gi % 2

        def load_group(gi):
            u, c, g, dil, unit_g, wstart, nblk, bi = ginfo(gi)
            qsrc = QK[0, g, c, :, u * 2048:(u + 1) * 2048].rearrange("p (b q) -> p b q", q=128)
            dma("sp", qbd[bi][0:64, :, 0:128], qsrc[0:64], writes=[bq[bi]])
            dma("sp", qbd[bi][64:128, :, 128:256], qsrc[64:128], writes=[bq[bi]])
            dma("sp", kbuf[bi][:, 0:nblk * 128], QK[1, g, c, :, wstart * 128:(wstart + nblk) * 128], writes=[bk[bi]])
            for n_ in range(wstart // dil, (wstart + nblk) // dil):
                sl0 = n_ * dil - wstart
                src = VX[g, c, n_ * unit_g:(n_ + 1) * unit_g, :].rearrange("(p r) f -> p r f", r=dil)
                dma("sp", vbuf[bi][:, sl0:sl0 + dil, :], src, writes=[bv[bi][sl] for sl in range(sl0, sl0 + dil)])

        st1 = {}

        def stage1(k):
            gi, bl = iters[k]
            u, c, g, dil, unit_g, wstart, nblk, bi = ginfo(gi)
            b = 16 * u + bl
            has_prev = (b - dil) >= 0
            sc = b - wstart
            spv = b - dil - wstart
            pss, bpss = ps_s.next()
            if has_prev:
                mm(pss[:, 0:256], kbuf[bi][:, spv * 128:(spv + 1) * 128], qbd[bi][:, bl, :], True, True,
                   reads=[bk[bi], bq[bi]], writes=[bpss])
            mm(pss[:, 256:512], kbuf[bi][:, sc * 128:(sc + 1) * 128], qbd[bi][:, bl, :], True, True,
               reads=[bk[bi], bq[bi]], writes=[bpss])
            e_i = k % NE
            ebt, pbt = ebuf[e_i], pbuf[e_i]
            ebg = eb[:, g * 4 + c, :]
            c0_ = 0 if has_prev else 256
            actf(ebt[:, c0_:512], pss[:, c0_:512], AF.Exp, reads=[], writes=[bpss, be[e_i]], scale=0.125)
            if has_prev:
                v_tt(pbt[:, 0:256], ebt[:, 0:256], ebg[:, 0:256], ALU.mult, reads=[be[e_i], beb], writes=[bpp[e_i]],
                     eng="pool")
            v_tt(pbt[:, 256:512], ebt[:, 256:512], ebg[:, 256:512], ALU.mult, reads=[be[e_i], beb], writes=[bp[e_i]])

        def stage2(k):
            gi, bl = iters[k]
            u, c, g, dil, unit_g, wstart, nblk, bi = ginfo(gi)
            b = 16 * u + bl
            has_prev = (b - dil) >= 0
            sc = b - wstart
            spv = b - dil - wstart
            e_i = k % NE
            pbt = pbuf[e_i]
            si = (u * 4 + c) % 2
            SAB, bS = SAB2[si], bS2[si]
            pso, bpso = ps_o.next()
            for hp in range(2):
                oc = slice(hp * 128, (hp + 1) * 128)
                if has_prev:
                    mm(pso[:, oc], vbuf[bi][:, spv, oc], pbt[:, hp * 128:hp * 128 + 128], True, False,
                       reads=[bv[bi][spv], bpp[e_i]], writes=[bpso])
                mm(pso[:, oc], vbuf[bi][:, sc, oc], pbt[:, 256 + hp * 128:256 + hp * 128 + 128], not has_prev, True,
                   reads=[bv[bi][sc], bp[e_i]], writes=[bpso])
            nl_, r_ = bl // dil, bl % dil
            off = nl_ * unit_g + r_
            dst = SAB[:, :, off:off + 127 * dil + 1:dil]
            src = pso[:, 0:256].rearrange("p (a q) -> p a q", a=2)
            if g == 0:
                v_copy("act", dst, src, reads=[], writes=[bpso, bS])
            else:
                v_tt(dst, dst, src, ALU.add, reads=[], writes=[bpso, bS])
            if g == 2 and bl == 15:
                oi = si

                def mk(kind, cs_):
                    def f():
                        if kind == 0:
                            actf(rd[0:64, cs_], SAB[64:128, 0, cs_], AF.Ln, reads=[bS], writes=[brd])
                        elif kind == 1:
                            actf(rd[64:128, cs_], SAB[0:64, 1, cs_], AF.Ln, reads=[bS], writes=[brd])
                        elif kind == 2:
                            actf(rd[:, cs_], rd[:, cs_], AF.Exp, reads=[], writes=[brd], scale=-1.0)
                        elif kind == 3:
                            v_tt(oast[oi][0:64, cs_], SAB[0:64, 0, cs_], rd[0:64, cs_], ALU.mult, reads=[bS, brd],
                                 writes=[boa[oi]])
                        elif kind == 4:
                            v_tt(oast[oi][64:128, cs_], SAB[64:128, 1, cs_], rd[64:128, cs_], ALU.mult, reads=[bS, brd],
                                 writes=[boa[oi]])
                    return f
                for q4 in range(4):
                    cs_ = slice(q4 * 512, (q4 + 1) * 512)
                    for kind in range(5):
                        pending.append(mk(kind, cs_))
                pending.append(lambda: dma("sp", OAT[c, :, u * 2048:(u + 1) * 2048], oast[oi], reads=[boa[oi]]))

        pending = []
        load_group(0)
        load_group(1)
        for k in range(NI + PIPE):
            if k < NI:
                gi, bl = iters[k]
                if bl == PIPE and gi >= 1 and gi + 1 < NG:
                    load_group(gi + 1)
                stage1(k)
            if k - PIPE >= 0:
                stage2(k - PIPE)
            if pending:
                pending.pop(0)()
        while pending:
            pending.pop(0)()
        T.barrier()

    def phase_B_ret(l):
        AR.off = PERSIST_END
        dmask = AR.alloc([512], F32)
        qdec = AR.alloc([512], F32)
        state_f = AR.alloc([4, 256], F32)
        state_b = AR.alloc([4, 256], BF16)
        TG = 512
        NCH = TG // 128
        qrg = [AR.alloc([4, TG], BF16) for _ in range(2)]
        krg = [AR.alloc([4, TG], BF16) for _ in range(2)]
        krk = [AR.alloc([NCH, 512], BF16) for _ in range(2)]
        vrg = [AR.alloc([NCH, 1024], BF16) for _ in range(2)]
        gtg = [AR.alloc([8, TG], BF16) for _ in range(2)]
        yst = [AR.alloc([8, TG], BF16) for _ in range(2)]
        NR = 10
        innT = [AR.alloc([128], BF16) for _ in range(NR)]
        qd = [AR.alloc([128], BF16) for _ in range(NR)]
        on = [AR.alloc([256], BF16) for _ in range(NR)]
        stats = [AR.alloc([6], F32) for _ in range(NR)]
        mv = [AR.alloc([2], F32) for _ in range(NR)]
        rs = [AR.alloc([1], F32) for _ in range(NR)]
        nmr = [AR.alloc([1], F32) for _ in range(NR)]
        bc2 = Buf()
        bsf = [Buf() for _ in range(4)]
        bsb = [Buf() for _ in range(4)]
        bin_ = [Buf(), Buf()]
        byst = [Buf(), Buf()]
        binn = [Buf() for _ in range(NR)]
        bqd = [Buf() for _ in range(NR)]
        bon = [Buf() for _ in range(NR)]
        bst = [Buf() for _ in range(NR)]
        gam = _gammas()
        dma("sp", dmask, c_dmask, writes=[bc2])
        dma("sp", qdec, c_qdec, writes=[bc2])
        T.op("dve", lambda: dve.memset(state_f, 0.0), writes=bsf)
        ps_i = PsRot([0, 1])
        ps_oo = PsRot([2, 3])
        ps_st = PsRot([4, 5])
        ps_t = PsRot([6, 7])
        NGRP = S // TG
        NIT = NGRP * NCH * 4
        ctx = {}

        def load_grp(tg):
            t0 = tg * TG
            gi = tg % 2
            dma("sp", qrg[gi], QRT.rearrange("h p t -> p h t")[:, :, t0:t0 + TG], writes=[bin_[gi]])
            dma("sp", krg[gi], KRT.rearrange("h p t -> p h t")[:, :, t0:t0 + TG], writes=[bin_[gi]])
            dma("sp", krk[gi], KRK[t0:t0 + TG, :].rearrange("(n p) f -> p n f", p=128), writes=[bin_[gi]])
            dma("sp", vrg[gi], VR[t0:t0 + TG, :].rearrange("(n p) f -> p n f", p=128), writes=[bin_[gi]])
            dma("sp", gtg[gi], GT.rearrange("c p t -> p c t")[:, 0:8, t0:t0 + TG], writes=[bin_[gi]])

        def info(k):
            tg = k // (NCH * 4)
            nl = (k // 4) % NCH
            h = k % 4
            return tg, tg % 2, nl, tg * NCH + nl, h, slice(nl * 128, (nl + 1) * 128), slice(h * 128, (h + 1) * 128), k % NR

        ps_po = PsRot([2, 3, 4])
        ps_st2 = PsRot([5, 6])
        ps_it = Rot([0, 1, 7])

        def t0_(k):
            tg, gi, nl, n, h, cs, hs, r3 = info(k)
            bk_ = ps_it.next()
            ctx[("i", k)] = bk_
            mm(psb[bk_][:, 0:128], krg[gi][:, h, cs], qrg[gi][:, h, cs], True, True, reads=[bin_[gi]], writes=[bps[bk_]])

        def t1_(k):
            tg, gi, nl, n, h, cs, hs, r3 = info(k)
            bk_ = ctx[("i", k)]
            v_tt(innT[r3], psb[bk_][:, 0:128], dmask[:, hs], ALU.mult, reads=[bc2], writes=[bps[bk_], binn[r3]])
            if n > 0:
                v_tt(qd[r3], qrg[gi][:, h, cs], qdec[:, hs], ALU.mult, reads=[bin_[gi], bc2], writes=[bqd[r3]], eng="pool")

        def t2_(k):
            tg, gi, nl, n, h, cs, hs, r3 = info(k)
            po, bpo = ps_po.next()
            pst_, bpst = ps_st2.next()
            ctx[k] = (po, bpo, pst_, bpst)
            mm(po[:, 0:256], innT[r3], vrg[gi][:, nl, h * 256:(h + 1) * 256], True, n == 0,
               reads=[binn[r3], bin_[gi]], writes=[bpo])
            if n > 0:
                mm(po[:, 0:256], qd[r3], state_b[:, h, :], False, True, reads=[bqd[r3], bsb[h]], writes=[bpo])
            mm(pst_[:, 0:256], krk[gi][:, nl, hs], vrg[gi][:, nl, h * 256:(h + 1) * 256], True, True,
               reads=[bin_[gi]], writes=[bpst])

        def t3_(k):
            tg, gi, nl, n, h, cs, hs, r3 = info(k)
            po, bpo, pst_, bpst = ctx[k]
            v_stt(state_f[:, h, :], state_f[:, h, :], float(gam[h] ** 128), pst_[:, 0:256], ALU.mult, ALU.add,
                  reads=[], writes=[bpst, bsf[h]])
            T.op("dve", (lambda a=stats[r3], b_=po[:, 0:256]: dve.bn_stats(out=a, in_=b_)), reads=[], writes=[bpo, bst[r3]])
            T.op("dve", (lambda a=mv[r3], b_=stats[r3]: dve.bn_aggr(out=a, in_=b_)), reads=[], writes=[bst[r3]])
            if n < 31:
                v_copy("act", state_b[:, h, :], state_f[:, h, :], reads=[bsf[h]], writes=[bsb[h]])
            actf(rs[r3], mv[r3][:, 1:2], AF.Sqrt, reads=[], writes=[bst[r3]], bias=GN_EPS)

        def t4_(k):
            tg, gi, nl, n, h, cs, hs, r3 = info(k)
            po, bpo, pst_, bpst = ctx.pop(k)
            ctx[("po", k)] = (po, bpo)
            v_recip(rs[r3], rs[r3], reads=[], writes=[bst[r3]])
            v_stt(nmr[r3], mv[r3][:, 0:1], -1.0, rs[r3], ALU.mult, ALU.mult, reads=[], writes=[bst[r3]])
            T.op("act", (lambda o_=on[r3], i_=po[:, 0:256], sc_=rs[r3][:, 0:1], b_=nmr[r3][:, 0:1]:
                         act.activation(out=o_, in_=i_, func=AF.Identity, scale=sc_, bias=b_)),
                 reads=[bst[r3]], writes=[bpo, bon[r3]])

        def t5_(k):
            tg, gi, nl, n, h, cs, hs, r3 = info(k)
            ctx.pop(("i", k))
            po, bpo = ctx.pop(("po", k))
            ptb = po[:, 256:512].bitcast(BF16)
            ctx[("t", k)] = (ptb, bpo)
            for ec in range(2):
                tr(ptb[:, ec * 128:(ec + 1) * 128], on[r3][:, ec * 128:(ec + 1) * 128], identb,
                   reads=[bon[r3], bconst], writes=[bpo])

        def t6_(k):
            tg, gi, nl, n, h, cs, hs, r3 = info(k)
            ptb, bpt = ctx.pop(("t", k))
            for ec in range(2):
                ch = 2 * h + ec
                gcol = l * 32 + 24 + ch
                v_stt(yst[gi][:, ch, cs], ptb[:, ec * 128:(ec + 1) * 128], gains[:, gcol:gcol + 1],
                      gtg[gi][:, ch, cs], ALU.mult, ALU.mult, reads=[bin_[gi], bconst], writes=[bpt, byst[gi]])
            if nl == NCH - 1 and h == 3:
                t0g = tg * TG
                dma("sp", YRT.rearrange("c p t -> p c t")[:, :, t0g:t0g + TG], yst[gi], reads=[byst[gi]])

        stages_ = [t0_, t1_, t2_, t3_, t4_, t5_, t6_]
        NST = len(stages_)
        load_grp(0)
        load_grp(1)
        per = NCH * 4
        for k in range(NIT + NST - 1):
            if k < NIT:
                tg = k // per
                if k % per == NST - 1 and tg >= 1 and tg + 1 < NGRP:
                    load_grp(tg + 1)
            for si_, fn_ in enumerate(stages_):
                kk = k - si_
                if 0 <= kk < NIT:
                    fn_(kk)
        dma("sp", retp[l].rearrange("h d e -> d h e"), state_f, reads=bsf)
        T.barrier()

    def phase_B_sample(l):
        AR.off = PERSIST_END
        gam = _gammas()
        stk = AR.alloc([6656], F32)
        sbias = AR.alloc([3, 8], F32)
        sel = AR.alloc([4, 4], F32)
        kvt = [AR.alloc([1024], F32) for _ in range(2)]
        qbc = [AR.alloc([512], F32) for _ in range(2)]
        prod = [AR.alloc([512], F32) for _ in range(2)]
        scb = [AR.alloc([8], F32) for _ in range(2)]
        pb2 = [AR.alloc([8], F32) for _ in range(2)]
        wt = [AR.alloc([512], F32) for _ in range(2)]
        prn = AR.alloc([512], F32)
        sn = AR.alloc([8], F32)
        pn = AR.alloc([3, 8], F32)
        wn = AR.alloc([512], F32)
        numn = AR.alloc([512], F32)
        denn = AR.alloc([8], F32)
        oS = AR.alloc([512], F32)
        S0 = AR.alloc([4, 4, 256], F32)
        Snew = AR.alloc([4, 4, 256], F32)
        qsel = AR.alloc([4, 16], F32)
        qk = AR.alloc([4], F32)
        o1 = AR.alloc([1024], F32)
        oR = AR.alloc([1024], F32)
        onS = AR.alloc([1024], F32)
        statS = AR.alloc([4, 6], F32)
        mvS = AR.alloc([4, 2], F32)
        rsS = AR.alloc([4], F32)
        ksel = [AR.alloc([512], F32) for _ in range(2)]
        bstk, bcs = Buf(), Buf()
        bkvt = [Buf(), Buf()]
        bqbc = [Buf(), Buf()]
        bprod = [Buf(), Buf()]
        bsc = [Buf(), Buf()]
        bpb = [Buf(), Buf()]
        bwt = [Buf(), Buf()]
        bmisc = Buf()
        bS0, bSn, bqsel = Buf(), Buf(), Buf()
        bksel = [Buf(), Buf()]
        AX = mybir.AxisListType.X

        def red(out, in_, reads, writes):
            return T.op("dve", (lambda: dve.tensor_reduce(out=out, in_=in_, axis=AX, op=ALU.add)), reads=reads,
                        writes=writes)

        dma("sp", sbias, c_sbias, writes=[bcs])
        dma("sp", sel, c_sel, writes=[bcs])
        NUMps, bNUM = psb[0], bps[0]
        DENps, bDEN = psb[1], bps[1]
        idx = 0
        for s_ in range(NS):
            for g in range(3):
                dil, wb_ = DILS[g], WINS[g]
                i2 = idx % 2
                dma("sp", kvt[i2], cache_in[g][l, s_, 0:wb_ - dil + 1:dil, :], writes=[bkvt[i2]])
                dma("sp", qbc[i2], STOK[s_:s_ + 1, g * 512:(g + 1) * 512].broadcast_to([128, 512]), reads=[],
                    writes=[bqbc[i2]], )
                v_tt(prod[i2], kvt[i2][:, 0:512], qbc[i2], ALU.mult, reads=[bkvt[i2], bqbc[i2]], writes=[bprod[i2]])
                red(scb[i2], prod[i2].rearrange("p (h e) -> p h e", e=64), reads=[bprod[i2]], writes=[bsc[i2]])
                v_stt(scb[i2], scb[i2], 0.125, sbias[:, g, :], ALU.mult, ALU.add, reads=[bcs], writes=[bsc[i2]])
                actf(pb2[i2], scb[i2], AF.Exp, reads=[bsc[i2]], writes=[bpb[i2]])
                v_tt(wt[i2].rearrange("p (h e) -> p h e", e=64), kvt[i2][:, 512:1024].rearrange("p (h e) -> p h e", e=64),
                     pb2[i2].unsqueeze(2).broadcast_to([128, 8, 64]), ALU.mult, reads=[bkvt[i2], bpb[i2]],
                     writes=[bwt[i2]])
                mm(NUMps[0:NS, :], sel[:, s_, :], wt[i2], idx == 0, idx == 11, reads=[bcs, bwt[i2]], writes=[bNUM])
                mm(DENps[0:NS, 0:8], sel[:, s_, :], pb2[i2], idx == 0, idx == 11, reads=[bcs, bpb[i2]], writes=[bDEN])
                idx += 1
                if idx == 2:
                    dma("sp", stk[0:NS, :], STOK, writes=[bstk])
                    dma("sp", S0, sret[l].rearrange("s h d e -> d s h e"), writes=[bS0])
        for g in range(3):
            qn = stk[0:NS, g * 512:(g + 1) * 512]
            kn = stk[0:NS, 1536 + g * 512:1536 + (g + 1) * 512]
            vn = stk[0:NS, 3072 + g * 512:3072 + (g + 1) * 512]
            v_tt(prn[0:NS, :], qn, kn, ALU.mult, reads=[bstk], writes=[bmisc])
            red(sn[0:NS, :], prn[0:NS, :].rearrange("p (h e) -> p h e", e=64), reads=[], writes=[bmisc])
            actf(pn[0:NS, g, :], sn[0:NS, :], AF.Exp, reads=[], writes=[bmisc], scale=0.125)
            dstn = numn if g == 0 else wn
            v_tt(dstn[0:NS, :].rearrange("p (h e) -> p h e", e=64), vn.rearrange("p (h e) -> p h e", e=64),
                 pn[0:NS, g, :].unsqueeze(2).broadcast_to([NS, 8, 64]), ALU.mult, reads=[bstk], writes=[bmisc])
            if g > 0:
                v_tt(numn[0:NS, :], numn[0:NS, :], wn[0:NS, :], ALU.add, reads=[], writes=[bmisc])
        v_tt(denn[0:NS, :], pn[0:NS, 0, :], pn[0:NS, 1, :], ALU.add, reads=[], writes=[bmisc])
        v_tt(denn[0:NS, :], denn[0:NS, :], pn[0:NS, 2, :], ALU.add, reads=[], writes=[bmisc])
        v_tt(numn[0:NS, :], numn[0:NS, :], NUMps[0:NS, :], ALU.add, reads=[], writes=[bmisc, bNUM])
        v_tt(denn[0:NS, :], denn[0:NS, :], DENps[0:NS, 0:8], ALU.add, reads=[], writes=[bmisc, bDEN])
        v_recip(denn[0:NS, :], denn[0:NS, :], reads=[], writes=[bmisc])
        v_tt(oS[0:NS, :].rearrange("p (h e) -> p h e", e=64), numn[0:NS, :].rearrange("p (h e) -> p h e", e=64),
             denn[0:NS, :].unsqueeze(2).broadcast_to([NS, 8, 64]), ALU.mult, reads=[], writes=[bmisc])
        pt, bpt = psb[2], bps[2]
        for cc in range(4):
            tr(pt[:, cc * NS:(cc + 1) * NS], oS[0:NS, cc * 128:(cc + 1) * 128], ident[0:NS, 0:NS], reads=[bmisc, bconst],
               writes=[bpt])
        v_copy("dve", actS[:, 0:4, :], pt[:, 0:4 * NS].rearrange("p (c s) -> p c s", s=NS), reads=[],
               writes=[bpt] + bas[0:4])

        QR0, KR0, VR0 = 4608, 5120, 5632
        T.op("dve", (lambda: dve.memset(qsel, 0.0)), writes=[bqsel])
        v_copy("dve", qsel[:, :, 0:16:5], qrS[:, :, :], reads=[bqrS], writes=[bqsel])
        pq = [psb[3], psb[4]]
        bpq = [bps[3], bps[4]]
        for h in range(4):
            for s_ in range(NS):
                mm(pq[h // 2][0:NS, (h % 2) * 256:(h % 2) * 256 + 256], qsel[:, h, s_ * 4:(s_ + 1) * 4], S0[:, s_, h, :],
                   s_ == 0, s_ == NS - 1, reads=[bqsel, bS0], writes=[bpq[h // 2]])
        v_tt(prn[0:NS, :], stk[0:NS, QR0:QR0 + 512], stk[0:NS, KR0:KR0 + 512], ALU.mult, reads=[bstk], writes=[bmisc])
        red(qk[0:NS, :], prn[0:NS, :].rearrange("p (h d) -> p h d", d=128), reads=[], writes=[bmisc])
        v_ts(qk[0:NS, :], qk[0:NS, :], float(128.0 ** -0.5), None, ALU.mult, None, reads=[], writes=[bmisc])
        v_tt(o1[0:NS, :].rearrange("p (h e) -> p h e", e=256), stk[0:NS, VR0:VR0 + 1024].rearrange("p (h e) -> p h e", e=256),
             qk[0:NS, :].unsqueeze(2).broadcast_to([NS, 4, 256]), ALU.mult, reads=[bstk], writes=[bmisc])
        for h in range(4):
            hs = slice(h * 256, (h + 1) * 256)
            v_stt(oR[0:NS, hs], pq[h // 2][0:NS, (h % 2) * 256:(h % 2) * 256 + 256], float(gam[h]), o1[0:NS, hs],
                  ALU.mult, ALU.add, reads=[], writes=[bmisc, bpq[h // 2]])
            T.op("dve", (lambda h=h, hs=hs: dve.bn_stats(out=statS[0:NS, h, :], in_=oR[0:NS, hs])), reads=[],
                 writes=[bmisc])
            T.op("dve", (lambda h=h: dve.bn_aggr(out=mvS[0:NS, h, :], in_=statS[0:NS, h, :])), reads=[], writes=[bmisc])
        actf(rsS[0:NS, :], mvS[0:NS, :, 1], AF.Sqrt, reads=[], writes=[bmisc], bias=GN_EPS)
        v_recip(rsS[0:NS, :], rsS[0:NS, :], reads=[], writes=[bmisc])
        for h in range(4):
            hs = slice(h * 256, (h + 1) * 256)
            v_ts(onS[0:NS, hs], oR[0:NS, hs], mvS[0:NS, h, 0:1], rsS[0:NS, h:h + 1], ALU.subtract, ALU.mult, reads=[],
                 writes=[bmisc])
        pt2, bpt2 = psb[5], bps[5]
        for ch in range(8):
            tr(pt2[:, ch * NS:(ch + 1) * NS], onS[0:NS, ch * 128:(ch + 1) * 128], ident[0:NS, 0:NS], reads=[bmisc, bconst],
               writes=[bpt2])
        for ch in range(8):
            gcol = l * 32 + 24 + ch
            v_stt(actS[:, 4 + ch, :], pt2[:, ch * NS:(ch + 1) * NS], gains[:, gcol:gcol + 1], gS[:, ch, :], ALU.mult,
                  ALU.mult, reads=[bgS, bconst], writes=[bpt2, bas[4 + ch]])
        ps_r = PsRot([6, 7])
        for s_ in range(NS):
            k2 = s_ % 2
            v_ts(ksel[k2][0:NS, :], stk[0:NS, KR0:KR0 + 512], ident[0:NS, s_:s_ + 1], float(128.0 ** -0.5), ALU.mult,
                 ALU.mult, reads=[bstk, bconst], writes=[bksel[k2]])
            for h in range(4):
                pr, bpr = ps_r.next()
                mm(pr[:, 0:256], ksel[k2][0:NS, h * 128:(h + 1) * 128], stk[0:NS, VR0 + h * 256:VR0 + (h + 1) * 256], True,
                   True, reads=[bksel[k2], bstk], writes=[bpr])
                v_stt(Snew[:, s_, h, :], S0[:, s_, h, :], float(gam[h]), pr[:, 0:256], ALU.mult, ALU.add, reads=[bS0],
                      writes=[bpr, bSn])
        dma("sp", rets[l].rearrange("s h d e -> d s h e"), Snew, reads=[bSn])
        T.barrier()

    def phase_C(l):
        AR.off = PERSIST_END
        xT = AR.alloc([8, TT], F32)
        hT = AR.alloc([8, TT], BF16)
        actT = AR.alloc([NJ, TT], BF16)
        act_off = AR.last
        rstd = AR.alloc([TT], F32)
        tmps = [AR.alloc([512], BF16) for _ in range(3)]
        t12 = [AR.alloc([512], F32) for _ in range(4)]
        ystg = [AR.alloc([D], F32) for _ in range(2)]
        cin = AR.alloc([12, TT], BF16)
        gbuf = [[AR.alloc([TT], BF16) for _ in range(2)] for _ in range(2)]
        bx = [[Buf() for _ in range(2)] for _ in range(8)]
        bh = [[Buf() for _ in range(2)] for _ in range(8)]
        bact = [[Buf() for _ in range(2)] for _ in range(NJ)]
        bcin = [[Buf() for _ in range(2)] for _ in range(12)]
        bgb2 = [[[Buf(), Buf()] for _ in range(2)] for _ in range(2)]
        brs = [Buf(), Buf()]
        btmps = [Buf() for _ in range(3)]
        bt12 = [Buf() for _ in range(4)]
        bystg = [Buf(), Buf()]
        segs = make_segs(xT, hT, actT, rstd, bx, bh, bact, brs)
        ev_rot = Rot(["act", "dve"])
        XR = XRES.rearrange("c p t -> p c t")

        def load_cin(t):
            t0_ = t * TT
            dma("sp", cin[:, 0:4, :], OAT.rearrange("c p t -> p c t")[:, :, t0_:t0_ + TT],
                writes=[bcin[j][s] for j in range(0, 4) for s in range(2)])
            dma("sp", cin[:, 4:12, :], YRT.rearrange("c p t -> p c t")[:, :, t0_:t0_ + TT],
                writes=[bcin[j][s] for j in range(4, 12) for s in range(2)])

        load_cin(0)
        for t in range(NTILE):
            t0 = t * TT
            psr = PsRot(range(8))
            last = (t == NTILE - 1)
            segs_t = segs + ([sseg] if last else [])
            for hf in range(2):
                wa_s, bwa = WS1.get()
                wb_s, bwb = WS1.get()
                for mi in range(4):
                    m = hf * 4 + mi
                    gi_ = m % 2
                    dma("sp", gbuf[gi_][0], GT[8 + m, :, t0:t0 + TT], writes=bgb2[gi_][0])
                    dma("sp", gbuf[gi_][1], GT[16 + m, :, t0:t0 + TT], writes=bgb2[gi_][1])
                    if m == 1:
                        dma("sp", xT[:, :, :], XR[:, :, t0:t0 + TT], writes=[bx[c][s] for c in range(8) for s in range(2)])
                    for sg in segs_t:
                        n = sg.n
                        if sg is sseg:
                            ga_ap, gb_ap = gS[:, 8 + m, :], gS[:, 16 + m, :]
                            bga, bgb = bgS, bgS
                            i1 = 0
                            oa_ = sg.a
                            boa_ = sg.ba
                        else:
                            ssl = slice(sg.s * 512, (sg.s + 1) * 512)
                            ga_ap, gb_ap = gbuf[gi_][0][:, ssl], gbuf[gi_][1][:, ssl]
                            bga, bgb = bgb2[gi_][0][sg.s], bgb2[gi_][1][sg.s]
                            i1 = (m * 2 + sg.s) % 2
                            oa_ = (lambda k, ssl=ssl: cin[:, k, ssl])
                            boa_ = [bcin[k][sg.s] for k in range(12)]
                        pa, bpa = psr.next()
                        pb_, bpb = psr.next()
                        for k in range(4):
                            mm(pa[:, :n], wa_s[:, k, mi * 128:(mi + 1) * 128], oa_(k), k == 0, k == 3,
                               reads=[bwa, boa_[k]], writes=[bpa])
                        for k in range(8):
                            mm(pb_[:, :n], wb_s[:, k, mi * 128:(mi + 1) * 128], oa_(4 + k), k == 0, k == 7,
                               reads=[bwb, boa_[4 + k]], writes=[bpb])
                        ta, tb_ = t12[2 * i1], t12[2 * i1 + 1]
                        v_tt(ta[:, :n], pa[:, :n], ga_ap, ALU.mult, reads=[bga], writes=[bpa, bt12[2 * i1]])
                        v_tt(tb_[:, :n], pb_[:, :n], gb_ap, ALU.mult, reads=[bgb], writes=[bpb, bt12[2 * i1 + 1]])
                        v_tt(sg.h(m), ta[:, :n], tb_[:, :n], ALU.add, reads=[bt12[2 * i1], bt12[2 * i1 + 1]],
                             writes=[sg.bh[m]], eng="pool")
            if t + 1 < NTILE:
                load_cin(t + 1)
            for hf in range(2):
                wo_s, bwo = WS1.get()
                for mi in range(4):
                    m = hf * 4 + mi
                    for sg in segs_t:
                        n = sg.n
                        pm, bpm = psr.next()
                        for k in range(8):
                            mm(pm[:, :n], wo_s[:, k, mi * 128:(mi + 1) * 128], sg.h(k), k == 0, k == 7,
                               reads=[bwo, sg.bh[k]], writes=[bpm])
                        v_tt(sg.x(m), pm[:, :n], sg.x(m), ALU.add, reads=[], writes=[bpm, sg.bx[m]])
            rmsnorm(segs_t, l * 32 + 16, PsRot([6, 7]))
            ffn(segs_t, lambda i: tmps[i % 3], lambda i: btmps[i % 3])
            if l < DEPTH - 1:
                dma("sp", XR[:, :, t0:t0 + TT], xT[:, :, :], reads=[bx[c][s] for c in range(8) for s in range(2)])
            else:
                yfin = AR.view(act_off, [8, TT], F32)
                psn = PsRot([6, 7])
                for sg in segs:
                    ssl = slice(sg.s * 512, (sg.s + 1) * 512)
                    for c in range(8):
                        actf(sg.h(c), sg.x(c), AF.Square, reads=[sg.bx[c]], writes=[sg.bh[c]])
                    pst, pbf = psn.next()
                    for c in range(8):
                        mm(pst[:, :], ones_b, sg.h(c), c == 0, c == 7, reads=[sg.bh[c], bconst], writes=[pbf])
                    actf(sg.rstd, pst[:, :], AF.Sqrt, reads=[], writes=[pbf, sg.brs], scale=1.0 / 1024.0, bias=NORM_EPS)
                    v_recip(sg.rstd, sg.rstd, reads=[], writes=[sg.brs])
                    for c in range(8):
                        v_stt(yfin[:, c, ssl], sg.x(c), gains[:, 64 + c:65 + c], sg.rstd, ALU.mult, ALU.mult,
                              reads=[sg.bx[c], sg.brs, bconst],
                              writes=[bact[2 * c][sg.s], bact[2 * c + 1][sg.s], bact[2 * c][1 - sg.s], bact[2 * c + 1][1 - sg.s]])
                pst_r = PsRot(range(6))
                for tb in range(8):
                    yi = tb % 2
                    s_ = tb // 4
                    for cg in range(2):
                        pst, pbf = pst_r.next()
                        for ci in range(4):
                            c = cg * 4 + ci
                            tr(pst[:, ci * 128:(ci + 1) * 128], yfin[:, c, tb * 128:(tb + 1) * 128], ident,
                               reads=[bact[2 * c][s_], bact[2 * c + 1][s_], bconst], writes=[pbf])
                        v_copy(ev_rot.next(), ystg[yi][:, cg * 512:(cg + 1) * 512], pst[:, :], reads=[],
                               writes=[pbf, bystg[yi]])
                    dma("sp", yp[t0 + tb * 128:t0 + (tb + 1) * 128, :], ystg[yi], reads=[bystg[yi]])
                if last:
                    yfS = t12[0].rearrange("p (c s) -> p c s", s=64)
                    for c in range(8):
                        actf(hsT[:, c, :], xsT[:, c, :], AF.Square, reads=[bxs[c]], writes=[bhs[c]])
                    pst, pbf = psn.next()
                    for c in range(8):
                        mm(pst[:, 0:NS], ones_b, hsT[:, c, :], c == 0, c == 7, reads=[bhs[c], bconst], writes=[pbf])
                    actf(rstdS, pst[:, 0:NS], AF.Sqrt, reads=[], writes=[pbf, brss], scale=1.0 / 1024.0, bias=NORM_EPS)
                    v_recip(rstdS, rstdS, reads=[], writes=[brss])
                    for c in range(8):
                        v_stt(yfS[:, c, 0:NS], xsT[:, c, :], gains[:, 64 + c:65 + c], rstdS, ALU.mult, ALU.mult,
                              reads=[bxs[c], brss, bconst], writes=[bt12[0]])
                    for cg in range(2):
                        pst, pbf = pst_r.next()
                        for ci in range(4):
                            c = cg * 4 + ci
                            tr(pst[0:NS, ci * 128:(ci + 1) * 128], yfS[:, c, 0:NS], ident, reads=[bt12[0], bconst],
                               writes=[pbf])
                        v_copy(ev_rot.next(), ystg[0][0:NS, cg * 512:(cg + 1) * 512], pst[0:NS, :], reads=[],
                               writes=[pbf, bystg[0]])
                    dma("sp", ys, ystg[0][0:NS, :], reads=[bystg[0]])
        T.barrier()

    stages = []
    for l in range(DEPTH):
        stages += [("A", l), ("BS", l), ("B1", l), ("B2", l), ("C", l)]
    if only is None:
        load_sample_x()
        T.barrier()
    for (ph, l) in stages:
        if only is not None and ph != only:
            continue
        if ph == "A":
            phase_A(l)
        elif ph == "BS":
            phase_B_sample(l)
        elif ph == "B1":
            phase_B_attn(l)
        elif ph == "B2":
            phase_B_ret(l)
        else:
            phase_C(l)
        if stop_after == f"{ph}{l}":
            break
    T.emit()
    es.close()
    return nc, T


def _gains_table(inp):
    g = np.zeros((128, 72), np.float32)
    for l in range(DEPTH):
        for i, nm in enumerate(("norm_ffn1", "norm_mix", "norm_ffn2", "ret_gn")):
            g[:, l * 32 + i * 8:l * 32 + i * 8 + 8] = np.asarray(inp[nm][l], np.float32).reshape(8, 128).T
    g[:, 64:72] = np.asarray(inp["norm_final"], np.float32).reshape(8, 128).T
    return g


def make_in_maps(inp):
    consts = _const_tables()
    gains = _gains_table(inp)
    shared = dict(consts)
    shared["gains"] = gains
    for nm in ("ffn1_wg", "ffn1_wu", "ffn1_wd", "ffn2_wg", "ffn2_wu", "ffn2_wd", "w_in", "w_br_a", "w_br_b", "w_out"):
        shared[nm] = np.ascontiguousarray(inp[nm], dtype=np.float32)
    maps = []
    for c in range(NCORES):
        m = dict(shared)
        m["xp"] = np.ascontiguousarray(inp["x_prompt"][c])
        sl = slice(c * NS, (c + 1) * NS)
        m["xs"] = np.ascontiguousarray(inp["x_sample"][sl, 0, :])
        for w, nm in zip(WINS, ("cache_kv_w128", "cache_kv_w512", "cache_kv_w2048")):
            m[f"c{w}"] = np.ascontiguousarray(inp[nm][:, sl]).reshape(DEPTH, NS, w, 1024)
        m["sret"] = np.ascontiguousarray(inp["state_ret"][:, sl])
        maps.append(m)
    return maps


_PROGRAM_CACHE = {}


def kernel(**inputs):
    inp = {k: np.asarray(v) for k, v in inputs.items()}
    if "prog" not in _PROGRAM_CACHE:
        _PROGRAM_CACHE["prog"] = build_program(debug=False)
    nc, _ = _PROGRAM_CACHE["prog"]
    maps = make_in_maps(inp)
    res = run_bass_kernel_spmd(nc, maps, core_ids=list(range(NCORES)))
    R = res.results
    f32 = np.float32
    y_prompt = np.stack([np.asarray(R[c]["yp"], f32) for c in range(NCORES)], 0)
    y_sample = np.concatenate([np.asarray(R[c]["ys"], f32) for c in range(NCORES)], 0).reshape(NCORES * NS, 1, D)
    outs = [y_prompt, y_sample]
    for w in WINS:
        kp = np.stack([np.asarray(R[c][f"kv{w}p"], f32) for c in range(NCORES)], 1)
        ks = np.concatenate([np.asarray(R[c][f"kv{w}s"], f32) for c in range(NCORES)], 1)
        outs.append(kp.reshape(DEPTH, NCORES, w, 2, 8, 64))
        outs.append(ks.reshape(DEPTH, NCORES * NS, w, 2, 8, 64))
    rp = np.stack([np.asarray(R[c]["retp"], f32) for c in range(NCORES)], 1)
    rs_ = np.concatenate([np.asarray(R[c]["rets"], f32) for c in range(NCORES)], 1)
    outs.append(rp)
    outs.append(rs_)
    return tuple(outs)
```

```python
import math
from contextlib import ExitStack
from functools import partial

import numpy as np
import ml_dtypes

import concourse.bass as bass
import concourse.mybir as mybir
from concourse.bass_utils import run_bass_kernel_spmd

F32 = mybir.dt.float32
BF16 = mybir.dt.bfloat16
AF = mybir.ActivationFunctionType
ALU = mybir.AluOpType

S = 4096
D = 1024
DFF = 2816
NJ = DFF // 128
NIN = 9728
DEPTH = 2
NS = 4
TT = 1024
NTILE = S // TT
NCORES = 8
NORM_EPS = 1e-6
GN_EPS = 1e-6
DILS = (1, 4, 16)
WINS = (128, 512, 2048)
NHEAD_RET = 4

ENG_NAMES = ("pe", "act", "dve", "pool", "sp")
SAME_ENGINE_SYNC = True
ATTACH_WAITS = True
N_DMA_SEMS = 24


class Buf:
    __slots__ = ("name", "w", "r")

    def __init__(self, name=""):
        self.name = name
        self.w = None
        self.r = {}


class _Op:
    __slots__ = ("eng", "fn", "deps", "dma", "signal", "snap")

    def __init__(self, eng, fn):
        self.eng = eng
        self.fn = fn
        self.deps = []
        self.dma = None
        self.signal = False
        self.snap = None


class Tracer:
    def __init__(self, nc):
        self.nc = nc
        self.h = {"pe": nc.tensor, "act": nc.scalar, "dve": nc.vector, "pool": nc.gpsimd, "sp": nc.sync}
        self.ops = []
        self.eng_ops = {e: [] for e in ENG_NAMES}
        self.known = {e: {e2: -1 for e2 in ENG_NAMES} for e in ENG_NAMES}
        self.known_dma = {e: set() for e in ENG_NAMES}
        self.dmas = []
        self.dma_q_count = {e: 0 for e in ENG_NAMES}
        self.bar_dma_start = 0

    def _add_dep(self, op, tok, att=False):
        if tok is None:
            return
        eng = op.eng
        if tok[0] == "e":
            _, e2, n = tok
            if e2 == eng and (eng == "pe" or eng == "sp" or not SAME_ENGINE_SYNC):
                return
            if n <= self.known[eng][e2]:
                return
            op.deps.append((tok, att))
            src = self.eng_ops[e2][n]
            src.signal = True
            kn = self.known[eng]
            kn[e2] = n
            for e3, v in src.snap.items():
                if v > kn[e3]:
                    kn[e3] = v
        else:
            did = tok[1]
            if did in self.known_dma[eng]:
                return
            op.deps.append((tok, att))
            self.known_dma[eng].add(did)
            snap = self.dmas[did][2]
            kn = self.known[eng]
            for e3, v in snap.items():
                if v > kn[e3]:
                    kn[e3] = v

    def op(self, eng, fn, reads=(), writes=(), dma=False, extra=()):
        op = _Op(eng, fn)
        idx = len(self.eng_ops[eng])
        if dma:
            did = len(self.dmas)
            op.dma = did
            tok = ("d", did)
        else:
            tok = ("e", eng, idx)
        att_r = ATTACH_WAITS and (not dma) and eng in ("act", "dve", "pool")
        att_w = ATTACH_WAITS and (not dma) and eng in ("act", "dve", "pool", "pe")
        for t in extra:
            self._add_dep(op, t)
        for b in reads:
            self._add_dep(op, b.w, att_r)
        for b in writes:
            self._add_dep(op, b.w, att_w)
            for rt in b.r.values():
                self._add_dep(op, rt, att_w)
        for b in reads:
            if dma:
                b.r[tok] = tok
            else:
                b.r[eng] = tok
        for b in writes:
            b.w = tok
            b.r = {}
        op.snap = dict(self.known[eng])
        if dma:
            self.dmas.append((eng, self.dma_q_count[eng], dict(self.known[eng])))
            self.dma_q_count[eng] += 1
        self.eng_ops[eng].append(op)
        self.ops.append(op)
        return tok

    def barrier(self):
        latest = {}
        covered = []
        for i in range(self.bar_dma_start, len(self.dmas)):
            q, k, _ = self.dmas[i]
            if q == "pool":
                continue
            covered.append(i)
            key = (q, k % N_DMA_SEMS)
            if key not in latest or k > self.dmas[latest[key]][1]:
                latest[key] = i
        extra = [("d", i) for i in sorted(latest.values())]
        self.bar_dma_start = len(self.dmas)
        for e in ENG_NAMES:
            if e != "sp" and self.eng_ops[e]:
                extra.append(("e", e, len(self.eng_ops[e]) - 1))
        b = Buf("barrier")
        sp = self.h["sp"]
        self.op("sp", lambda: sp.nop(), writes=[b], extra=extra)
        self.known_dma["sp"].update(covered)
        for e in ENG_NAMES:
            if e != "sp":
                h = self.h[e]
                self.op(e, (lambda h=h: h.nop()), reads=[b])

    def emit(self):
        nc = self.nc
        with ExitStack() as es:
            esem = {e: es.enter_context(nc.semaphore("s_" + e)) for e in ENG_NAMES}
            dsem = {}
            for q in ENG_NAMES:
                if self.dma_q_count[q]:
                    dsem[q] = [es.enter_context(nc.semaphore(f"d_{q}{i}")) for i in range(N_DMA_SEMS)]
            cum = {}
            for e in ENG_NAMES:
                c = 0
                arr = []
                for o in self.eng_ops[e]:
                    if o.signal and o.dma is None:
                        c += 1
                    arr.append(c)
                cum[e] = arr
            nwait = 0
            def semval(tok):
                if tok[0] == "e":
                    return esem[tok[1]], cum[tok[1]][tok[2]]
                q, k, _ = self.dmas[tok[1]]
                return dsem[q][k % N_DMA_SEMS], 16 * (k // N_DMA_SEMS + 1)

            nattach = 0
            for o in self.ops:
                h = self.h[o.eng]
                attach = None
                if o.dma is None:
                    for i_ in range(len(o.deps) - 1, -1, -1):
                        if o.deps[i_][1]:
                            attach = i_
                            break
                for i_, (tok, att) in enumerate(o.deps):
                    if i_ == attach:
                        continue
                    sem_, val_ = semval(tok)
                    h.wait_ge(sem_, val_)
                    nwait += 1
                if attach is not None:
                    sem_, val_ = semval(o.deps[attach][0])
                    ins = o.fn()
                    ins._wait_ge(sem_, val_)
                    nattach += 1
                    if o.signal:
                        ins.then_inc(esem[o.eng], 1)
                    continue
                if o.dma is not None:
                    q, k, _ = self.dmas[o.dma]
                    sem = dsem[q][k % N_DMA_SEMS]
                    if k >= N_DMA_SEMS:
                        h.wait_ge(sem, 16 * (k // N_DMA_SEMS))
                    o.fn().then_inc(sem, 16)
                else:
                    ins = o.fn()
                    if o.signal:
                        ins.then_inc(esem[o.eng], 1)
            h = self.h["sp"]
            for q, sems in dsem.items():
                n = self.dma_q_count[q]
                for i, sem in enumerate(sems):
                    cnt = (n - i + N_DMA_SEMS - 1) // N_DMA_SEMS if n > i else 0
                    if cnt > 0:
                        h.wait_ge(sem, 16 * cnt)
            self.stats = dict(n_ops=len(self.ops), n_wait=nwait, n_attach=nattach,
                              per_eng={e: len(v) for e, v in self.eng_ops.items()}, n_dma=len(self.dmas))


def _gammas():
    return [1.0 - 2.0 ** (-5.0 - h) for h in range(NHEAD_RET)]


def _const_tables():
    bf = ml_dtypes.bfloat16
    eb = np.zeros((3, 4, 128, 512), np.float64)
    kk = np.arange(128)[:, None].astype(np.float64)
    qi = np.arange(128)[None, :].astype(np.float64)
    for g in range(3):
        for c in range(4):
            for hp in range(2):
                hglob = 8 * g + 2 * c + hp
                slope = 2.0 ** (-8.0 * (hglob + 1) / 24.0)
                cc = slope * DILS[g]
                rel_prev = qi + 128.0 - kk
                prev = np.where(rel_prev <= 128.0, np.exp(-cc * rel_prev), 0.0)
                rel_cur = qi - kk
                cur = np.where(rel_cur >= 0.0, np.exp(-cc * np.maximum(rel_cur, 0.0)), 0.0)
                eb[g, c, :, hp * 128:hp * 128 + 128] = prev
                eb[g, c, :, 256 + hp * 128:256 + hp * 128 + 128] = cur
    gam = _gammas()
    sc = 128.0 ** -0.5
    dmaskT = np.zeros((128, 4 * 128), np.float64)
    kdec = np.zeros((128, 4 * 128), np.float64)
    qdec = np.zeros((128, 4 * 128), np.float64)
    jj = np.arange(128)[:, None].astype(np.float64)
    ii = np.arange(128)[None, :].astype(np.float64)
    for h in range(4):
        dm = np.where(ii >= jj, gam[h] ** np.maximum(ii - jj, 0.0), 0.0) * sc
        dmaskT[:, h * 128:(h + 1) * 128] = dm
        kdec[:, h * 128:(h + 1) * 128] = (gam[h] ** (127.0 - jj)) * sc
        qdec[:, h * 128:(h + 1) * 128] = gam[h] ** (ii + 1.0)
    ident = np.eye(128)
    sbias = np.zeros((128, 24), np.float64)
    kidx = np.arange(128).astype(np.float64)
    for g in range(3):
        for h in range(8):
            slope = 2.0 ** (-8.0 * (8 * g + h + 1) / 24.0)
            sbias[:, g * 8 + h] = -slope * DILS[g] * (128.0 - kidx)
    sel = np.zeros((128, 4, 4), np.float64)
    for s_ in range(4):
        sel[:, s_, s_] = 1.0
    return dict(
        c_sbias=sbias.astype(np.float32),
        c_sel=sel.reshape(128, 16).astype(np.float32),
        c_eb=eb.reshape(12, 128, 512).astype(bf),
        c_dmask=dmaskT.astype(np.float32),
        c_kdec=kdec.astype(np.float32),
        c_qdec=qdec.astype(np.float32),
        c_ident=ident.astype(np.float32),
        c_identb=ident.astype(bf),
    )


class Arena:
    def __init__(self, t, nbytes):
        self.t = t
        self.nbytes = nbytes
        self.off = 0

    def alloc(self, free_shape, dtype):
        esz = 4 if dtype == F32 else 2
        nel = int(np.prod(free_shape))
        nb = nel * esz
        start = self.off
        self.off = start + ((nb + 63) // 64) * 64
        assert self.off <= self.nbytes, f"SBUF arena overflow {self.off} > {self.nbytes}"
        self.last = start
        return self.view(start, free_shape, dtype)

    def view(self, start, free_shape, dtype):
        esz = 4 if dtype == F32 else 2
        nb = int(np.prod(free_shape)) * esz
        v = self.t[:, start // 2:(start + nb) // 2]
        if dtype == F32:
            v = v.bitcast(F32)
        if len(free_shape) == 2:
            v = v.rearrange("p (a b) -> p a b", b=free_shape[1])
        elif len(free_shape) == 3:
            v = v.rearrange("p (a b c) -> p a b c", b=free_shape[1], c=free_shape[2])
        return v


class Rot:
    def __init__(self, items):
        self.items = list(items)
        self.i = 0

    def next(self):
        it = self.items[self.i % len(self.items)]
        self.i += 1
        return it


def build_program(debug=False, stop_after=None, only=None, cut=99):
    nc = bass.Bass("TRN2", target_bir_lowering=False)
    T = Tracer(nc)
    es = ExitStack()

    def din(name, shape, dt=F32):
        return nc.dram_tensor(name, list(shape), dt, kind="ExternalInput").ap()

    def dout(name, shape, dt=F32):
        return nc.dram_tensor(name, list(shape), dt, kind="ExternalOutput").ap()

    def dscr(name, shape, dt):
        kind = "ExternalOutput" if debug else "Internal"
        if only is not None and name in ("QK", "VX", "QRT", "KRT", "KRK", "VR", "GT"):
            kind = "ExternalInput"
        return nc.dram_tensor(name, list(shape), dt, kind=kind).ap()

    xp = din("xp", [S, D])
    xs = din("xs", [NS, D])
    cache_in = [din(f"c{w}", [DEPTH, NS, w, 1024]) for w in WINS]
    sret = din("sret", [DEPTH, NS, 4, 128, 256])
    gains_d = din("gains", [128, 72])
    w_g = [din("ffn1_wg", [DEPTH, D, DFF]), din("ffn2_wg", [DEPTH, D, DFF])]
    w_u = [din("ffn1_wu", [DEPTH, D, DFF]), din("ffn2_wu", [DEPTH, D, DFF])]
    w_d = [din("ffn1_wd", [DEPTH, DFF, D]), din("ffn2_wd", [DEPTH, DFF, D])]
    w_in = din("w_in", [DEPTH, D, NIN])
    w_a = din("w_br_a", [DEPTH, 512, D])
    w_b = din("w_br_b", [DEPTH, D, D])
    w_o = din("w_out", [DEPTH, D, D])
    c_eb = din("c_eb", [12, 128, 512], BF16)
    c_dmask = din("c_dmask", [128, 512])
    c_kdec = din("c_kdec", [128, 512])
    c_qdec = din("c_qdec", [128, 512])
    c_ident = din("c_ident", [128, 128])
    c_identb = din("c_identb", [128, 128], BF16)
    c_sbias = din("c_sbias", [128, 24])
    c_sel = din("c_sel", [128, 16])

    yp = dout("yp", [S, D])
    ys = dout("ys", [NS, D])
    kvp = [dout(f"kv{w}p", [DEPTH, w, 1024]) for w in WINS]
    kvs = [dout(f"kv{w}s", [DEPTH, NS, w, 1024]) for w in WINS]
    retp = dout("retp", [DEPTH, 4, 128, 256])
    rets = dout("rets", [DEPTH, NS, 4, 128, 256])

    XRES = dscr("XRES", [8, 128, S], F32)
    QK = dscr("QK", [2, 3, 4, 128, S], BF16)
    VX = dscr("VX", [3, 4, S, 256], BF16)
    QRT = dscr("QRT", [4, 128, S], BF16)
    KRT = dscr("KRT", [4, 128, S], BF16)
    KRK = dscr("KRK", [S, 512], BF16)
    VR = dscr("VR", [S, 1024], BF16)
    GT = dscr("GT", [24, 128, S], BF16)
    OAT = dscr("OAT", [4, 128, S], BF16)
    YRT = dscr("YRT", [8, 128, S], BF16)
    STOK = dscr("STOK", [NS, 6656], F32)

    ARENA_BYTES = 206 * 1024
    arena_t = es.enter_context(nc.sbuf_tensor("arena", [128, ARENA_BYTES // 2], BF16))
    AR = Arena(arena_t, ARENA_BYTES)
    psw = [es.enter_context(nc.psum_tensor(f"psw{i}", [128, 1024], F32)) for i in range(4)]
    psb = []
    for i in range(4):
        psb.append(psw[i][:, 0:512])
        psb.append(psw[i][:, 512:1024])
    bps = [Buf(f"ps{i}") for i in range(8)]

    ws = [AR.alloc([8, 512], BF16) for _ in range(4)]
    bws = [Buf(f"ws{i}") for i in range(4)]
    wds = [AR.alloc([NJ, 256], BF16) for _ in range(2)]
    bwds = [Buf(f"wd{i}") for i in range(2)]
    ident = AR.alloc([128], F32)
    identb = AR.alloc([128], BF16)
    ones_b = AR.alloc([128], BF16)
    gains = AR.alloc([72], F32)
    gains32 = AR.alloc([72], F32)
    kdec = AR.alloc([512], F32)
    bconst = Buf("const")
    xsT = AR.alloc([8, NS], F32)
    hsT = AR.alloc([8, NS], BF16)
    actS = AR.alloc([NJ, NS], BF16)
    rstdS = AR.alloc([NS], F32)
    qrS = AR.alloc([4, NS], F32)
    gS = AR.alloc([24, NS], BF16)
    bxs = [Buf() for _ in range(8)]
    bhs = [Buf() for _ in range(8)]
    bas = [Buf() for _ in range(NJ)]
    brss = Buf()
    bqrS = Buf()
    bgS = Buf()
    PERSIST_END = AR.off

    sp, act, dve, pe, pool = nc.sync, nc.scalar, nc.vector, nc.tensor, nc.gpsimd

    def dma(q, out, in_, reads=(), writes=()):
        h = T.h[q]
        return T.op(q, (lambda: h.dma_start(out=out, in_=in_)), reads=reads, writes=writes, dma=True)

    def mm(out, lhsT, rhs, start, stop, reads, writes):
        return T.op("pe", (lambda: pe.matmul(out, lhsT, rhs, start=start, stop=stop)), reads=reads, writes=writes)

    def tr(out, in_, idn, reads, writes):
        return T.op("pe", (lambda: pe.transpose(out, in_, idn)), reads=reads, writes=writes)

    def actf(out, in_, func, reads, writes, scale=1.0, bias=None):
        if bias is None:
            return T.op("act", (lambda: act.activation(out=out, in_=in_, func=func, scale=scale)),
                        reads=reads, writes=writes)
        return T.op("act", (lambda: act.activation(out=out, in_=in_, func=func, scale=scale, bias=bias)),
                    reads=reads, writes=writes)

    def v_recip(out, in_, reads, writes):
        return T.op("dve", (lambda: dve.reciprocal(out=out, in_=in_)), reads=reads, writes=writes)

    def v_copy(eng, out, in_, reads, writes):
        if eng == "act":
            return T.op("act", (lambda: act.copy(out=out, in_=in_)), reads=reads, writes=writes)
        h = T.h[eng]
        return T.op(eng, (lambda: h.tensor_copy(out=out, in_=in_)), reads=reads, writes=writes)

    def v_tt(out, in0, in1, op, reads, writes, eng="dve"):
        h = T.h[eng]
        return T.op(eng, (lambda: h.tensor_tensor(out=out, in0=in0, in1=in1, op=op)), reads=reads, writes=writes)

    def v_ts(out, in0, s1, s2, op0, op1, reads, writes, eng="dve"):
        h = T.h[eng]
        if s2 is None:
            return T.op(eng, (lambda: h.tensor_scalar(out=out, in0=in0, scalar1=s1, scalar2=None, op0=op0)),
                        reads=reads, writes=writes)
        return T.op(eng, (lambda: h.tensor_scalar(out=out, in0=in0, scalar1=s1, scalar2=s2, op0=op0, op1=op1)),
                    reads=reads, writes=writes)

    def v_stt(out, in0, scalar, in1, op0, op1, reads, writes, eng="dve"):
        h = T.h[eng]
        return T.op(eng, (lambda: h.scalar_tensor_tensor(out=out, in0=in0, scalar=scalar, in1=in1, op0=op0, op1=op1)),
                    reads=reads, writes=writes)

    dma("sp", ident, c_ident, writes=[bconst])
    dma("sp", identb, c_identb, writes=[bconst])
    dma("sp", gains, gains_d, writes=[bconst])
    dma("sp", kdec, c_kdec, writes=[bconst])
    T.op("dve", lambda: dve.memset(ones_b, 1.0), writes=[bconst])
    v_ts(gains32, gains, 1.0, None, ALU.mult, None, reads=[bconst], writes=[bconst])

    def load_sample_x():
        AR.off = PERSIST_END
        xs_sb = AR.alloc([D], F32)
        bxs_sb = Buf()
        dma("sp", xs_sb[0:NS, :], xs, writes=[bxs_sb])
        for c in range(8):
            tr(psb[0][:, c * NS:(c + 1) * NS], xs_sb[0:NS, c * 128:(c + 1) * 128], ident[0:NS, 0:NS],
               reads=[bxs_sb, bconst], writes=[bps[0]])
        v_copy("dve", xsT[:, :, :], psb[0][:, 0:8 * NS].rearrange("p (c s) -> p c s", s=NS), reads=[],
               writes=[bps[0]] + bxs)

    cache_pieces = []
    for l_ in range(DEPTH):
        for s_ in range(NS):
            for g_, wb_ in enumerate(WINS):
                npiece = 4 if wb_ == 2048 else 1
                rows = wb_ - 1
                step = (rows + npiece - 1) // npiece
                for r0_ in range(0, rows, step):
                    r1_ = min(rows, r0_ + step)
                    cache_pieces.append((kvs[g_][l_, s_, r0_:r1_, :].rearrange("r f -> (r f)"),
                                         cache_in[g_][l_, s_, r0_ + 1:r1_ + 1, :].rearrange("r f -> (r f)")))

    def copy_cache_piece():
        if cache_pieces and only is None:
            dst, src = cache_pieces.pop(0)
            dma("act", dst, src)

    CG = [(i * 512, 512) for i in range(5)] + [(2560, 256)]

    class WStream:
        def __init__(self, slots, bufs, prefetch):
            self.slots, self.bufs, self.prefetch = slots, bufs, prefetch
            self.items = []
            self.issued = 0
            self.used = 0

        def add(self, src_ap, kc, ncols):
            self.items.append((src_ap, kc, ncols))

        def _issue(self, i):
            src, kc, ncols = self.items[i]
            s = i % len(self.slots)
            dma("pool", self.slots[s][:, 0:kc, 0:ncols], src, writes=[self.bufs[s]])

        def get(self):
            i = self.used
            self.used += 1
            lim = min(len(self.items), i + self.prefetch + 1)
            while self.issued < lim:
                self._issue(self.issued)
                self.issued += 1
            s = i % len(self.slots)
            return self.slots[s], self.bufs[s]

    WS1 = WStream(ws, bws, 2)
    WS2 = WStream(wds, bwds, 1)

    def plan_ffn(which, l, ntiles):
        for _ in range(ntiles):
            wgv = w_g[which][l].rearrange("(k p) n -> p k n", p=128)
            wuv = w_u[which][l].rearrange("(k p) n -> p k n", p=128)
            for (c0, n) in CG:
                WS1.add(wgv[:, :, c0:c0 + n], 8, n)
                WS1.add(wuv[:, :, c0:c0 + n], 8, n)
            wdv = w_d[which][l].rearrange("(j p) n -> p j n", p=128)
            for m2 in range(4):
                WS2.add(wdv[:, :, m2 * 256:(m2 + 1) * 256], NJ, 256)

    def plan_phaseA_tile(l):
        plan_ffn(0, l, 1)
        wv = w_in[l].rearrange("(k p) n -> p k n", p=128)
        for G in range(19):
            WS1.add(wv[:, :, G * 512:(G + 1) * 512], 8, 512)

    def plan_phaseC_tile(l):
        wav = w_a[l].rearrange("(k p) n -> p k n", p=128)
        wbv = w_b[l].rearrange("(k p) n -> p k n", p=128)
        wov = w_o[l].rearrange("(k p) n -> p k n", p=128)
        for hf in range(2):
            WS1.add(wav[:, :, hf * 512:(hf + 1) * 512], 4, 512)
            WS1.add(wbv[:, :, hf * 512:(hf + 1) * 512], 8, 512)
        for hf in range(2):
            WS1.add(wov[:, :, hf * 512:(hf + 1) * 512], 8, 512)
        plan_ffn(1, l, 1)

    for l in range(DEPTH):
        for t in range(NTILE):
            plan_phaseA_tile(l)
        for t in range(NTILE):
            plan_phaseC_tile(l)

    class PsRot:
        def __init__(self, idxs):
            self.idxs = list(idxs)
            self.i = 0

        def next(self):
            k = self.idxs[self.i % len(self.idxs)]
            self.i += 1
            return psb[k], bps[k]

    class Seg:
        pass

    def make_segs(xT, hT, actT, rstd, bx, bh, bact, brs):
        segs = []
        for s in range(2):
            sg = Seg()
            sg.n = 512
            sg.s = s
            sl = slice(s * 512, (s + 1) * 512)
            sg.x = (lambda c, sl=sl: xT[:, c, sl])
            sg.h = (lambda c, sl=sl: hT[:, c, sl])
            sg.a = (lambda j, sl=sl: actT[:, j, sl])
            sg.rstd = rstd[:, sl]
            sg.bx = [bx[c][s] for c in range(8)]
            sg.bh = [bh[c][s] for c in range(8)]
            sg.ba = [bact[j][s] for j in range(NJ)]
            sg.brs = brs[s]
            segs.append(sg)
        return segs

    sseg = Seg()
    sseg.n = NS
    sseg.s = None
    sseg.x = (lambda c: xsT[:, c, :])
    sseg.h = (lambda c: hsT[:, c, :])
    sseg.a = (lambda j: actS[:, j, :])
    sseg.rstd = rstdS
    sseg.bx, sseg.bh, sseg.ba, sseg.brs = bxs, bhs, bas, brss

    def rmsnorm(segs, gcol, psr):
        for sg in segs:
            n = sg.n
            for c in range(8):
                actf(sg.a(c), sg.x(c), AF.Square, reads=[sg.bx[c]], writes=[sg.ba[c]])
            pst, pbf = psr.next()
            for c in range(8):
                mm(pst[:, :n], ones_b, sg.a(c), c == 0, c == 7, reads=[sg.ba[c], bconst], writes=[pbf])
            actf(sg.rstd, pst[:, :n], AF.Sqrt, reads=[], writes=[pbf, sg.brs], scale=1.0 / 1024.0, bias=NORM_EPS)
            v_recip(sg.rstd, sg.rstd, reads=[], writes=[sg.brs])
            for c in range(8):
                v_stt(sg.h(c), sg.x(c), gains32[:, gcol + c:gcol + c + 1], sg.rstd, ALU.mult, ALU.mult,
                      reads=[sg.bx[c], sg.brs, bconst], writes=[sg.bh[c]])

    def ffn(segs, tmp_of, btmp_of):
        psr = PsRot(range(6))
        ti = 0
        for (c0, ncol) in CG:
            copy_cache_piece()
            wg_s, bwg = WS1.get()
            wu_s, bwu = WS1.get()
            for jj in range(ncol // 128):
                j = c0 // 128 + jj
                for sg in segs:
                    n = sg.n
                    pg, bpg = psr.next()
                    pu, bpu = psr.next()
                    for k in range(8):
                        mm(pg[:, :n], wg_s[:, k, jj * 128:(jj + 1) * 128], sg.h(k), k == 0, k == 7,
                           reads=[bwg, sg.bh[k]], writes=[bpg])
                        mm(pu[:, :n], wu_s[:, k, jj * 128:(jj + 1) * 128], sg.h(k), k == 0, k == 7,
                           reads=[bwu, sg.bh[k]], writes=[bpu])
                    tmp, btmp = tmp_of(ti), btmp_of(ti)
                    ti += 1
                    actf(tmp[:, :n], pg[:, :n], AF.Silu, reads=[], writes=[bpg, btmp])
                    v_tt(sg.a(j), tmp[:, :n], pu[:, :n], ALU.mult, reads=[btmp], writes=[bpu, sg.ba[j]])
        for m2 in range(4):
            wd_s, bwd = WS2.get()
            for mm_ in range(2):
                m = m2 * 2 + mm_
                for sg in segs:
                    n = sg.n
                    py, bpy = psr.next()
                    for j in range(NJ):
                        mm(py[:, :n], wd_s[:, j, mm_ * 128:(mm_ + 1) * 128], sg.a(j), j == 0, j == NJ - 1,
                           reads=[bwd, sg.ba[j]], writes=[bpy])
                    v_stt(sg.x(m), py[:, :n], 0.5, sg.x(m), ALU.mult, ALU.add, reads=[], writes=[bpy, sg.bx[m]])

    def phase_A(l):
        AR.off = PERSIST_END
        xT = AR.alloc([8, TT], F32)
        hT = AR.alloc([8, TT], BF16)
        actT = AR.alloc([NJ, TT], BF16)
        act_off = AR.last

        def act_f32(j):
            return AR.view(act_off + j * 2048, [512], F32)
        rstd = AR.alloc([TT], F32)
        tmps = [AR.alloc([512], BF16) for _ in range(3)]
        ust = AR.alloc([8, 2048], BF16)
        vxst = [AR.alloc([8, 128], BF16) for _ in range(4)]
        stokst = [AR.alloc([512], F32) for _ in range(2)]
        bstok = [Buf(), Buf()]
        bx = [[Buf() for _ in range(2)] for _ in range(8)]
        bh = [[Buf() for _ in range(2)] for _ in range(8)]
        bact = [[Buf() for _ in range(2)] for _ in range(NJ)]
        brs = [Buf(), Buf()]
        btmps = [Buf() for _ in range(3)]
        bust = [Buf() for _ in range(8)]
        bvxst = [Buf() for _ in range(4)]
        segs = make_segs(xT, hT, actT, rstd, bx, bh, bact, brs)
        for i in range(4):
            T.op("dve", (lambda i=i: dve.memset(vxst[i], 1.0)), writes=[bvxst[i]])

        def chunk_bufs(j):
            return [bact[j][0], bact[j][1]]

        fm_rot = Rot([8, 9, 10, 11, 12, 13])
        vr_rot = Rot([14, 15, 16])
        krk_rot = Rot([(17, 0), (17, 1)])
        f32_rot = Rot([18, 19, 20, 21])
        ev_rot = Rot(["act", "dve"])

        for t in range(NTILE):
            t0 = t * TT
            if l == 0:
                xin = AR.view(act_off, [4, D], F32)
                psr = PsRot(range(8))
                for half in range(2):
                    for tb in range(4):
                        r0 = t0 + half * 512 + tb * 128
                        dma("sp", xin[:, tb, :], xp[r0:r0 + 128, :], writes=chunk_bufs(2 * tb) + chunk_bufs(2 * tb + 1))
                    for c in range(8):
                        pst, pbf = psr.next()
                        for tb in range(4):
                            tr(pst[:, tb * 128:(tb + 1) * 128], xin[:, tb, c * 128:(c + 1) * 128], ident,
                               reads=chunk_bufs(2 * tb) + chunk_bufs(2 * tb + 1) + [bconst], writes=[pbf])
                        v_copy(ev_rot.next(), xT[:, c, half * 512:(half + 1) * 512], pst[:, :], reads=[],
                               writes=[pbf, bx[c][half]])
            elif t == 0:
                dma("sp", xT[:, :, :], XRES.rearrange("c p t -> p c t")[:, :, t0:t0 + TT],
                    writes=[bx[c][s] for c in range(8) for s in range(2)])

            last = (t == NTILE - 1)
            segs_t = segs + ([sseg] if last else [])
            rmsnorm(segs_t, l * 32 + 0, PsRot([6, 7]))
            ffn(segs_t, lambda i: tmps[i % 3], lambda i: btmps[i % 3])
            rmsnorm(segs_t, l * 32 + 8, PsRot([6, 7]))
            dma("sp", XRES.rearrange("c p t -> p c t")[:, :, t0:t0 + TT], xT[:, :, :],
                reads=[bx[c][s] for c in range(8) for s in range(2)])
            if l > 0 and t + 1 < NTILE:
                dma("sp", xT[:, :, :], XRES.rearrange("c p t -> p c t")[:, :, t0 + TT:t0 + 2 * TT],
                    writes=[bx[c][s] for c in range(8) for s in range(2)])

            psr = PsRot(range(8))

            def proj_fm(slot, bslot, cc, sg):
                pst, pbf = psr.next()
                for k in range(8):
                    mm(pst[:, :sg.n], slot[:, k, cc * 128:(cc + 1) * 128], sg.h(k), k == 0, k == 7,
                       reads=[bslot, sg.bh[k]], writes=[pbf])
                return pst, pbf

            def proj_tm(slot, bslot, tb):
                pst, pbf = psr.next()
                s = tb // 4
                for k in range(8):
                    mm(pst[:, :], hT[:, k, tb * 128:(tb + 1) * 128], slot[:, k, :], k == 0, k == 7,
                       reads=[bslot, bh[k][s]], writes=[pbf])
                return pst, pbf

            def stage_fm(j):
                return actT[:, j, :], chunk_bufs(j)

            def sample_proj(G, slot, bslot):
                if G <= 12:
                    pst, pbf = psr.next()
                    for k in range(8):
                        mm(pst[0:NS, :], hsT[:, k, :], slot[:, k, :], k == 0, k == 7, reads=[bslot, bhs[k]], writes=[pbf])
                    si = G % 2
                    v_copy(ev_rot.next(), stokst[si][0:NS, :], pst[0:NS, :], reads=[], writes=[pbf, bstok[si]])
                    dma("sp", STOK[:, G * 512:(G + 1) * 512], stokst[si][0:NS, :], reads=[bstok[si]])
                    if 3 <= G < 9:
                        g_ = (G - 3) % 3
                        half = 0 if G < 6 else 1
                        wb_ = WINS[g_]
                        dma("sp", kvs[g_][l, :, wb_ - 1, half * 512:(half + 1) * 512], stokst[si][0:NS, :],
                            reads=[bstok[si]])
                if G == 9:
                    for cc in range(4):
                        pst, pbf = proj_fm(slot, bslot, cc, sseg)
                        v_copy(ev_rot.next(), qrS[:, cc, :], pst[:, 0:NS], reads=[], writes=[pbf, bqrS])
                if G >= 13:
                    func = AF.Silu if G < 15 else AF.Sigmoid
                    for cc in range(4):
                        pst, pbf = proj_fm(slot, bslot, cc, sseg)
                        actf(gS[:, (G - 13) * 4 + cc, :], pst[:, 0:NS], func, reads=[], writes=[pbf, bgS])

            for G in range(19):
                slot, bslot = WS1.get()
                if last:
                    sample_proj(G, slot, bslot)
                if G < 6:
                    which, g = (0, G) if G < 3 else (1, G - 3)
                    for cc in range(4):
                        if g < 2:
                            j = fm_rot.next()
                            st, bst = stage_fm(j)
                        for sg in segs:
                            pst, pbf = proj_fm(slot, bslot, cc, sg)
                            eng = ev_rot.next()
                            if g == 0:
                                v_copy(eng, st[:, sg.s * 512:(sg.s + 1) * 512], pst[:, :], reads=[], writes=[pbf, bst[sg.s]])
                            elif g == 1:
                                v_copy(eng, st[:, sg.s * 512:(sg.s + 1) * 512].rearrange("p (r i) -> p r i", r=4),
                                       pst[:, :].rearrange("p (i r) -> p r i", r=4), reads=[], writes=[pbf, bst[sg.s]])
                            else:
                                i0 = ((t0 + sg.s * 512) % 2048) // 16
                                v_copy(eng, ust[:, which * 4 + cc, :].rearrange("p (r i) -> p r i", r=16)[:, :, i0:i0 + 32],
                                       pst[:, :].rearrange("p (i r) -> p r i", r=16), reads=[],
                                       writes=[pbf, bust[which * 4 + cc]])
                        if g < 2:
                            dma("sp", QK[which, g, cc, :, t0:t0 + TT], st, reads=bst)
                        elif t % 2 == 1:
                            u0 = (t // 2) * 2048
                            dma("sp", QK[which, 2, cc, :, u0:u0 + 2048], ust[:, which * 4 + cc, :],
                                reads=[bust[which * 4 + cc]])
                    if which == 1:
                        win = WINS[g]
                        for tb in range(8):
                            r0 = t0 + tb * 128
                            if r0 >= S - win:
                                pst, pbf = proj_tm(slot, bslot, tb)
                                j = f32_rot.next()
                                st = act_f32(j)
                                v_copy(ev_rot.next(), st, pst[:, :], reads=[], writes=[pbf] + chunk_bufs(j))
                                o0 = r0 - (S - win)
                                dma("sp", kvp[g][l, o0:o0 + 128, 0:512], st, reads=chunk_bufs(j))
                elif G < 9:
                    g = G - 6
                    win = WINS[g]
                    for tb in range(8):
                        r0 = t0 + tb * 128
                        pst, pbf = proj_tm(slot, bslot, tb)
                        vi = (tb + 8 * g) % 4
                        dst = vxst[vi].rearrange("p (m hp) e -> p m hp e", hp=2)
                        src = pst[:, :].rearrange("p (m hp e) -> p m hp e", hp=2, e=64)
                        v_copy(ev_rot.next(), dst[:, :, 0, 0:64], src[:, :, 0, :], reads=[], writes=[pbf, bvxst[vi]])
                        v_copy(ev_rot.next(), dst[:, :, 1, 64:128], src[:, :, 1, :], reads=[], writes=[pbf, bvxst[vi]])
                        dma("sp", VX[g, :, r0:r0 + 128, :].rearrange("c t f -> t c f"),
                            vxst[vi].rearrange("p (c h) e -> p c (h e)", h=2), reads=[bvxst[vi]])
                        if r0 >= S - win:
                            j = f32_rot.next()
                            st = act_f32(j)
                            v_copy(ev_rot.next(), st, pst[:, :], reads=[], writes=[pbf] + chunk_bufs(j))
                            o0 = r0 - (S - win)
                            dma("sp", kvp[g][l, o0:o0 + 128, 512:1024], st, reads=chunk_bufs(j))
                elif G in (9, 10):
                    dstT = QRT if G == 9 else KRT
                    for cc in range(4):
                        j = fm_rot.next()
                        st, bst = stage_fm(j)
                        for sg in segs:
                            pst, pbf = proj_fm(slot, bslot, cc, sg)
                            v_copy(ev_rot.next(), st[:, sg.s * 512:(sg.s + 1) * 512], pst[:, :], reads=[],
                                   writes=[pbf, bst[sg.s]])
                        dma("sp", dstT[cc, :, t0:t0 + TT], st, reads=bst)
                    if G == 10:
                        for tb in range(8):
                            r0 = t0 + tb * 128
                            pst, pbf = proj_tm(slot, bslot, tb)
                            j, hf = krk_rot.next()
                            st = actT[:, j, hf * 512:(hf + 1) * 512]
                            v_tt(st, pst[:, :], kdec, ALU.mult, reads=[bconst], writes=[pbf, bact[j][hf]])
                            dma("sp", KRK[r0:r0 + 128, :], st, reads=[bact[j][hf]])
                elif G < 13:
                    hf = G - 11
                    for tb in range(8):
                        r0 = t0 + tb * 128
                        pst, pbf = proj_tm(slot, bslot, tb)
                        j, h2 = krk_rot.next()
                        st = actT[:, j, h2 * 512:(h2 + 1) * 512]
                        v_copy(ev_rot.next(), st, pst[:, :], reads=[], writes=[pbf, bact[j][h2]])
                        dma("sp", VR[r0:r0 + 128, hf * 512:(hf + 1) * 512], st, reads=[bact[j][h2]])
                else:
                    gi = (G - 13) * 4
                    func = AF.Silu if G < 15 else AF.Sigmoid
                    for cc in range(4):
                        j = fm_rot.next()
                        st, bst = stage_fm(j)
                        for sg in segs:
                            pst, pbf = proj_fm(slot, bslot, cc, sg)
                            actf(st[:, sg.s * 512:(sg.s + 1) * 512], pst[:, :], func, reads=[], writes=[pbf, bst[sg.s]])
                        dma("sp", GT[gi + cc, :, t0:t0 + TT], st, reads=bst)
        T.barrier()

    def phase_B_attn(l):
        AR.off = PERSIST_END
        NE = 4
        PIPE = 2
        eb = AR.alloc([12, 512], BF16)
        SAB2 = [AR.alloc([2, 2048], F32) for _ in range(2)]
        qbd = [AR.alloc([16, 256], BF16) for _ in range(2)]
        kbuf = [AR.alloc([4096], BF16) for _ in range(2)]
        vbuf = [AR.alloc([32, 256], BF16) for _ in range(2)]
        ebuf = [AR.alloc([512], BF16) for _ in range(NE)]
        pbuf = [AR.alloc([512], BF16) for _ in range(NE)]
        oast = [AR.alloc([2048], BF16) for _ in range(2)]
        rd = AR.alloc([2048], F32)
        beb = Buf()
        bS2 = [Buf(), Buf()]
        bq = [Buf(), Buf()]
        bk = [Buf(), Buf()]
        bv = [[Buf() for _ in range(32)] for _ in range(2)]
        be = [Buf() for _ in range(NE)]
        bp = [Buf() for _ in range(NE)]
        bpp = [Buf() for _ in range(NE)]
        boa = [Buf(), Buf()]
        brd = Buf()
        dma("sp", eb, c_eb.rearrange("i p n -> p i n"), writes=[beb])
        ps_s = PsRot([0, 1, 2, 3])
        ps_o = PsRot([4, 5, 6, 7])
        for i_ in range(2):
            T.op("dve", (lambda i_=i_: dve.memset(qbd[i_], 0.0)), writes=[bq[i_]])
        groups = [(u, c, g) for u in range(2) for c in range(4) for g in range(3)]
        NG = len(groups)
        iters = [(gi, bl) for gi in range(NG) for bl in range(16)]
        NI = len(iters)

        def ginfo(gi):
            u, c, g = groups[gi]
            dil = DILS[g]
            nprev = dil if u == 1 else 0
            return u, c, g, dil, 128 * dil, 16 * u - nprev, 16 + nprev, gi % 2

        def load_group(gi):
            u, c, g, dil, unit_g, wstart, nblk, bi = ginfo(gi)
            qsrc = QK[0, g, c, :, u * 2048:(u + 1) * 2048].rearrange("p (b q) -> p b q", q=128)
            dma("sp", qbd[bi][0:64, :, 0:128], qsrc[0:64], writes=[bq[bi]])
            dma("sp", qbd[bi][64:128, :, 128:256], qsrc[64:128], writes=[bq[bi]])
            dma("sp", kbuf[bi][:, 0:nblk * 128], QK[1, g, c, :, wstart * 128:(wstart + nblk) * 128], writes=[bk[bi]])
            for n_ in range(wstart // dil, (wstart + nblk) // dil):
                sl0 = n_ * dil - wstart
                src = VX[g, c, n_ * unit_g:(n_ + 1) * unit_g, :].rearrange("(p r) f -> p r f", r=dil)
                dma("sp", vbuf[bi][:, sl0:sl0 + dil, :], src, writes=[bv[bi][sl] for sl in range(sl0, sl0 + dil)])

        st1 = {}

        def stage1(k):
            gi, bl = iters[k]
            u, c, g, dil, unit_g, wstart, nblk, bi = ginfo(gi)
            b = 16 * u + bl
            has_prev = (b - dil) >= 0
            sc = b - wstart
            spv = b - dil - wstart
            pss, bpss = ps_s.next()
            if has_prev:
                mm(pss[:, 0:256], kbuf[bi][:, spv * 128:(spv + 1) * 128], qbd[bi][:, bl, :], True, True,
                   reads=[bk[bi], bq[bi]], writes=[bpss])
            mm(pss[:, 256:512], kbuf[bi][:, sc * 128:(sc + 1) * 128], qbd[bi][:, bl, :], True, True,
               reads=[bk[bi], bq[bi]], writes=[bpss])
            e_i = k % NE
            ebt, pbt = ebuf[e_i], pbuf[e_i]
            ebg = eb[:, g * 4 + c, :]
            c0_ = 0 if has_prev else 256
            actf(ebt[:, c0_:512], pss[:, c0_:512], AF.Exp, reads=[], writes=[bpss, be[e_i]], scale=0.125)
            if has_prev:
                v_tt(pbt[:, 0:256], ebt[:, 0:256], ebg[:, 0:256], ALU.mult, reads=[be[e_i], beb], writes=[bpp[e_i]],
                     eng="pool")
            v_tt(pbt[:, 256:512], ebt[:, 256:512], ebg[:, 256:512], ALU.mult, reads=[be[e_i], beb], writes=[bp[e_i]])

        def stage2(k):
            gi, bl = iters[k]
            u, c, g, dil, unit_g, wstart, nblk, bi = ginfo(gi)
            b = 16 * u + bl
            has_prev = (b - dil) >= 0
            sc = b - wstart
            spv = b - dil - wstart
            e_i = k % NE
            pbt = pbuf[e_i]
            si = (u * 4 + c) % 2
            SAB, bS = SAB2[si], bS2[si]
            pso, bpso = ps_o.next()
            for hp in range(2):
                oc = slice(hp * 128, (hp + 1) * 128)
                if has_prev:
                    mm(pso[:, oc], vbuf[bi][:, spv, oc], pbt[:, hp * 128:hp * 128 + 128], True, False,
                       reads=[bv[bi][spv], bpp[e_i]], writes=[bpso])
                mm(pso[:, oc], vbuf[bi][:, sc, oc], pbt[:, 256 + hp * 128:256 + hp * 128 + 128], not has_prev, True,
                   reads=[bv[bi][sc], bp[e_i]], writes=[bpso])
            nl_, r_ = bl // dil, bl % dil
            off = nl_ * unit_g + r_
            dst = SAB[:, :, off:off + 127 * dil + 1:dil]
            src = pso[:, 0:256].rearrange("p (a q) -> p a q", a=2)
            if g == 0:
                v_copy("act", dst, src, reads=[], writes=[bpso, bS])
            else:
                v_tt(dst, dst, src, ALU.add, reads=[], writes=[bpso, bS])
            if g == 2 and bl == 15:
                oi = si

                def mk(kind, cs_):
                    def f():
                        if kind == 0:
                            actf(rd[0:64, cs_], SAB[64:128, 0, cs_], AF.Ln, reads=[bS], writes=[brd])
                        elif kind == 1:
                            actf(rd[64:128, cs_], SAB[0:64, 1, cs_], AF.Ln, reads=[bS], writes=[brd])
                        elif kind == 2:
                            actf(rd[:, cs_], rd[:, cs_], AF.Exp, reads=[], writes=[brd], scale=-1.0)
                        elif kind == 3:
                            v_tt(oast[oi][0:64, cs_], SAB[0:64, 0, cs_], rd[0:64, cs_], ALU.mult, reads=[bS, brd],
                                 writes=[boa[oi]])
                        elif kind == 4:
                            v_tt(oast[oi][64:128, cs_], SAB[64:128, 1, cs_], rd[64:128, cs_], ALU.mult, reads=[bS, brd],
                                 writes=[boa[oi]])
                    return f
                for q4 in range(4):
                    cs_ = slice(q4 * 512, (q4 + 1) * 512)
                    for kind in range(5):
                        pending.append(mk(kind, cs_))
                pending.append(lambda: dma("sp", OAT[c, :, u * 2048:(u + 1) * 2048], oast[oi], reads=[boa[oi]]))

        pending = []
        load_group(0)
        load_group(1)
        for k in range(NI + PIPE):
            if k < NI:
                gi, bl = iters[k]
                if bl == PIPE and gi >= 1 and gi + 1 < NG:
                    load_group(gi + 1)
                stage1(k)
            if k - PIPE >= 0:
                stage2(k - PIPE)
            if pending:
                pending.pop(0)()
        while pending:
            pending.pop(0)()
        T.barrier()

    def phase_B_ret(l):
        AR.off = PERSIST_END
        dmask = AR.alloc([512], F32)
        qdec = AR.alloc([512], F32)
        state_f = AR.alloc([4, 256], F32)
        state_b = AR.alloc([4, 256], BF16)
        TG = 512
        NCH = TG // 128
        qrg = [AR.alloc([4, TG], BF16) for _ in range(2)]
        krg = [AR.alloc([4, TG], BF16) for _ in range(2)]
        krk = [AR.alloc([NCH, 512], BF16) for _ in range(2)]
        vrg = [AR.alloc([NCH, 1024], BF16) for _ in range(2)]
        gtg = [AR.alloc([8, TG], BF16) for _ in range(2)]
        yst = [AR.alloc([8, TG], BF16) for _ in range(2)]
        NR = 10
        innT = [AR.alloc([128], BF16) for _ in range(NR)]
        qd = [AR.alloc([128], BF16) for _ in range(NR)]
        on = [AR.alloc([256], BF16) for _ in range(NR)]
        stats = [AR.alloc([6], F32) for _ in range(NR)]
        mv = [AR.alloc([2], F32) for _ in range(NR)]
        rs = [AR.alloc([1], F32) for _ in range(NR)]
        nmr = [AR.alloc([1], F32) for _ in range(NR)]
        bc2 = Buf()
        bsf = [Buf() for _ in range(4)]
        bsb = [Buf() for _ in range(4)]
        bin_ = [Buf(), Buf()]
        byst = [Buf(), Buf()]
        binn = [Buf() for _ in range(NR)]
        bqd = [Buf() for _ in range(NR)]
        bon = [Buf() for _ in range(NR)]
        bst = [Buf() for _ in range(NR)]
        gam = _gammas()
        dma("sp", dmask, c_dmask, writes=[bc2])
        dma("sp", qdec, c_qdec, writes=[bc2])
        T.op("dve", lambda: dve.memset(state_f, 0.0), writes=bsf)
        ps_i = PsRot([0, 1])
        ps_oo = PsRot([2, 3])
        ps_st = PsRot([4, 5])
        ps_t = PsRot([6, 7])
        NGRP = S // TG
        NIT = NGRP * NCH * 4
        ctx = {}

        def load_grp(tg):
            t0 = tg * TG
            gi = tg % 2
            dma("sp", qrg[gi], QRT.rearrange("h p t -> p h t")[:, :, t0:t0 + TG], writes=[bin_[gi]])
            dma("sp", krg[gi], KRT.rearrange("h p t -> p h t")[:, :, t0:t0 + TG], writes=[bin_[gi]])
            dma("sp", krk[gi], KRK[t0:t0 + TG, :].rearrange("(n p) f -> p n f", p=128), writes=[bin_[gi]])
            dma("sp", vrg[gi], VR[t0:t0 + TG, :].rearrange("(n p) f -> p n f", p=128), writes=[bin_[gi]])
            dma("sp", gtg[gi], GT.rearrange("c p t -> p c t")[:, 0:8, t0:t0 + TG], writes=[bin_[gi]])

        def info(k):
            tg = k // (NCH * 4)
            nl = (k // 4) % NCH
            h = k % 4
            return tg, tg % 2, nl, tg * NCH + nl, h, slice(nl * 128, (nl + 1) * 128), slice(h * 128, (h + 1) * 128), k % NR

        ps_po = PsRot([2, 3, 4])
        ps_st2 = PsRot([5, 6])
        ps_it = Rot([0, 1, 7])

        def t0_(k):
            tg, gi, nl, n, h, cs, hs, r3 = info(k)
            bk_ = ps_it.next()
            ctx[("i", k)] = bk_
            mm(psb[bk_][:, 0:128], krg[gi][:, h, cs], qrg[gi][:, h, cs], True, True, reads=[bin_[gi]], writes=[bps[bk_]])

        def t1_(k):
            tg, gi, nl, n, h, cs, hs, r3 = info(k)
            bk_ = ctx[("i", k)]
            v_tt(innT[r3], psb[bk_][:, 0:128], dmask[:, hs], ALU.mult, reads=[bc2], writes=[bps[bk_], binn[r3]])
            if n > 0:
                v_tt(qd[r3], qrg[gi][:, h, cs], qdec[:, hs], ALU.mult, reads=[bin_[gi], bc2], writes=[bqd[r3]], eng="pool")

        def t2_(k):
            tg, gi, nl, n, h, cs, hs, r3 = info(k)
            po, bpo = ps_po.next()
            pst_, bpst = ps_st2.next()
            ctx[k] = (po, bpo, pst_, bpst)
            mm(po[:, 0:256], innT[r3], vrg[gi][:, nl, h * 256:(h + 1) * 256], True, n == 0,
               reads=[binn[r3], bin_[gi]], writes=[bpo])
            if n > 0:
                mm(po[:, 0:256], qd[r3], state_b[:, h, :], False, True, reads=[bqd[r3], bsb[h]], writes=[bpo])
            mm(pst_[:, 0:256], krk[gi][:, nl, hs], vrg[gi][:, nl, h * 256:(h + 1) * 256], True, True,
               reads=[bin_[gi]], writes=[bpst])

        def t3_(k):
            tg, gi, nl, n, h, cs, hs, r3 = info(k)
            po, bpo, pst_, bpst = ctx[k]
            v_stt(state_f[:, h, :], state_f[:, h, :], float(gam[h] ** 128), pst_[:, 0:256], ALU.mult, ALU.add,
                  reads=[], writes=[bpst, bsf[h]])
            T.op("dve", (lambda a=stats[r3], b_=po[:, 0:256]: dve.bn_stats(out=a, in_=b_)), reads=[], writes=[bpo, bst[r3]])
            T.op("dve", (lambda a=mv[r3], b_=stats[r3]: dve.bn_aggr(out=a, in_=b_)), reads=[], writes=[bst[r3]])
            if n < 31:
                v_copy("act", state_b[:, h, :], state_f[:, h, :], reads=[bsf[h]], writes=[bsb[h]])
            actf(rs[r3], mv[r3][:, 1:2], AF.Sqrt, reads=[], writes=[bst[r3]], bias=GN_EPS)

        def t4_(k):
            tg, gi, nl, n, h, cs, hs, r3 = info(k)
            po, bpo, pst_, bpst = ctx.pop(k)
            ctx[("po", k)] = (po, bpo)
            v_recip(rs[r3], rs[r3], reads=[], writes=[bst[r3]])
            v_stt(nmr[r3], mv[r3][:, 0:1], -1.0, rs[r3], ALU.mult, ALU.mult, reads=[], writes=[bst[r3]])
            T.op("act", (lambda o_=on[r3], i_=po[:, 0:256], sc_=rs[r3][:, 0:1], b_=nmr[r3][:, 0:1]:
                         act.activation(out=o_, in_=i_, func=AF.Identity, scale=sc_, bias=b_)),
                 reads=[bst[r3]], writes=[bpo, bon[r3]])

        def t5_(k):
            tg, gi, nl, n, h, cs, hs, r3 = info(k)
            ctx.pop(("i", k))
            po, bpo = ctx.pop(("po", k))
            ptb = po[:, 256:512].bitcast(BF16)
            ctx[("t", k)] = (ptb, bpo)
            for ec in range(2):
                tr(ptb[:, ec * 128:(ec + 1) * 128], on[r3][:, ec * 128:(ec + 1) * 128], identb,
                   reads=[bon[r3], bconst], writes=[bpo])

        def t6_(k):
            tg, gi, nl, n, h, cs, hs, r3 = info(k)
            ptb, bpt = ctx.pop(("t", k))
            for ec in range(2):
                ch = 2 * h + ec
                gcol = l * 32 + 24 + ch
                v_stt(yst[gi][:, ch, cs], ptb[:, ec * 128:(ec + 1) * 128], gains[:, gcol:gcol + 1],
                      gtg[gi][:, ch, cs], ALU.mult, ALU.mult, reads=[bin_[gi], bconst], writes=[bpt, byst[gi]])
            if nl == NCH - 1 and h == 3:
                t0g = tg * TG
                dma("sp", YRT.rearrange("c p t -> p c t")[:, :, t0g:t0g + TG], yst[gi], reads=[byst[gi]])

        stages_ = [t0_, t1_, t2_, t3_, t4_, t5_, t6_]
        NST = len(stages_)
        load_grp(0)
        load_grp(1)
        per = NCH * 4
        for k in range(NIT + NST - 1):
            if k < NIT:
                tg = k // per
                if k % per == NST - 1 and tg >= 1 and tg + 1 < NGRP:
                    load_grp(tg + 1)
            for si_, fn_ in enumerate(stages_):
                kk = k - si_
                if 0 <= kk < NIT:
                    fn_(kk)
        dma("sp", retp[l].rearrange("h d e -> d h e"), state_f, reads=bsf)
        T.barrier()

    def phase_B_sample(l):
        AR.off = PERSIST_END
        gam = _gammas()
        stk = AR.alloc([6656], F32)
        sbias = AR.alloc([3, 8], F32)
        sel = AR.alloc([4, 4], F32)
        kvt = [AR.alloc([1024], F32) for _ in range(2)]
        qbc = [AR.alloc([512], F32) for _ in range(2)]
        prod = [AR.alloc([512], F32) for _ in range(2)]
        scb = [AR.alloc([8], F32) for _ in range(2)]
        pb2 = [AR.alloc([8], F32) for _ in range(2)]
        wt = [AR.alloc([512], F32) for _ in range(2)]
        prn = AR.alloc([512], F32)
        sn = AR.alloc([8], F32)
        pn = AR.alloc([3, 8], F32)
        wn = AR.alloc([512], F32)
        numn = AR.alloc([512], F32)
        denn = AR.alloc([8], F32)
        oS = AR.alloc([512], F32)
        S0 = AR.alloc([4, 4, 256], F32)
        Snew = AR.alloc([4, 4, 256], F32)
        qsel = AR.alloc([4, 16], F32)
        qk = AR.alloc([4], F32)
        o1 = AR.alloc([1024], F32)
        oR = AR.alloc([1024], F32)
        onS = AR.alloc([1024], F32)
        statS = AR.alloc([4, 6], F32)
        mvS = AR.alloc([4, 2], F32)
        rsS = AR.alloc([4], F32)
        ksel = [AR.alloc([512], F32) for _ in range(2)]
        bstk, bcs = Buf(), Buf()
        bkvt = [Buf(), Buf()]
        bqbc = [Buf(), Buf()]
        bprod = [Buf(), Buf()]
        bsc = [Buf(), Buf()]
        bpb = [Buf(), Buf()]
        bwt = [Buf(), Buf()]
        bmisc = Buf()
        bS0, bSn, bqsel = Buf(), Buf(), Buf()
        bksel = [Buf(), Buf()]
        AX = mybir.AxisListType.X

        def red(out, in_, reads, writes):
            return T.op("dve", (lambda: dve.tensor_reduce(out=out, in_=in_, axis=AX, op=ALU.add)), reads=reads,
                        writes=writes)

        dma("sp", sbias, c_sbias, writes=[bcs])
        dma("sp", sel, c_sel, writes=[bcs])
        NUMps, bNUM = psb[0], bps[0]
        DENps, bDEN = psb[1], bps[1]
        idx = 0
        for s_ in range(NS):
            for g in range(3):
                dil, wb_ = DILS[g], WINS[g]
                i2 = idx % 2
                dma("sp", kvt[i2], cache_in[g][l, s_, 0:wb_ - dil + 1:dil, :], writes=[bkvt[i2]])
                dma("sp", qbc[i2], STOK[s_:s_ + 1, g * 512:(g + 1) * 512].broadcast_to([128, 512]), reads=[],
                    writes=[bqbc[i2]], )
                v_tt(prod[i2], kvt[i2][:, 0:512], qbc[i2], ALU.mult, reads=[bkvt[i2], bqbc[i2]], writes=[bprod[i2]])
                red(scb[i2], prod[i2].rearrange("p (h e) -> p h e", e=64), reads=[bprod[i2]], writes=[bsc[i2]])
                v_stt(scb[i2], scb[i2], 0.125, sbias[:, g, :], ALU.mult, ALU.add, reads=[bcs], writes=[bsc[i2]])
                actf(pb2[i2], scb[i2], AF.Exp, reads=[bsc[i2]], writes=[bpb[i2]])
                v_tt(wt[i2].rearrange("p (h e) -> p h e", e=64), kvt[i2][:, 512:1024].rearrange("p (h e) -> p h e", e=64),
                     pb2[i2].unsqueeze(2).broadcast_to([128, 8, 64]), ALU.mult, reads=[bkvt[i2], bpb[i2]],
                     writes=[bwt[i2]])
                mm(NUMps[0:NS, :], sel[:, s_, :], wt[i2], idx == 0, idx == 11, reads=[bcs, bwt[i2]], writes=[bNUM])
                mm(DENps[0:NS, 0:8], sel[:, s_, :], pb2[i2], idx == 0, idx == 11, reads=[bcs, bpb[i2]], writes=[bDEN])
                idx += 1
                if idx == 2:
                    dma("sp", stk[0:NS, :], STOK, writes=[bstk])
                    dma("sp", S0, sret[l].rearrange("s h d e -> d s h e"), writes=[bS0])
        for g in range(3):
            qn = stk[0:NS, g * 512:(g + 1) * 512]
            kn = stk[0:NS, 1536 + g * 512:1536 + (g + 1) * 512]
            vn = stk[0:NS, 3072 + g * 512:3072 + (g + 1) * 512]
            v_tt(prn[0:NS, :], qn, kn, ALU.mult, reads=[bstk], writes=[bmisc])
            red(sn[0:NS, :], prn[0:NS, :].rearrange("p (h e) -> p h e", e=64), reads=[], writes=[bmisc])
            actf(pn[0:NS, g, :], sn[0:NS, :], AF.Exp, reads=[], writes=[bmisc], scale=0.125)
            dstn = numn if g == 0 else wn
            v_tt(dstn[0:NS, :].rearrange("p (h e) -> p h e", e=64), vn.rearrange("p (h e) -> p h e", e=64),
                 pn[0:NS, g, :].unsqueeze(2).broadcast_to([NS, 8, 64]), ALU.mult, reads=[bstk], writes=[bmisc])
            if g > 0:
                v_tt(numn[0:NS, :], numn[0:NS, :], wn[0:NS, :], ALU.add, reads=[], writes=[bmisc])
        v_tt(denn[0:NS, :], pn[0:NS, 0, :], pn[0:NS, 1, :], ALU.add, reads=[], writes=[bmisc])
        v_tt(denn[0:NS, :], denn[0:NS, :], pn[0:NS, 2, :], ALU.add, reads=[], writes=[bmisc])
        v_tt(numn[0:NS, :], numn[0:NS, :], NUMps[0:NS, :], ALU.add, reads=[], writes=[bmisc, bNUM])
        v_tt(denn[0:NS, :], denn[0:NS, :], DENps[0:NS, 0:8], ALU.add, reads=[], writes=[bmisc, bDEN])
        v_recip(denn[0:NS, :], denn[0:NS, :], reads=[], writes=[bmisc])
        v_tt(oS[0:NS, :].rearrange("p (h e) -> p h e", e=64), numn[0:NS, :].rearrange("p (h e) -> p h e", e=64),
             denn[0:NS, :].unsqueeze(2).broadcast_to([NS, 8, 64]), ALU.mult, reads=[], writes=[bmisc])
        pt, bpt = psb[2], bps[2]
        for cc in range(4):
            tr(pt[:, cc * NS:(cc + 1) * NS], oS[0:NS, cc * 128:(cc + 1) * 128], ident[0:NS, 0:NS], reads=[bmisc, bconst],
               writes=[bpt])
        v_copy("dve", actS[:, 0:4, :], pt[:, 0:4 * NS].rearrange("p (c s) -> p c s", s=NS), reads=[],
               writes=[bpt] + bas[0:4])

        QR0, KR0, VR0 = 4608, 5120, 5632
        T.op("dve", (lambda: dve.memset(qsel, 0.0)), writes=[bqsel])
        v_copy("dve", qsel[:, :, 0:16:5], qrS[:, :, :], reads=[bqrS], writes=[bqsel])
        pq = [psb[3], psb[4]]
        bpq = [bps[3], bps[4]]
        for h in range(4):
            for s_ in range(NS):
                mm(pq[h // 2][0:NS, (h % 2) * 256:(h % 2) * 256 + 256], qsel[:, h, s_ * 4:(s_ + 1) * 4], S0[:, s_, h, :],
                   s_ == 0, s_ == NS - 1, reads=[bqsel, bS0], writes=[bpq[h // 2]])
        v_tt(prn[0:NS, :], stk[0:NS, QR0:QR0 + 512], stk[0:NS, KR0:KR0 + 512], ALU.mult, reads=[bstk], writes=[bmisc])
        red(qk[0:NS, :], prn[0:NS, :].rearrange("p (h d) -> p h d", d=128), reads=[], writes=[bmisc])
        v_ts(qk[0:NS, :], qk[0:NS, :], float(128.0 ** -0.5), None, ALU.mult, None, reads=[], writes=[bmisc])
        v_tt(o1[0:NS, :].rearrange("p (h e) -> p h e", e=256), stk[0:NS, VR0:VR0 + 1024].rearrange("p (h e) -> p h e", e=256),
             qk[0:NS, :].unsqueeze(2).broadcast_to([NS, 4, 256]), ALU.mult, reads=[bstk], writes=[bmisc])
        for h in range(4):
            hs = slice(h * 256, (h + 1) * 256)
            v_stt(oR[0:NS, hs], pq[h // 2][0:NS, (h % 2) * 256:(h % 2) * 256 + 256], float(gam[h]), o1[0:NS, hs],
                  ALU.mult, ALU.add, reads=[], writes=[bmisc, bpq[h // 2]])
            T.op("dve", (lambda h=h, hs=hs: dve.bn_stats(out=statS[0:NS, h, :], in_=oR[0:NS, hs])), reads=[],
                 writes=[bmisc])
            T.op("dve", (lambda h=h: dve.bn_aggr(out=mvS[0:NS, h, :], in_=statS[0:NS, h, :])), reads=[], writes=[bmisc])
        actf(rsS[0:NS, :], mvS[0:NS, :, 1], AF.Sqrt, reads=[], writes=[bmisc], bias=GN_EPS)
        v_recip(rsS[0:NS, :], rsS[0:NS, :], reads=[], writes=[bmisc])
        for h in range(4):
            hs = slice(h * 256, (h + 1) * 256)
            v_ts(onS[0:NS, hs], oR[0:NS, hs], mvS[0:NS, h, 0:1], rsS[0:NS, h:h + 1], ALU.subtract, ALU.mult, reads=[],
                 writes=[bmisc])
        pt2, bpt2 = psb[5], bps[5]
        for ch in range(8):
            tr(pt2[:, ch * NS:(ch + 1) * NS], onS[0:NS, ch * 128:(ch + 1) * 128], ident[0:NS, 0:NS], reads=[bmisc, bconst],
               writes=[bpt2])
        for ch in range(8):
            gcol = l * 32 + 24 + ch
            v_stt(actS[:, 4 + ch, :], pt2[:, ch * NS:(ch + 1) * NS], gains[:, gcol:gcol + 1], gS[:, ch, :], ALU.mult,
                  ALU.mult, reads=[bgS, bconst], writes=[bpt2, bas[4 + ch]])
        ps_r = PsRot([6, 7])
        for s_ in range(NS):
            k2 = s_ % 2
            v_ts(ksel[k2][0:NS, :], stk[0:NS, KR0:KR0 + 512], ident[0:NS, s_:s_ + 1], float(128.0 ** -0.5), ALU.mult,
                 ALU.mult, reads=[bstk, bconst], writes=[bksel[k2]])
            for h in range(4):
                pr, bpr = ps_r.next()
                mm(pr[:, 0:256], ksel[k2][0:NS, h * 128:(h + 1) * 128], stk[0:NS, VR0 + h * 256:VR0 + (h + 1) * 256], True,
                   True, reads=[bksel[k2], bstk], writes=[bpr])
                v_stt(Snew[:, s_, h, :], S0[:, s_, h, :], float(gam[h]), pr[:, 0:256], ALU.mult, ALU.add, reads=[bS0],
                      writes=[bpr, bSn])
        dma("sp", rets[l].rearrange("s h d e -> d s h e"), Snew, reads=[bSn])
        T.barrier()

    def phase_C(l):
        AR.off = PERSIST_END
        xT = AR.alloc([8, TT], F32)
        hT = AR.alloc([8, TT], BF16)
        actT = AR.alloc([NJ, TT], BF16)
        act_off = AR.last
        rstd = AR.alloc([TT], F32)
        tmps = [AR.alloc([512], BF16) for _ in range(3)]
        t12 = [AR.alloc([512], F32) for _ in range(4)]
        ystg = [AR.alloc([D], F32) for _ in range(2)]
        cin = AR.alloc([12, TT], BF16)
        gbuf = [[AR.alloc([TT], BF16) for _ in range(2)] for _ in range(2)]
        bx = [[Buf() for _ in range(2)] for _ in range(8)]
        bh = [[Buf() for _ in range(2)] for _ in range(8)]
        bact = [[Buf() for _ in range(2)] for _ in range(NJ)]
        bcin = [[Buf() for _ in range(2)] for _ in range(12)]
        bgb2 = [[[Buf(), Buf()] for _ in range(2)] for _ in range(2)]
        brs = [Buf(), Buf()]
        btmps = [Buf() for _ in range(3)]
        bt12 = [Buf() for _ in range(4)]
        bystg = [Buf(), Buf()]
        segs = make_segs(xT, hT, actT, rstd, bx, bh, bact, brs)
        ev_rot = Rot(["act", "dve"])
        XR = XRES.rearrange("c p t -> p c t")

        def load_cin(t):
            t0_ = t * TT
            dma("sp", cin[:, 0:4, :], OAT.rearrange("c p t -> p c t")[:, :, t0_:t0_ + TT],
                writes=[bcin[j][s] for j in range(0, 4) for s in range(2)])
            dma("sp", cin[:, 4:12, :], YRT.rearrange("c p t -> p c t")[:, :, t0_:t0_ + TT],
                writes=[bcin[j][s] for j in range(4, 12) for s in range(2)])

        load_cin(0)
        for t in range(NTILE):
            t0 = t * TT
            psr = PsRot(range(8))
            last = (t == NTILE - 1)
            segs_t = segs + ([sseg] if last else [])
            for hf in range(2):
                wa_s, bwa = WS1.get()
                wb_s, bwb = WS1.get()
                for mi in range(4):
                    m = hf * 4 + mi
                    gi_ = m % 2
                    dma("sp", gbuf[gi_][0], GT[8 + m, :, t0:t0 + TT], writes=bgb2[gi_][0])
                    dma("sp", gbuf[gi_][1], GT[16 + m, :, t0:t0 + TT], writes=bgb2[gi_][1])
                    if m == 1:
                        dma("sp", xT[:, :, :], XR[:, :, t0:t0 + TT], writes=[bx[c][s] for c in range(8) for s in range(2)])
                    for sg in segs_t:
                        n = sg.n
                        if sg is sseg:
                            ga_ap, gb_ap = gS[:, 8 + m, :], gS[:, 16 + m, :]
                            bga, bgb = bgS, bgS
                            i1 = 0
                            oa_ = sg.a
                            boa_ = sg.ba
                        else:
                            ssl = slice(sg.s * 512, (sg.s + 1) * 512)
                            ga_ap, gb_ap = gbuf[gi_][0][:, ssl], gbuf[gi_][1][:, ssl]
                            bga, bgb = bgb2[gi_][0][sg.s], bgb2[gi_][1][sg.s]
                            i1 = (m * 2 + sg.s) % 2
                            oa_ = (lambda k, ssl=ssl: cin[:, k, ssl])
                            boa_ = [bcin[k][sg.s] for k in range(12)]
                        pa, bpa = psr.next()
                        pb_, bpb = psr.next()
                        for k in range(4):
                            mm(pa[:, :n], wa_s[:, k, mi * 128:(mi + 1) * 128], oa_(k), k == 0, k == 3,
                               reads=[bwa, boa_[k]], writes=[bpa])
                        for k in range(8):
                            mm(pb_[:, :n], wb_s[:, k, mi * 128:(mi + 1) * 128], oa_(4 + k), k == 0, k == 7,
                               reads=[bwb, boa_[4 + k]], writes=[bpb])
                        ta, tb_ = t12[2 * i1], t12[2 * i1 + 1]
                        v_tt(ta[:, :n], pa[:, :n], ga_ap, ALU.mult, reads=[bga], writes=[bpa, bt12[2 * i1]])
                        v_tt(tb_[:, :n], pb_[:, :n], gb_ap, ALU.mult, reads=[bgb], writes=[bpb, bt12[2 * i1 + 1]])
                        v_tt(sg.h(m), ta[:, :n], tb_[:, :n], ALU.add, reads=[bt12[2 * i1], bt12[2 * i1 + 1]],
                             writes=[sg.bh[m]], eng="pool")
            if t + 1 < NTILE:
                load_cin(t + 1)
            for hf in range(2):
                wo_s, bwo = WS1.get()
                for mi in range(4):
                    m = hf * 4 + mi
                    for sg in segs_t:
                        n = sg.n
                        pm, bpm = psr.next()
                        for k in range(8):
                            mm(pm[:, :n], wo_s[:, k, mi * 128:(mi + 1) * 128], sg.h(k), k == 0, k == 7,
                               reads=[bwo, sg.bh[k]], writes=[bpm])
                        v_tt(sg.x(m), pm[:, :n], sg.x(m), ALU.add, reads=[], writes=[bpm, sg.bx[m]])
            rmsnorm(segs_t, l * 32 + 16, PsRot([6, 7]))
            ffn(segs_t, lambda i: tmps[i % 3], lambda i: btmps[i % 3])
            if l < DEPTH - 1:
                dma("sp", XR[:, :, t0:t0 + TT], xT[:, :, :], reads=[bx[c][s] for c in range(8) for s in range(2)])
            else:
                yfin = AR.view(act_off, [8, TT], F32)
                psn = PsRot([6, 7])
                for sg in segs:
                    ssl = slice(sg.s * 512, (sg.s + 1) * 512)
                    for c in range(8):
                        actf(sg.h(c), sg.x(c), AF.Square, reads=[sg.bx[c]], writes=[sg.bh[c]])
                    pst, pbf = psn.next()
                    for c in range(8):
                        mm(pst[:, :], ones_b, sg.h(c), c == 0, c == 7, reads=[sg.bh[c], bconst], writes=[pbf])
                    actf(sg.rstd, pst[:, :], AF.Sqrt, reads=[], writes=[pbf, sg.brs], scale=1.0 / 1024.0, bias=NORM_EPS)
                    v_recip(sg.rstd, sg.rstd, reads=[], writes=[sg.brs])
                    for c in range(8):
                        v_stt(yfin[:, c, ssl], sg.x(c), gains[:, 64 + c:65 + c], sg.rstd, ALU.mult, ALU.mult,
                              reads=[sg.bx[c], sg.brs, bconst],
                              writes=[bact[2 * c][sg.s], bact[2 * c + 1][sg.s], bact[2 * c][1 - sg.s], bact[2 * c + 1][1 - sg.s]])
                pst_r = PsRot(range(6))
                for tb in range(8):
                    yi = tb % 2
                    s_ = tb // 4
                    for cg in range(2):
                        pst, pbf = pst_r.next()
                        for ci in range(4):
                            c = cg * 4 + ci
                            tr(pst[:, ci * 128:(ci + 1) * 128], yfin[:, c, tb * 128:(tb + 1) * 128], ident,
                               reads=[bact[2 * c][s_], bact[2 * c + 1][s_], bconst], writes=[pbf])
                        v_copy(ev_rot.next(), ystg[yi][:, cg * 512:(cg + 1) * 512], pst[:, :], reads=[],
                               writes=[pbf, bystg[yi]])
                    dma("sp", yp[t0 + tb * 128:t0 + (tb + 1) * 128, :], ystg[yi], reads=[bystg[yi]])
                if last:
                    yfS = t12[0].rearrange("p (c s) -> p c s", s=64)
                    for c in range(8):
                        actf(hsT[:, c, :], xsT[:, c, :], AF.Square, reads=[bxs[c]], writes=[bhs[c]])
                    pst, pbf = psn.next()
                    for c in range(8):
                        mm(pst[:, 0:NS], ones_b, hsT[:, c, :], c == 0, c == 7, reads=[bhs[c], bconst], writes=[pbf])
                    actf(rstdS, pst[:, 0:NS], AF.Sqrt, reads=[], writes=[pbf, brss], scale=1.0 / 1024.0, bias=NORM_EPS)
                    v_recip(rstdS, rstdS, reads=[], writes=[brss])
                    for c in range(8):
                        v_stt(yfS[:, c, 0:NS], xsT[:, c, :], gains[:, 64 + c:65 + c], rstdS, ALU.mult, ALU.mult,
                              reads=[bxs[c], brss, bconst], writes=[bt12[0]])
                    for cg in range(2):
                        pst, pbf = pst_r.next()
                        for ci in range(4):
                            c = cg * 4 + ci
                            tr(pst[0:NS, ci * 128:(ci + 1) * 128], yfS[:, c, 0:NS], ident, reads=[bt12[0], bconst],
                               writes=[pbf])
                        v_copy(ev_rot.next(), ystg[0][0:NS, cg * 512:(cg + 1) * 512], pst[0:NS, :], reads=[],
                               writes=[pbf, bystg[0]])
                    dma("sp", ys, ystg[0][0:NS, :], reads=[bystg[0]])
        T.barrier()

    stages = []
    for l in range(DEPTH):
        stages += [("A", l), ("BS", l), ("B1", l), ("B2", l), ("C", l)]
    if only is None:
        load_sample_x()
        T.barrier()
    for (ph, l) in stages:
        if only is not None and ph != only:
            continue
        if ph == "A":
            phase_A(l)
        elif ph == "BS":
            phase_B_sample(l)
        elif ph == "B1":
            phase_B_attn(l)
        elif ph == "B2":
            phase_B_ret(l)
        else:
            phase_C(l)
        if stop_after == f"{ph}{l}":
            break
    T.emit()
    es.close()
    return nc, T


def _gains_table(inp):
    g = np.zeros((128, 72), np.float32)
    for l in range(DEPTH):
        for i, nm in enumerate(("norm_ffn1", "norm_mix", "norm_ffn2", "ret_gn")):
            g[:, l * 32 + i * 8:l * 32 + i * 8 + 8] = np.asarray(inp[nm][l], np.float32).reshape(8, 128).T
    g[:, 64:72] = np.asarray(inp["norm_final"], np.float32).reshape(8, 128).T
    return g


def make_in_maps(inp):
    consts = _const_tables()
    gains = _gains_table(inp)
    shared = dict(consts)
    shared["gains"] = gains
    for nm in ("ffn1_wg", "ffn1_wu", "ffn1_wd", "ffn2_wg", "ffn2_wu", "ffn2_wd", "w_in", "w_br_a", "w_br_b", "w_out"):
        shared[nm] = np.ascontiguousarray(inp[nm], dtype=np.float32)
    maps = []
    for c in range(NCORES):
        m = dict(shared)
        m["xp"] = np.ascontiguousarray(inp["x_prompt"][c])
        sl = slice(c * NS, (c + 1) * NS)
        m["xs"] = np.ascontiguousarray(inp["x_sample"][sl, 0, :])
        for w, nm in zip(WINS, ("cache_kv_w128", "cache_kv_w512", "cache_kv_w2048")):
            m[f"c{w}"] = np.ascontiguousarray(inp[nm][:, sl]).reshape(DEPTH, NS, w, 1024)
        m["sret"] = np.ascontiguousarray(inp["state_ret"][:, sl])
        maps.append(m)
    return maps


_PROGRAM_CACHE = {}


def kernel(**inputs):
    inp = {k: np.asarray(v) for k, v in inputs.items()}
    if "prog" not in _PROGRAM_CACHE:
        _PROGRAM_CACHE["prog"] = build_program(debug=False)
    nc, _ = _PROGRAM_CACHE["prog"]
    maps = make_in_maps(inp)
    res = run_bass_kernel_spmd(nc, maps, core_ids=list(range(NCORES)))
    R = res.results
    f32 = np.float32
    y_prompt = np.stack([np.asarray(R[c]["yp"], f32) for c in range(NCORES)], 0)
    y_sample = np.concatenate([np.asarray(R[c]["ys"], f32) for c in range(NCORES)], 0).reshape(NCORES * NS, 1, D)
    outs = [y_prompt, y_sample]
    for w in WINS:
        kp = np.stack([np.asarray(R[c][f"kv{w}p"], f32) for c in range(NCORES)], 1)
        ks = np.concatenate([np.asarray(R[c][f"kv{w}s"], f32) for c in range(NCORES)], 1)
        outs.append(kp.reshape(DEPTH, NCORES, w, 2, 8, 64))
        outs.append(ks.reshape(DEPTH, NCORES * NS, w, 2, 8, 64))
    rp = np.stack([np.asarray(R[c]["retp"], f32) for c in range(NCORES)], 1)
    rs_ = np.concatenate([np.asarray(R[c]["rets"], f32) for c in range(NCORES)], 1)
    outs.append(rp)
    outs.append(rs_)
    return tuple(outs)
```
